# Optimizing a Trainium2 kernel written in Bass

```python
import jax, jax.numpy as jnp
from jax import lax
import numpy as np

D_MODEL = 1024
BATCH = 32
SEQ = 2048
DEPTH = 4

N_MIXERS = 3
NORM_EPS = 1e-5
D_FF = 4 * D_MODEL
NEG_INF = -1e30
FORCE = 1e6

SG_CHUNK = 128
SG_WIDTH = 2 * D_MODEL
SG_GROUPS = 8

SSM_D_INNER = 2 * D_MODEL
SSM_HEAD_DIM = 64
SSM_HEADS = SSM_D_INNER // SSM_HEAD_DIM
SSM_GROUPS = 8
SSM_STATE = 128
SSM_CONV = 4
SSM_CHUNK = 128
SSM_CONV_CH = SSM_D_INNER + 2 * SSM_GROUPS * SSM_STATE
SSM_IN = 2 * SSM_D_INNER + 2 * SSM_GROUPS * SSM_STATE + SSM_HEADS

NSA_HEADS = 16
NSA_KV_HEADS = 4
NSA_HEAD_DIM = D_MODEL // NSA_HEADS
CMP_BLOCK = 32
CMP_STRIDE = 16
CMP_HIDDEN = 4 * NSA_HEAD_DIM
SEL_BLOCK = 64
SEL_TOP_N = 16
WINDOW = 512
NSA_QBLOCK = 32
NSA_IN = NSA_HEADS * NSA_HEAD_DIM + 6 * NSA_KV_HEADS * NSA_HEAD_DIM + 3 * NSA_HEADS

kernel_name = "hybrid_gmlp_ssd_nsa_trunk"


def _rmsnorm(x, g):
    xf = x.astype(jnp.float32)
    y = xf * lax.rsqrt(jnp.mean(jnp.square(xf), axis=-1, keepdims=True) + NORM_EPS)
    return (y * g).astype(x.dtype)


def _layernorm(x, g, b):
    xf = x.astype(jnp.float32)
    mu = jnp.mean(xf, axis=-1, keepdims=True)
    var = jnp.mean(jnp.square(xf - mu), axis=-1, keepdims=True)
    return ((xf - mu) * lax.rsqrt(var + NORM_EPS) * g + b).astype(x.dtype)


def _grouped_rmsnorm(y, g, groups):
    b, s, d = y.shape
    yf = y.astype(jnp.float32).reshape(b, s, groups, d // groups)
    yf = yf * lax.rsqrt(jnp.mean(jnp.square(yf), axis=-1, keepdims=True) + NORM_EPS)
    return (yf.reshape(b, s, d) * g).astype(y.dtype)


def _sqrelu_mlp(h, w1, w2):
    return jnp.square(jax.nn.relu(h @ w1)) @ w2


def spatial_gating_mixer(h, w_in, vnorm_g, vnorm_b, w_s, b_s, w_out):
    b, s, _ = h.shape
    uv = jax.nn.gelu(h @ w_in, approximate=False)
    u, v = jnp.split(uv, 2, axis=-1)
    v = _layernorm(v, vnorm_g, vnorm_b)
    nc = s // SG_CHUNK
    v = v.reshape(b, nc, SG_CHUNK, SG_GROUPS, SG_WIDTH // SG_GROUPS)
    causal = jnp.tril(jnp.ones((SG_CHUNK, SG_CHUNK), dtype=bool))
    w = jnp.where(causal[None], w_s, jnp.zeros((), w_s.dtype))
    v = jnp.einsum("gts,bcsgd->bctgd", w, v) + b_s.T[None, None, :, :, None]
    return (u * v.reshape(b, s, SG_WIDTH)) @ w_out


def _causal_depthwise_conv(x, w, bias):
    k = w.shape[0]
    y = lax.conv_general_dilated(x, w[:, None, :], window_strides=(1,), padding=[(k - 1, 0)],
                                 dimension_numbers=("NWC", "WIO", "NWC"),
                                 feature_group_count=x.shape[-1])
    return y + bias


def _ssd_chunked(x, dt, a, bm, cm):
    f32 = jnp.float32
    b, s, h, p = x.shape
    g, n = bm.shape[2], bm.shape[3]
    r = h // g
    L = SSM_CHUNK
    c = s // L
    xdt = (x.astype(f32) * dt[..., None]).reshape(b, c, L, g, r, p)
    da_cs = jnp.cumsum((dt * a).reshape(b, c, L, g, r), axis=2)
    bm = bm.astype(f32).reshape(b, c, L, g, n)
    cm = cm.astype(f32).reshape(b, c, L, g, n)
    causal = jnp.tril(jnp.ones((L, L), dtype=bool))[None, None, :, :, None, None]
    seg = da_cs[:, :, :, None] - da_cs[:, :, None, :]
    decay = jnp.exp(jnp.where(causal, seg, -jnp.inf))
    cb = jnp.einsum("bclgn,bcsgn->bclsg", cm, bm)
    y_diag = jnp.einsum("bclsgr,bcsgrp->bclgrp", cb[..., None] * decay, xdt)
    decay_states = jnp.exp(da_cs[:, :, -1:] - da_cs)
    states = jnp.einsum("bclgn,bclgrp->bcgrpn", bm, xdt * decay_states[..., None])
    chunk_decay = jnp.exp(da_cs[:, :, -1])

    def step(carry, inp):
        st, dec = inp
        return carry * dec[..., None, None] + st, carry

    init = jnp.zeros((b, g, r, p, n), f32)
    _, prev = lax.scan(step, init, (jnp.moveaxis(states, 1, 0), jnp.moveaxis(chunk_decay, 1, 0)))
    prev = jnp.moveaxis(prev, 0, 1)
    y_off = jnp.einsum("bclgn,bcgrpn->bclgrp", cm, prev) * jnp.exp(da_cs)[..., None]
    return (y_diag + y_off).reshape(b, s, h, p).astype(x.dtype)


def mamba2_mixer(h, w_in, conv_w, conv_b, dt_bias, a_log, d_skip, norm_g, w_out):
    b, s, _ = h.shape
    gn = SSM_GROUPS * SSM_STATE
    z, xbc, dt = jnp.split(h @ w_in, [SSM_D_INNER, SSM_D_INNER + SSM_CONV_CH], axis=-1)
    xbc = jax.nn.silu(_causal_depthwise_conv(xbc, conv_w, conv_b))
    xs, bm, cm = jnp.split(xbc, [SSM_D_INNER, SSM_D_INNER + gn], axis=-1)
    dt = jax.nn.softplus(dt.astype(jnp.float32) + dt_bias.astype(jnp.float32))
    a = -jnp.exp(a_log.astype(jnp.float32))
    xs = xs.reshape(b, s, SSM_HEADS, SSM_HEAD_DIM)
    bm = bm.reshape(b, s, SSM_GROUPS, SSM_STATE)
    cm = cm.reshape(b, s, SSM_GROUPS, SSM_STATE)
    y = _ssd_chunked(xs, dt, a, bm, cm) + d_skip[:, None] * xs
    y = y.reshape(b, s, SSM_D_INNER)
    y = _grouped_rmsnorm(y * jax.nn.silu(z), norm_g, SSM_GROUPS)
    return y @ w_out


def _compress(kk, tok, pos, w1, w2):
    b, _, g, dh = kk.shape
    nc = tok.shape[0]
    blk = kk[:, tok] + pos[None, None, :, None, :]
    blk = jnp.moveaxis(blk, 3, 2).reshape(b, nc, g, CMP_BLOCK * dh)
    return jax.nn.gelu(blk @ w1, approximate=False) @ w2


def _masked_softmax(scores, mask):
    return jax.nn.softmax(jnp.where(mask, scores, NEG_INF), axis=-1)


def nsa_mixer(h, w_in, cmp_pos_k, cmp_pos_v, cmp_k_w1, cmp_k_w2, cmp_v_w1, cmp_v_w2, w_out):
    f32 = jnp.float32
    b, s, _ = h.shape
    H, G, dh = NSA_HEADS, NSA_KV_HEADS, NSA_HEAD_DIM
    r = H // G
    sizes = [H * dh] + [G * dh] * 6
    cuts = [int(c) for c in np.cumsum(sizes)]
    q, kc, vc, ks, vs, kw, vw, gl = jnp.split(h @ w_in, cuts, axis=-1)
    q = q.reshape(b, s, G, r, dh) * (dh ** -0.5)
    kc, vc, ks, vs, kw, vw = [t.reshape(b, s, G, dh) for t in (kc, vc, ks, vs, kw, vw)]
    gates = jax.nn.sigmoid(gl.astype(f32)).reshape(b, s, G, r, 3)

    nc = (s - CMP_BLOCK) // CMP_STRIDE + 1
    cmp_start = jnp.arange(nc) * CMP_STRIDE
    cmp_end = cmp_start + CMP_BLOCK - 1
    tok = cmp_start[:, None] + jnp.arange(CMP_BLOCK)[None, :]
    k_cmp = _compress(kc, tok, cmp_pos_k, cmp_k_w1, cmp_k_w2)
    v_cmp = _compress(vc, tok, cmp_pos_v, cmp_v_w1, cmp_v_w2)

    nsel = s // SEL_BLOCK
    sel_start = jnp.arange(nsel) * SEL_BLOCK
    overlap = ((cmp_start[:, None] <= sel_start[None, :] + SEL_BLOCK - 1)
               & (cmp_end[:, None] >= sel_start[None, :])).astype(f32)
    k_top = min(SEL_TOP_N, nsel)
    ks_blk = jnp.moveaxis(ks.reshape(b, nsel, SEL_BLOCK, G, dh), 3, 1)
    vs_blk = jnp.moveaxis(vs.reshape(b, nsel, SEL_BLOCK, G, dh), 3, 1)
    bi = jnp.arange(b)[:, None, None, None]
    gi = jnp.arange(G)[None, :, None, None]

    kw_pad = jnp.pad(kw, ((0, 0), (WINDOW, 0), (0, 0), (0, 0)))
    vw_pad = jnp.pad(vw, ((0, 0), (WINDOW, 0), (0, 0), (0, 0)))

    QB = NSA_QBLOCK
    nqb = s // QB
    q_blocks = jnp.moveaxis(q.reshape(b, nqb, QB, G, r, dh), 1, 0)
    g_blocks = jnp.moveaxis(gates.reshape(b, nqb, QB, G, r, 3), 1, 0)

    def attend_block(inp):
        qi, q_blk, g_blk = inp
        t = qi * QB + jnp.arange(QB)
        valid_c = cmp_end[None, :] <= t[:, None]
        sc = jnp.einsum("bqgrd,bngd->bgrqn", q_blk, k_cmp).astype(f32)
        p_c = _masked_softmax(sc, valid_c) * jnp.any(valid_c, axis=-1)[:, None].astype(f32)
        o_c = jnp.einsum("bgrqn,bngd->bqgrd", p_c.astype(v_cmp.dtype), v_cmp)
        imp = jnp.einsum("bgrqn,nj->bgqj", p_c, overlap)
        j = jnp.arange(nsel)[None, :]
        cur = (t // SEL_BLOCK)[:, None]
        forced = (j == 0) | (j == cur) | (j == cur - 1)
        future = sel_start[None, :] > t[:, None]
        imp = jnp.where(forced, FORCE, jnp.where(future, -FORCE, imp))
        _, idx = lax.top_k(imp, k_top)
        kg = ks_blk[bi, gi, idx]
        vg = vs_blk[bi, gi, idx]
        tokpos = idx[..., None] * SEL_BLOCK + jnp.arange(SEL_BLOCK)
        valid_s = (tokpos <= t[None, None, :, None, None])[:, :, None]
        ss = jnp.einsum("bqgrd,bgqksd->bgrqks", q_blk, kg).astype(f32)
        ss = jnp.where(valid_s, ss, NEG_INF).reshape(b, G, r, QB, k_top * SEL_BLOCK)
        ps = jax.nn.softmax(ss, axis=-1).reshape(b, G, r, QB, k_top, SEL_BLOCK)
        o_s = jnp.einsum("bgrqks,bgqksd->bqgrd", ps.astype(vg.dtype), vg)
        kwin = lax.dynamic_slice_in_dim(kw_pad, qi * QB, WINDOW + QB, axis=1)
        vwin = lax.dynamic_slice_in_dim(vw_pad, qi * QB, WINDOW + QB, axis=1)
        kp = qi * QB - WINDOW + jnp.arange(WINDOW + QB)
        valid_w = (kp[None, :] >= 0) & (kp[None, :] <= t[:, None]) & (kp[None, :] > t[:, None] - WINDOW)
        sw = jnp.einsum("bqgrd,bkgd->bgrqk", q_blk, kwin).astype(f32)
        pw = _masked_softmax(sw, valid_w)
        o_w = jnp.einsum("bgrqk,bkgd->bqgrd", pw.astype(vwin.dtype), vwin)
        o = g_blk[..., 0:1] * o_c + g_blk[..., 1:2] * o_s + g_blk[..., 2:3] * o_w
        return o.astype(q_blk.dtype)

    o = lax.map(attend_block, (jnp.arange(nqb), q_blocks, g_blocks))
    o = jnp.moveaxis(o, 0, 1).reshape(b, s, H * dh)
    return o @ w_out


def _normal(key, shape, scale):
    return jax.random.normal(key, shape, jnp.float32) * scale


def _sg_params(key, prefix):
    k = jax.random.split(key, 6)
    return {
        prefix + "sg_w_in": _normal(k[0], (D_MODEL, 2 * SG_WIDTH), D_MODEL ** -0.5),
        prefix + "sg_vnorm_g": 1.0 + _normal(k[1], (SG_WIDTH,), 0.05),
        prefix + "sg_vnorm_b": _normal(k[2], (SG_WIDTH,), 0.02),
        prefix + "sg_w_s": _normal(k[3], (SG_GROUPS, SG_CHUNK, SG_CHUNK), SG_CHUNK ** -0.5),
        prefix + "sg_b_s": 1.0 + _normal(k[4], (SG_GROUPS, SG_CHUNK), 0.1),
        prefix + "sg_w_out": _normal(k[5], (SG_WIDTH, D_MODEL), SG_WIDTH ** -0.5),
    }


def _ssm_params(key, prefix):
    k = jax.random.split(key, 8)
    dt = jnp.exp(jax.random.uniform(k[3], (SSM_HEADS,), jnp.float32,
                                    jnp.log(jnp.float32(1e-3)), jnp.log(jnp.float32(1e-1))))
    return {
        prefix + "ssm_w_in": _normal(k[0], (D_MODEL, SSM_IN), D_MODEL ** -0.5),
        prefix + "ssm_conv_w": _normal(k[1], (SSM_CONV, SSM_CONV_CH), SSM_CONV ** -0.5),
        prefix + "ssm_conv_b": _normal(k[2], (SSM_CONV_CH,), 0.02),
        prefix + "ssm_dt_bias": dt + jnp.log(-jnp.expm1(-dt)),
        prefix + "ssm_a_log": jnp.log(jax.random.uniform(k[4], (SSM_HEADS,), jnp.float32, 1.0, 16.0)),
        prefix + "ssm_d_skip": 1.0 + _normal(k[5], (SSM_HEADS,), 0.1),
        prefix + "ssm_norm_g": 1.0 + _normal(k[6], (SSM_D_INNER,), 0.05),
        prefix + "ssm_w_out": _normal(k[7], (SSM_D_INNER, D_MODEL), SSM_D_INNER ** -0.5),
    }


def _nsa_params(key, prefix):
    k = jax.random.split(key, 8)
    flat = CMP_BLOCK * NSA_HEAD_DIM
    return {
        prefix + "nsa_w_in": _normal(k[0], (D_MODEL, NSA_IN), D_MODEL ** -0.5),
        prefix + "nsa_cmp_pos_k": _normal(k[1], (CMP_BLOCK, NSA_HEAD_DIM), 0.1),
        prefix + "nsa_cmp_pos_v": _normal(k[2], (CMP_BLOCK, NSA_HEAD_DIM), 0.1),
        prefix + "nsa_cmp_k_w1": _normal(k[3], (flat, CMP_HIDDEN), flat ** -0.5),
        prefix + "nsa_cmp_k_w2": _normal(k[4], (CMP_HIDDEN, NSA_HEAD_DIM), CMP_HIDDEN ** -0.5),
        prefix + "nsa_cmp_v_w1": _normal(k[5], (flat, CMP_HIDDEN), flat ** -0.5),
        prefix + "nsa_cmp_v_w2": _normal(k[6], (CMP_HIDDEN, NSA_HEAD_DIM), CMP_HIDDEN ** -0.5),
        prefix + "nsa_w_out": _normal(k[7], (NSA_HEADS * NSA_HEAD_DIM, D_MODEL), (NSA_HEADS * NSA_HEAD_DIM) ** -0.5),
    }


def setup_inputs(seed: int = 0) -> dict:
    key = jax.random.key(seed)
    keys = jax.random.split(key, 6)
    params = {
        "x": jax.random.normal(keys[0], (BATCH, SEQ, D_MODEL), jnp.float32),
        "norm_gains": 1.0 + _normal(keys[1], (DEPTH, 2, D_MODEL), 0.05),
        "final_norm": 1.0 + _normal(keys[2], (D_MODEL,), 0.05),
        "ff_w1": _normal(keys[3], (DEPTH, D_MODEL, D_FF), D_MODEL ** -0.5),
        "ff_w2": _normal(keys[4], (DEPTH, D_FF, D_MODEL), D_FF ** -0.5),
    }
    builders = (_sg_params, _ssm_params, _nsa_params)
    layer_keys = jax.random.split(keys[5], DEPTH)
    for i in range(DEPTH):
        params.update(builders[i % N_MIXERS](layer_keys[i], "l%d_" % i))
    return params


def reference(x, norm_gains, final_norm, ff_w1, ff_w2,
              l0_sg_w_in, l0_sg_vnorm_g, l0_sg_vnorm_b, l0_sg_w_s, l0_sg_b_s, l0_sg_w_out,
              l1_ssm_w_in, l1_ssm_conv_w, l1_ssm_conv_b, l1_ssm_dt_bias, l1_ssm_a_log,
              l1_ssm_d_skip, l1_ssm_norm_g, l1_ssm_w_out,
              l2_nsa_w_in, l2_nsa_cmp_pos_k, l2_nsa_cmp_pos_v, l2_nsa_cmp_k_w1, l2_nsa_cmp_k_w2,
              l2_nsa_cmp_v_w1, l2_nsa_cmp_v_w2, l2_nsa_w_out,
              l3_sg_w_in, l3_sg_vnorm_g, l3_sg_vnorm_b, l3_sg_w_s, l3_sg_b_s, l3_sg_w_out):
    mixers = (spatial_gating_mixer, mamba2_mixer, nsa_mixer)
    layer_params = (
        (l0_sg_w_in, l0_sg_vnorm_g, l0_sg_vnorm_b, l0_sg_w_s, l0_sg_b_s, l0_sg_w_out),
        (l1_ssm_w_in, l1_ssm_conv_w, l1_ssm_conv_b, l1_ssm_dt_bias, l1_ssm_a_log,
         l1_ssm_d_skip, l1_ssm_norm_g, l1_ssm_w_out),
        (l2_nsa_w_in, l2_nsa_cmp_pos_k, l2_nsa_cmp_pos_v, l2_nsa_cmp_k_w1, l2_nsa_cmp_k_w2,
         l2_nsa_cmp_v_w1, l2_nsa_cmp_v_w2, l2_nsa_w_out),
        (l3_sg_w_in, l3_sg_vnorm_g, l3_sg_vnorm_b, l3_sg_w_s, l3_sg_b_s, l3_sg_w_out),
    )
    for i in range(DEPTH):
        h = _rmsnorm(x, norm_gains[i, 0])
        x = x + mixers[i % N_MIXERS](h, *layer_params[i])
        h = _rmsnorm(x, norm_gains[i, 1])
        x = x + _sqrelu_mlp(h, ff_w1[i], ff_w2[i])
    return _rmsnorm(x, final_norm)
```

```python
import contextlib
import numpy as np
import concourse.bass as bass
import concourse.mybir as mybir
from concourse.bass_utils import run_bass_kernel_spmd

F32 = mybir.dt.float32
BF16 = mybir.dt.bfloat16
AF = mybir.ActivationFunctionType
ALU = mybir.AluOpType
AX = mybir.AxisListType

D = 1024
S = 2048
DFF = 4096
EPS = 1e-5
COMPUTE = ("pe", "act", "dve", "pool")


class Res:
    __slots__ = ("name", "w", "r", "dma_n")

    def __init__(self, name):
        self.name = name
        self.w = {}
        self.r = {}
        self.dma_n = 0


class Op:
    __slots__ = ("eng", "fn", "deps", "chan", "seq", "signal", "sigval", "isdma")


class Sched:
    def __init__(self, nc, es):
        self.nc = nc
        self.es = es
        self.ops = {e: [] for e in ("pe", "act", "dve", "pool", "sp")}
        self.cops = {e: [] for e in COMPUTE}
        self.sems = {e: es.enter_context(nc.semaphore("s_" + e)) for e in COMPUTE}
        self.dma_sems = {}
        self.dma_cnt = {}
        self.nres = 0

    def res(self, name=""):
        self.nres += 1
        return Res(name or ("r%d" % self.nres))

    def add(self, eng, fn, reads=(), writes=(), dma=None):
        deps = {}
        raw_same = 0
        for r in reads:
            for ch, s in r.w.items():
                if deps.get(ch, 0) < s:
                    deps[ch] = s
                if ch == eng and s > raw_same:
                    raw_same = s
        for r in writes:
            for ch, s in r.w.items():
                if deps.get(ch, 0) < s:
                    deps[ch] = s
            for ch, s in r.r.items():
                if deps.get(ch, 0) < s:
                    deps[ch] = s
        op = Op()
        op.eng = eng
        op.fn = fn
        op.signal = False
        op.sigval = 0
        op.isdma = dma is not None
        if fn is None:
            chan, seq = None, 0
        elif dma is not None:
            chan = ("dma", dma.name)
            if chan not in self.dma_sems:
                self.dma_sems[chan] = self.es.enter_context(
                    self.nc.semaphore("d%d" % len(self.dma_sems)))
                self.dma_cnt[chan] = 0
            self.dma_cnt[chan] += 1
            seq = self.dma_cnt[chan]
        else:
            chan = eng
            self.cops[eng].append(op)
            seq = len(self.cops[eng])
        if eng in COMPUTE and eng in deps and not op.isdma:
            if eng == "pe" or raw_same == 0:
                del deps[eng]
            else:
                deps[eng] = raw_same
        op.deps = deps
        op.chan = chan
        op.seq = seq
        if fn is not None:
            for r in reads:
                if r.r.get(chan, 0) < seq:
                    r.r[chan] = seq
            for r in writes:
                r.w = {chan: seq}
                r.r = {}
        self.ops[eng].append(op)
        return op

    def emit(self):
        for lst in self.ops.values():
            for op in lst:
                for ch, s in op.deps.items():
                    if ch in self.cops:
                        self.cops[ch][s - 1].signal = True
        for ch in COMPUTE:
            n = 0
            for op in self.cops[ch]:
                if op.signal:
                    n += 1
                    op.sigval = n

        def run(eng, h):
            seen = {}
            for op in self.ops[eng]:
                for ch, s in op.deps.items():
                    if ch in self.cops:
                        sem = self.sems[ch]
                        val = self.cops[ch][s - 1].sigval
                    else:
                        sem = self.dma_sems[ch]
                        val = 16 * s
                    if seen.get(ch, 0) < val:
                        h.wait_ge(sem, val)
                        seen[ch] = val
                if op.fn is None:
                    continue
                inst = op.fn(h)
                if op.isdma:
                    inst.then_inc(self.dma_sems[op.chan], 16)
                elif op.signal:
                    inst.then_inc(self.sems[op.chan], 1)

        with self.nc.Block() as block:
            @block.sync
            def _(h):
                run("sp", h)

            @block.gpsimd
            def _(h):
                run("pool", h)

            @block.scalar
            def _(h):
                run("act", h)

            @block.vector
            def _(h):
                run("dve", h)

            @block.tensor
            def _(h):
                run("pe", h)


class Ring:
    def __init__(self, S_, tensor, n):
        self.t = tensor
        self.n = n
        self.res = [S_.res("ring%d" % i) for i in range(n)]
        self.i = 0

    def next(self):
        i = self.i % self.n
        self.i += 1
        return i, self.res[i]


class Prog:
    def __init__(self, nseq, stages, final=True):
        self.nseq = nseq
        self.stages = stages
        self.final = final
        self.nc = bass.Bass("TRN2", target_bir_lowering=False)
        self.es = contextlib.ExitStack()
        self.dram = {}
        self.dbgs = {}
        self.dbg_res = []

    def din(self, name, shape):
        t = self.nc.dram_tensor(name, list(shape), F32, kind="ExternalInput").ap()
        self.dram[name] = t
        return t

    def sb(self, name, shape, dt=F32):
        return self.es.enter_context(self.nc.sbuf_tensor(name, list(shape), dt))

    def scr(self, off, shape, dt):
        n = 1
        for d in shape:
            n *= d
        nb = n * (4 if dt == F32 else 2)
        assert off % 4 == 0 and off + nb <= self.SCR_BYTES, (off, nb)
        v = self.SCR[:, off // 2:(off + nb) // 2]
        if dt == F32:
            v = v.bitcast(F32)
        if len(shape) == 2:
            v = v.rearrange("p (a b) -> p a b", a=shape[0])
        elif len(shape) == 3:
            v = v.rearrange("p (a b c) -> p a b c", a=shape[0], b=shape[1])
        return v

    def dbg(self, name, ap, reads, dt=F32):
        if not getattr(self, "debug_on", False) or name in self.dbgs:
            return
        t = self.nc.dram_tensor("dbg_" + name, list(ap.shape), dt, kind="ExternalOutput").ap()
        self.dbgs[name] = t
        r = self.Sc.res("dbg_" + name)
        self.dbg_res.append(r)
        self.Sc.add("sp", lambda h: h.dma_start(out=t, in_=ap), reads=list(reads), writes=[r], dma=r)

    def begin_phase(self):
        merged = {}
        for o in self.phase_res:
            for dct in (o.w, o.r):
                for ch, sq in dct.items():
                    if merged.get(ch, 0) < sq:
                        merged[ch] = sq
        self.phase_merged = merged
        self.phase_res = []

    def pres(self, name=""):
        r = self.Sc.res(name)
        r.w = dict(self.phase_merged)
        self.phase_res.append(r)
        return r

    def build(self):
        nc, es = self.nc, self.es
        nseq = self.nseq
        x = self.din("x", (nseq, S, D))
        out = nc.dram_tensor("out", [nseq, S, D], F32, kind="ExternalOutput").ap()
        norm_gains = self.din("norm_gains", (4, 2, D))
        final_norm = self.din("final_norm", (D,))
        ff_w1 = self.din("ff_w1", (4, D, DFF))
        ff_w2 = self.din("ff_w2", (4, DFF, D))
        ident_d = self.din("c_ident", (128, 128))
        tril_d = self.din("c_tril", (128, 128))
        kinds = set(k for k, _ in self.stages)
        lw = {}
        for kind, li in self.stages:
            if kind == "sg":
                p = "l%d_sg_" % li
                lw[li] = dict(
                    w_in=self.din(p + "w_in", (D, 4096)), g=self.din(p + "vnorm_g", (2048,)),
                    b=self.din(p + "vnorm_b", (2048,)), w_s=self.din(p + "w_s", (8, 128, 128)),
                    b_s=self.din(p + "b_s", (8, 128)), w_out=self.din(p + "w_out", (2048, D)))

            elif kind == "ssm":
                p = "l%d_ssm_" % li
                lw[li] = dict(
                    w_in=self.din(p + "w_in", (D, 6176)), conv_w=self.din(p + "conv_w", (4, 4096)),
                    conv_b=self.din(p + "conv_b", (4096,)), dt_bias=self.din(p + "dt_bias", (32,)),
                    a_log=self.din(p + "a_log", (32,)), d_skip=self.din(p + "d_skip", (32,)),
                    norm_g=self.din(p + "norm_g", (2048,)), w_out=self.din(p + "w_out", (2048, D)))
            elif kind == "nsa":
                p = "l%d_nsa_" % li
                lw[li] = dict(
                    w_in=self.din(p + "w_in", (D, 2608)), pos_k=self.din(p + "cmp_pos_k", (32, 64)),
                    pos_v=self.din(p + "cmp_pos_v", (32, 64)), k_w1=self.din(p + "cmp_k_w1", (2048, 256)),
                    k_w2=self.din(p + "cmp_k_w2", (256, 64)), v_w1=self.din(p + "cmp_v_w1", (2048, 256)),
                    v_w2=self.din(p + "cmp_v_w2", (256, 64)), w_out=self.din(p + "w_out", (D, D)),
                    maskc=self.din("c_maskc", (127, 2048)), ov1=self.din("c_ov1", (127, 33)),
                    selA=self.din("c_selA", (2048, 32)), selB=self.din("c_selB", (2048, 32)),
                    ex=self.din("c_ex", (32, 2048)))
        triu_d = self.din("c_triu", (128, 128))
        sl_d = self.din("c_sl", (128, 128))

        self.Sc = Sc = Sched(nc, es)
        self.phase_res = []
        self.phase_merged = {}
        self.XT = self.sb("XT", (128, 8, S), F32)
        self.HT = self.sb("HT", (128, 8, S), BF16)
        self.rXT = [Sc.res("XT%d" % i) for i in range(4)]
        self.rHT = [Sc.res("HT%d" % i) for i in range(4)]
        self.PS = es.enter_context(nc.psum_tensor("PS", [128, 8, 512], F32))
        self.rPS = [Sc.res("ps%d" % i) for i in range(8)]
        self.psi = 0
        self.SCR_BYTES = 78848
        self.SCR = self.sb("SCR", (128, self.SCR_BYTES // 2), BF16)
        self.ident = self.sb("ident", (128, 128), F32)
        self.tril = self.sb("tril", (128, 128), F32)
        self.identb = self.sb("identb", (128, 128), BF16)
        self.onesb = self.sb("onesb", (128, 128), BF16)
        self.gains = self.sb("gains", (128, 8, 8), F32)
        rC = self.rC = Sc.res("consts")
        Sc.add("sp", lambda h: h.dma_start(out=self.ident[:], in_=ident_d[:, :]), writes=[rC], dma=rC)
        rC2 = self.rC2 = Sc.res("consts2")
        Sc.add("sp", lambda h: h.dma_start(out=self.tril[:], in_=tril_d[:, :]), writes=[rC2], dma=rC2)
        self.triu = self.sb("triu", (128, 128), F32)
        self.triub = self.sb("triub", (128, 128), BF16)
        self.slb = self.sb("slb", (128, 128), BF16)
        self.onesf = self.sb("onesf", (128, 128), F32)
        rC3 = self.rC3 = Sc.res("consts3")
        Sc.add("sp", lambda h: h.dma_start(out=self.triu[:], in_=triu_d[:, :]), writes=[rC3], dma=rC3)
        rC4 = Sc.res("consts4")
        Sc.add("sp", lambda h: h.dma_start(out=self.onesf[:], in_=sl_d[:, :]), writes=[rC4], dma=rC4)
        Sc.add("dve", lambda h: h.tensor_copy(self.slb[:], self.onesf[:]), reads=[rC4], writes=[rC3])
        Sc.add("dve", lambda h: h.tensor_copy(self.triub[:], self.triu[:]), reads=[rC3], writes=[rC3])
        Sc.add("dve", lambda h: h.memset(self.onesf[:], 1.0), reads=[rC3], writes=[rC4])
        self.rC4 = rC4
        rG = self.rG = Sc.res("gains")
        self.graw = self.sb("graw", (64, 128), F32)
        Sc.add("sp", lambda h: h.dma_start(
            out=self.graw[:], in_=norm_gains.rearrange("l j (c p) -> (l j c) p", p=128)),
            writes=[rG], dma=rG)
        bg, rbg = self.bank()
        Sc.add("pe", lambda h: h.transpose(self.PS[:, bg, 0:64], self.graw[:], self.ident[0:64, 0:64]),
               reads=[rG, rC], writes=[rbg])
        Sc.add("dve", lambda h: h.tensor_copy(
            self.gains[:, :, :].rearrange("p a b -> p (a b)"), self.PS[:, bg, 0:64]),
            reads=[rbg], writes=[rG])
        self.final_norm_d = final_norm
        Sc.add("dve", lambda h: h.tensor_copy(self.identb[:], self.ident[:]), reads=[rC], writes=[rC])
        Sc.add("dve", lambda h: h.memset(self.onesb[:], 1.0 / 1024.0), writes=[rC])

        self.WR = self.sb("WR", (128, 3, 4096), BF16)
        self.wring = Ring(Sc, self.WR, 3)
        self.SQ = self.sb("SQ", (128, 2, 512), BF16)
        self.rSQ = [Sc.res("SQ0"), Sc.res("SQ1")]
        self.RS = self.sb("RS", (128, 2, 512), F32)
        self.rRS = [Sc.res("RS0"), Sc.res("RS1")]
        self.ST = self.sb("ST", (128, 64), F32)
        self.cnt = 0

        for sq in range(nseq):
            self.load_x(x, sq)
            for st in self.stages:
                kind, li = st
                if kind == "ffn":
                    self.rmsnorm(li * 2 + 1)
                    self.ffn(ff_w1[li], ff_w2[li])
                elif kind == "sg":
                    self.rmsnorm(li * 2)
                    self.sgmlp(lw[li])
                elif kind == "ssm":
                    self.rmsnorm(li * 2)
                    self.ssm(lw[li])
                elif kind == "nsa":
                    self.rmsnorm(li * 2)
                    self.nsa(lw[li])
                else:
                    raise ValueError(kind)
            self.store_x(out, sq)
        Sc.add("sp", None, writes=self.rXS + self.dbg_res)
        Sc.emit()
        return nc

    def bank(self):
        lo = getattr(self, "bank_lo", 0)
        b = lo + self.psi % (8 - lo)
        self.psi += 1
        return b, self.rPS[b]

    def bank2(self):
        if self.psi % 2:
            self.psi += 1
        b = self.psi % 8
        self.psi += 2
        return b, self.rPS[b], self.rPS[b + 1]

    def load_x(self, x, sq):
        Sc = self.Sc
        self.begin_phase()
        self.XS = self.scr(0, (2, D), F32)
        self.rXS = [self.pres("XS0"), self.pres("XS1")]
        for t in range(16):
            sl = t % 2
            rs = self.rXS[sl]
            Sc.add("sp", lambda h, sl=sl, t=t: h.dma_start(
                out=self.XS[:, sl, :], in_=x[sq, t * 128:(t + 1) * 128, :]), writes=[rs], dma=rs)
            b0, r0, r1 = self.bank2()
            b1 = b0 + 1
            for k in range(8):
                bb, rb = (b0, r0) if k < 4 else (b1, r1)
                Sc.add("pe", lambda h, sl=sl, k=k, bb=bb: h.transpose(
                    self.PS[:, bb, (k % 4) * 128:(k % 4 + 1) * 128],
                    self.XS[:, sl, k * 128:(k + 1) * 128], self.ident[:]),
                    reads=[rs, self.rC], writes=[rb])
            tb = t // 4
            Sc.add("act", lambda h, b0=b0, t=t: h.copy(
                self.XT[:, :, t * 128:(t + 1) * 128],
                self.PS[:, b0:b0 + 2, :].rearrange("p b (k t) -> p (b k) t", t=128)),
                reads=[r0, r1], writes=[self.rXT[tb]])

    def store_x(self, out, sq):
        Sc = self.Sc
        self.begin_phase()
        self.XS = self.scr(0, (2, D), F32)
        self.rXS = [self.pres("XS0"), self.pres("XS1")]
        JK = self.scr(8192, (D,), F32)
        rJK = self.pres("JK")
        rST = self.pres("ST")
        GF = self.scr(12288, (D,), F32)
        rGF = self.pres("gfin")
        Sc.add("sp", lambda h: h.dma_start(out=GF, in_=self.final_norm_d.partition_broadcast(128)),
               writes=[rGF], dma=rGF)
        for t in range(16):
            tb = t // 4
            b0, r0, r1 = self.bank2()
            b1 = b0 + 1
            for k in range(8):
                bb, rb = (b0, r0) if k < 4 else (b1, r1)
                Sc.add("pe", lambda h, k=k, bb=bb, t=t: h.transpose(
                    self.PS[:, bb, (k % 4) * 128:(k % 4 + 1) * 128],
                    self.XT[:, k, t * 128:(t + 1) * 128], self.ident[:]),
                    reads=[self.rXT[tb], self.rC], writes=[rb])
            sl = t % 2
            rs = self.rXS[sl]
            psv = self.PS[:, b0:b0 + 2, :].rearrange("p b f -> p (b f)")
            if self.final:
                Sc.add("act", lambda h, psv=psv: h.activation(
                    JK, psv, AF.Square, accum_out=self.ST[:, 0:1]),
                    reads=[r0, r1], writes=[rJK, rST])
                Sc.add("act", lambda h: h.activation(
                    self.ST[:, 1:2], self.ST[:, 0:1], AF.Sqrt, bias=EPS, scale=1.0 / D),
                    reads=[rST], writes=[rST])
                Sc.add("dve", lambda h: h.reciprocal(self.ST[:, 2:3], self.ST[:, 1:2]),
                       reads=[rST], writes=[rST])
                Sc.add("dve", lambda h, psv=psv, sl=sl: h.scalar_tensor_tensor(
                    self.XS[:, sl, :], psv, self.ST[:, 2:3], GF, ALU.mult, ALU.mult),
                    reads=[r0, r1, rST, rGF], writes=[rs])
            else:
                Sc.add("dve", lambda h, psv=psv, sl=sl: h.tensor_copy(self.XS[:, sl, :], psv),
                       reads=[r0, r1], writes=[rs])
            Sc.add("sp", lambda h, sl=sl, t=t: h.dma_start(
                out=out[sq, t * 128:(t + 1) * 128, :], in_=self.XS[:, sl, :]), reads=[rs], dma=rs)

    def rmsnorm(self, gi):
        Sc = self.Sc
        for tb in range(4):
            tsl = slice(tb * 512, (tb + 1) * 512)
            b, rb = self.bank()
            for k in range(8):
                sl = self.cnt % 2
                self.cnt += 1
                Sc.add("act", lambda h, sl=sl, k=k, tsl=tsl: h.activation(
                    self.SQ[:, sl, :], self.XT[:, k, tsl], AF.Square),
                    reads=[self.rXT[tb]], writes=[self.rSQ[sl]])
                Sc.add("pe", lambda h, sl=sl, k=k, b=b: h.matmul(
                    self.PS[:, b, :], self.onesb[:], self.SQ[:, sl, :], start=(k == 0), stop=(k == 7)),
                    reads=[self.rSQ[sl], self.rC], writes=[rb])
            rsl = tb % 2
            Sc.add("act", lambda h, rsl=rsl, b=b: h.activation(
                self.RS[:, rsl, :], self.PS[:, b, :], AF.Sqrt, bias=EPS),
                reads=[rb], writes=[self.rRS[rsl]])
            Sc.add("dve", lambda h, rsl=rsl: h.reciprocal(self.RS[:, rsl, :], self.RS[:, rsl, :]),
                   reads=[self.rRS[rsl]], writes=[self.rRS[rsl]])
            for k in range(8):
                Sc.add("dve", lambda h, rsl=rsl, k=k, tsl=tsl: h.scalar_tensor_tensor(
                    self.HT[:, k, tsl], self.XT[:, k, tsl], self.gains[:, gi, k:k + 1], self.RS[:, rsl, :],
                    ALU.mult, ALU.mult),
                    reads=[self.rXT[tb], self.rG, self.rRS[rsl]], writes=[self.rHT[tb]])

    def wload(self, src_ap, nelem):
        Sc = self.Sc
        i, r = self.wring.next()
        shp = src_ap.shape
        dst = self.WR[:, i, 0:nelem]
        if len(shp) == 3:
            dst = dst.rearrange("p (a b) -> p a b", a=shp[1])
        Sc.add("pool", lambda h: h.dma_start(out=dst, in_=src_ap), writes=[r], dma=r)
        return dst, r

    def xt_add(self, m, tb, b, rb):
        tsl = slice(tb * 512, (tb + 1) * 512)
        self.Sc.add("dve", lambda h: h.tensor_tensor(
            self.XT[:, m, tsl], self.XT[:, m, tsl], self.PS[:, b, :], ALU.add),
            reads=[rb, self.rXT[tb]], writes=[self.rXT[tb]])

    def ffn(self, w1, w2):
        Sc = self.Sc
        self.begin_phase()
        AT = self.scr(0, (2, 4, 512), BF16)
        rAT = [self.pres("AT0"), self.pres("AT1")]
        RL = self.scr(8192, (2, 512), F32)
        rRL = [self.pres("RL0"), self.pres("RL1")]
        w1v = w1.rearrange("(k p) n -> p k n", p=128)
        w2v = w2.rearrange("(c p) n -> p c n", p=128)
        pend = None
        it = 0
        for q in range(8):
            W1, r1 = self.wload(w1v[:, :, q * 512:(q + 1) * 512], 4096)
            W2, r2 = self.wload(w2v[:, q * 4:(q + 1) * 4, :], 4096)
            for tb in range(4):
                tsl = slice(tb * 512, (tb + 1) * 512)
                asl = it % 2
                it += 1
                for c in range(4):
                    b, rb = self.bank()
                    for k in range(8):
                        Sc.add("pe", lambda h, b=b, k=k, c=c, W1=W1, tsl=tsl: h.matmul(
                            self.PS[:, b, :], W1[:, k, c * 128:(c + 1) * 128], self.HT[:, k, tsl],
                            start=(k == 0), stop=(k == 7)),
                            reads=[r1, self.rHT[tb]], writes=[rb])
                    rl = self.cnt % 2
                    self.cnt += 1
                    Sc.add("act", lambda h, b=b, rl=rl: h.activation(
                        RL[:, rl, :], self.PS[:, b, :], AF.Relu),
                        reads=[rb], writes=[rRL[rl]])
                    Sc.add("dve", lambda h, c=c, asl=asl, rl=rl: h.tensor_tensor(
                        AT[:, asl, c, :], RL[:, rl, :], RL[:, rl, :], ALU.mult),
                        reads=[rRL[rl]], writes=[rAT[asl]])
                if pend is not None:
                    self.ffn2(AT, rAT, *pend)
                pend = (W2, r2, asl, tb)
        self.ffn2(AT, rAT, *pend)

    def ffn2(self, AT, rAT, W2, r2, asl, tb):
        Sc = self.Sc
        for m in range(8):
            b, rb = self.bank()
            for c in range(4):
                Sc.add("pe", lambda h, b=b, c=c, m=m: h.matmul(
                    self.PS[:, b, :], W2[:, c, m * 128:(m + 1) * 128], AT[:, asl, c, :],
                    start=(c == 0), stop=(c == 3)),
                    reads=[r2, rAT[asl]], writes=[rb])
            self.xt_add(m, tb, b, rb)

    def sgmlp(self, w):
        Sc = self.Sc
        self.begin_phase()
        UT2 = self.scr(0, (2, 16, 512), BF16)
        VB = self.scr(32768, (4, 2048), BF16)
        GB = self.scr(49152, (2, 2048), BF16)
        BS = self.scr(57344, (8, 128), F32)
        WST = self.scr(61440, (8, 128), BF16)
        T1 = self.scr(63488, (2, 512), F32)
        JK = self.scr(67584, (512,), BF16)
        WSR = self.scr(68608, (8, 128), F32)
        rUT2 = [[self.pres("UT%d_%d" % (j, i)) for i in range(16)] for j in range(2)]
        rVB = [self.pres("VB%d" % i) for i in range(4)]
        rGB, rBS, rWST, rWSR = self.pres("GB"), self.pres("BS"), self.pres("WST"), self.pres("WSR")
        rT1 = [self.pres("T1a"), self.pres("T1b")]
        rST = self.pres("STsg")
        rJK = self.pres("JKsg")
        Sc.add("pool", lambda h: h.dma_start(out=GB[:, 0, :], in_=w["g"].partition_broadcast(128)),
               writes=[rGB], dma=rGB)
        rGB2 = self.pres("GB2")
        Sc.add("pool", lambda h: h.dma_start(out=GB[:, 1, :], in_=w["b"].partition_broadcast(128)),
               writes=[rGB2], dma=rGB2)
        Sc.add("sp", lambda h: h.dma_start(
            out=BS.rearrange("p g t -> p (g t)"),
            in_=w["b_s"].rearrange("g t -> (g t)").partition_broadcast(128)), writes=[rBS], dma=rBS)
        Sc.add("sp", lambda h: h.dma_start(out=WSR, in_=w["w_s"].rearrange("g t s -> t g s")),
               writes=[rWSR], dma=rWSR)
        Sc.add("dve", lambda h: h.tensor_tensor(
            WSR, WSR, self.tril[:].unsqueeze(1).to_broadcast([128, 8, 128]), ALU.mult),
            reads=[rWSR, self.rC2], writes=[rWSR])
        for half in range(2):
            b0, rb = self.bank()
            for gg in range(4):
                g = half * 4 + gg
                Sc.add("pe", lambda h, g=g, gg=gg, b0=b0: h.transpose(
                    self.PS[:, b0, gg * 128:(gg + 1) * 128], WSR[:, g, :], self.ident[:]),
                    reads=[rWSR, self.rC], writes=[rb])
            Sc.add("act", lambda h, half=half, b0=b0: h.copy(
                WST[:, half * 4:(half + 1) * 4, :].rearrange("p g t -> p (g t)"), self.PS[:, b0, :]),
                reads=[rb], writes=[rWST])
        w_in = w["w_in"].rearrange("(k p) n -> p k n", p=128)
        w_out = w["w_out"].rearrange("(c p) n -> p c n", p=128)
        def u_phase(tb):
            tsl = slice(tb * 512, (tb + 1) * 512)
            UT, rUT = UT2[:, tb % 2], rUT2[tb % 2]
            for gu in range(4):
                Wg, rw = self.wload(w_in[:, :, gu * 512:(gu + 1) * 512], 4096)
                for fc in range(4):
                    b, rb = self.bank()
                    for k in range(8):
                        Sc.add("pe", lambda h, b=b, k=k, fc=fc, Wg=Wg, tsl=tsl: h.matmul(
                            self.PS[:, b, :], Wg[:, k, fc * 128:(fc + 1) * 128], self.HT[:, k, tsl],
                            start=(k == 0), stop=(k == 7)),
                            reads=[rw, self.rHT[tb]], writes=[rb])
                    fi = gu * 4 + fc
                    Sc.add("act", lambda h, b=b, fi=fi, UT=UT: h.activation(
                        UT[:, fi, :], self.PS[:, b, :], AF.Gelu),
                        reads=[rb], writes=[rUT[fi]])

        u_phase(0)
        for tb in range(4):
            tsl = slice(tb * 512, (tb + 1) * 512)
            UT, rUT = UT2[:, tb % 2], rUT2[tb % 2]
            for gv in range(4):
                Wg, rw = self.wload(w_in[:, :, 2048 + gv * 512:2048 + (gv + 1) * 512], 4096)
                for tc in range(4):
                    b, rb = self.bank()
                    t0 = tb * 512 + tc * 128
                    for k in range(8):
                        Sc.add("pe", lambda h, b=b, k=k, Wg=Wg, t0=t0: h.matmul(
                            self.PS[:, b, :], self.HT[:, k, t0:t0 + 128], Wg[:, k, :],
                            start=(k == 0), stop=(k == 7)),
                            reads=[rw, self.rHT[tb]], writes=[rb])
                    Sc.add("act", lambda h, b=b, tc=tc, gv=gv: h.activation(
                        VB[:, tc, gv * 512:(gv + 1) * 512], self.PS[:, b, :], AF.Gelu,
                        accum_out=self.ST[:, tc * 4 + gv:tc * 4 + gv + 1]),
                        reads=[rb], writes=[rVB[tc], rST])
                    Sc.add("act", lambda h, tc=tc, gv=gv: h.activation(
                        JK, VB[:, tc, gv * 512:(gv + 1) * 512], AF.Square,
                        accum_out=self.ST[:, 16 + tc * 4 + gv:16 + tc * 4 + gv + 1]),
                        reads=[rVB[tc]], writes=[rJK, rST])
            self.dbg("VBraw", VB, rVB, BF16)
            self.dbg("STraw", self.ST[:, :], [rST])
            Sc.add("dve", lambda h: h.reduce_sum(
                self.ST[:, 32:40], self.ST[:, 0:32].rearrange("p (a b) -> p a b", b=4), AX.X),
                reads=[rST], writes=[rST])
            Sc.add("dve", lambda h: h.tensor_scalar(
                self.ST[:, 32:40], self.ST[:, 32:40], 1.0 / 2048.0, None, ALU.mult),
                reads=[rST], writes=[rST])
            Sc.add("dve", lambda h: h.tensor_tensor(
                self.ST[:, 44:48], self.ST[:, 32:36], self.ST[:, 32:36], ALU.mult),
                reads=[rST], writes=[rST])
            Sc.add("dve", lambda h: h.tensor_tensor(
                self.ST[:, 48:52], self.ST[:, 36:40], self.ST[:, 44:48], ALU.subtract),
                reads=[rST], writes=[rST])
            Sc.add("act", lambda h: h.activation(
                self.ST[:, 52:56], self.ST[:, 48:52], AF.Sqrt, bias=EPS),
                reads=[rST], writes=[rST])
            Sc.add("dve", lambda h: h.reciprocal(self.ST[:, 40:44], self.ST[:, 52:56]),
                   reads=[rST], writes=[rST])
            if tb < 3:
                u_phase(tb + 1)
            for tc in range(4):
                Sc.add("dve", lambda h, tc=tc: h.tensor_scalar(
                    VB[:, tc, :], VB[:, tc, :], self.ST[:, 32 + tc:33 + tc], self.ST[:, 40 + tc:41 + tc],
                    ALU.subtract, ALU.mult), reads=[rVB[tc], rST], writes=[rVB[tc]])
                Sc.add("dve", lambda h, tc=tc: h.tensor_tensor(
                    VB[:, tc, :], VB[:, tc, :], GB[:, 0, :], ALU.mult),
                    reads=[rVB[tc], rGB], writes=[rVB[tc]])
                Sc.add("dve", lambda h, tc=tc: h.tensor_tensor(
                    VB[:, tc, :], VB[:, tc, :], GB[:, 1, :], ALU.add),
                    reads=[rVB[tc], rGB2], writes=[rVB[tc]])
            self.dbg("VBn", VB, rVB, BF16)
            self.dbg("STn", self.ST[:, :], [rST])
            self.dbg("WST", WST, [rWST], BF16)
            for fi in range(16):
                g = fi // 2
                b, rb = self.bank()
                for tc in range(4):
                    Sc.add("pe", lambda h, b=b, tc=tc, fi=fi, g=g: h.matmul(
                        self.PS[:, b, tc * 128:(tc + 1) * 128], VB[:, tc, fi * 128:(fi + 1) * 128],
                        WST[:, g, :], start=True, stop=True),
                        reads=[rVB[tc], rWST], writes=[rb])
                tl = fi % 2
                Sc.add("dve", lambda h, b=b, g=g, tl=tl: h.tensor_tensor(
                    T1[:, tl, :].rearrange("p (a t) -> p a t", a=4),
                    self.PS[:, b, :].rearrange("p (a t) -> p a t", a=4),
                    BS[:, g, :].unsqueeze(1).to_broadcast([128, 4, 128]), ALU.add),
                    reads=[rb, rBS], writes=[rT1[tl]])
                Sc.add("dve", lambda h, fi=fi, tl=tl, UT=UT: h.tensor_tensor(
                    UT[:, fi, :], UT[:, fi, :], T1[:, tl, :], ALU.mult),
                    reads=[rT1[tl], rUT[fi]], writes=[rUT[fi]])
            self.dbg("G", UT, rUT, BF16)
            for mg in range(4):
                Wo, rw = self.wload(w_out[:, :, mg * 256:(mg + 1) * 256], 4096)
                for mm in range(2):
                    b, rb = self.bank()
                    for fc in range(16):
                        Sc.add("pe", lambda h, b=b, fc=fc, mm=mm, Wo=Wo, UT=UT: h.matmul(
                            self.PS[:, b, :], Wo[:, fc, mm * 128:(mm + 1) * 128], UT[:, fc, :],
                            start=(fc == 0), stop=(fc == 15)),
                            reads=[rw, rUT[fc]], writes=[rb])
                    self.xt_add(mg * 2 + mm, tb, b, rb)


    def ssm(self, w):
        Sc = self.Sc
        self.begin_phase()
        o = [0]

        def al(shape, dt, name):
            n = 1
            for d_ in shape:
                n *= d_
            nb = n * (4 if dt == F32 else 2)
            nb = (nb + 31) // 32 * 32
            v = self.scr(o[0], shape, dt)
            o[0] += nb
            return v, self.pres(name)
        XC2, rXC = al((32, 256), BF16, "XC")
        Z, rZ = al((2048,), BF16, "Z")
        XTM, rXTM = al((2048,), BF16, "XTM")
        BTM, rBTM = al((1024,), BF16, "BTM")
        XDT, rXDT = al((32, 64), BF16, "XDT")
        XDD, rXDD = al((32, 64), BF16, "XDD")
        SR, rSR = al((32, 128), BF16, "SR")
        DEC, rDEC = al((32, 128), BF16, "DEC")
        CBm, rCBm = al((8, 128), BF16, "CBm")
        CARRY, rCARRY = al((2048,), F32, "CARRY")
        PREVb, rPREVb = al((2048,), BF16, "PREVb")
        NGc, rNG = al((16,), F32, "NGc")
        YT2, rYT2 = al((16, 256), BF16, "YT2")
        CW, rCW = al((5, 32), F32, "CW")
        RB, rRB = al((3, 32), F32, "RB")
        SM, rSM = al((8, 32), F32, "SM")
        ACC, rACC = al((2, 256), F32, "ACC")
        yt_off = o[0] - 8192
        CWR, rCWR = self.scr(yt_off, (128,), F32), rYT2
        CBR, rCBR = self.scr(yt_off + 512, (128,), F32), rYT2
        NGR = self.scr(yt_off + 1024, (128,), F32)
        sr_off = 16384 + 4096 + 4096 + 2048 + 4096 + 4096
        T = self.scr(sr_off, (2048,), F32)
        assert o[0] <= self.SCR_BYTES, o[0]
        w_in = w["w_in"].rearrange("(k p) n -> p k n", p=128)
        w_out = w["w_out"].rearrange("(c p) n -> p c n", p=128)
        Sc.add("sp", lambda h: h.dma_start(out=CWR, in_=w["conv_w"].rearrange("j (f p) -> (j f) p", p=128)),
               writes=[rCWR], dma=rCWR)
        Sc.add("sp", lambda h: h.dma_start(out=CBR[0:32, :], in_=w["conv_b"].rearrange("(f p) -> f p", p=128)),
               writes=[rCBR], dma=rCBR)
        b, rb = self.bank()
        Sc.add("pe", lambda h, b=b: h.transpose(self.PS[:, b, 0:128], CWR, self.ident[:]),
               reads=[rCWR, self.rC], writes=[rb])
        Sc.add("pe", lambda h, b=b: h.transpose(self.PS[:, b, 128:160], CBR[0:32, :], self.ident[0:32, 0:32]),
               reads=[rCBR, self.rC], writes=[rb])
        Sc.add("dve", lambda h, b=b: h.tensor_copy(CW.rearrange("p j f -> p (j f)"), self.PS[:, b, 0:160]),
               reads=[rb], writes=[rCW])
        for i, nm in enumerate(("dt_bias", "a_log", "d_skip")):
            Sc.add("sp", lambda h, i=i, nm=nm: h.dma_start(out=RB[:, i, :], in_=w[nm].partition_broadcast(128)),
                   writes=[rRB], dma=rRB)
        Sc.add("act", lambda h: h.activation(RB[:, 1, :], RB[:, 1, :], AF.Exp), reads=[rRB], writes=[rRB])
        Sc.add("dve", lambda h: h.tensor_scalar(RB[:, 1, :], RB[:, 1, :], -1.0, None, ALU.mult),
               reads=[rRB], writes=[rRB])
        Sc.add("sp", lambda h: h.dma_start(out=NGR[0:16, :], in_=w["norm_g"].rearrange("(f p) -> f p", p=128)),
               writes=[rYT2], dma=rYT2)
        b, rb = self.bank()
        Sc.add("pe", lambda h, b=b: h.transpose(self.PS[:, b, 0:16], NGR[0:16, :], self.ident[0:16, 0:16]),
               reads=[rYT2, self.rC], writes=[rb])
        Sc.add("dve", lambda h, b=b: h.tensor_copy(NGc, self.PS[:, b, 0:16]), reads=[rb], writes=[rNG])
        Sc.add("dve", lambda h: h.memset(CARRY, 0.0), writes=[rCARRY])
        Sc.add("dve", lambda h: h.memset(PREVb, 0.0), writes=[rPREVb])

        for cp in range(8):
            tp0 = cp * 256
            tb = tp0 // 512
            rH = self.rHT[tb]
            rHp = self.rHT[(tp0 - 3) // 512] if cp > 0 else rH
            for fg in range(8):
                Wg, rw = self.wload(w_in[:, :, 2048 + fg * 512:2048 + (fg + 1) * 512], 4096)
                for fc in range(4):
                    fch = fg * 4 + fc
                    b, rb = self.bank()
                    if cp == 0:
                        Sc.add("dve", lambda h, b=b: h.memset(self.PS[:, b, 0:3], 0.0), writes=[rb])
                        for k in range(8):
                            Sc.add("pe", lambda h, b=b, k=k, fc=fc, Wg=Wg: h.matmul(
                                self.PS[:, b, 3:259], Wg[:, k, fc * 128:(fc + 1) * 128], self.HT[:, k, 0:256],
                                start=(k == 0), stop=(k == 7)), reads=[rw, rH], writes=[rb])
                    else:
                        for k in range(8):
                            Sc.add("pe", lambda h, b=b, k=k, fc=fc, Wg=Wg, tp0=tp0: h.matmul(
                                self.PS[:, b, 0:259], Wg[:, k, fc * 128:(fc + 1) * 128],
                                self.HT[:, k, tp0 - 3:tp0 + 256],
                                start=(k == 0), stop=(k == 7)), reads=[rw, rH, rHp], writes=[rb])
                    a = fch % 2
                    Sc.add("dve", lambda h, b=b, fch=fch, a=a: h.tensor_scalar(
                        ACC[:, a, :], self.PS[:, b, 0:256], CW[:, 0, fch:fch + 1], CW[:, 4, fch:fch + 1],
                        ALU.mult, ALU.add), reads=[rb, rCW], writes=[rACC])
                    for j in range(1, 4):
                        Sc.add("dve", lambda h, b=b, fch=fch, a=a, j=j: h.scalar_tensor_tensor(
                            ACC[:, a, :], self.PS[:, b, j:j + 256], CW[:, j, fch:fch + 1], ACC[:, a, :],
                            ALU.mult, ALU.add), reads=[rb, rCW, rACC], writes=[rACC])
                    Sc.add("act", lambda h, fch=fch, a=a: h.activation(XC2[:, fch, :], ACC[:, a, :], AF.Silu),
                           reads=[rACC], writes=[rXC])
            for ci in range(2):
                c = cp * 2 + ci
                t0 = c * 128
                XC = XC2[:, :, ci * 128:(ci + 1) * 128]
                for gz in range(4):
                    Wg, rw = self.wload(w_in[:, :, gz * 512:(gz + 1) * 512], 4096)
                    b, rb = self.bank()
                    for k in range(8):
                        Sc.add("pe", lambda h, XC=XC, b=b, k=k, Wg=Wg, t0=t0: h.matmul(
                            self.PS[:, b, :], self.HT[:, k, t0:t0 + 128], Wg[:, k, :],
                            start=(k == 0), stop=(k == 7)), reads=[rw, rH], writes=[rb])
                    Sc.add("act", lambda h, XC=XC, b=b, gz=gz: h.activation(
                        Z[:, gz * 512:(gz + 1) * 512], self.PS[:, b, :], AF.Silu), reads=[rb], writes=[rZ])
                Wd, rw = self.wload(w_in[:, :, 6144:6176], 256)
                b, rb = self.bank()
                for k in range(8):
                    Sc.add("pe", lambda h, XC=XC, b=b, k=k, Wd=Wd, t0=t0: h.matmul(
                        self.PS[:, b, 0:32], self.HT[:, k, t0:t0 + 128], Wd[:, k, :],
                        start=(k == 0), stop=(k == 7)), reads=[rw, rH], writes=[rb])
                Sc.add("dve", lambda h, XC=XC, b=b: h.tensor_tensor(SM[:, 0, :], self.PS[:, b, 0:32], RB[:, 0, :], ALU.add),
                       reads=[rb, rRB], writes=[rSM])
                Sc.add("act", lambda h, XC=XC: h.activation(SM[:, 0, :], SM[:, 0, :], AF.Exp), reads=[rSM], writes=[rSM])
                Sc.add("act", lambda h, XC=XC: h.activation(SM[:, 0, :], SM[:, 0, :], AF.Ln, bias=1.0),
                       reads=[rSM], writes=[rSM])
                Sc.add("dve", lambda h, XC=XC: h.tensor_tensor(SM[:, 1, :], SM[:, 0, :], RB[:, 1, :], ALU.mult),
                       reads=[rSM, rRB], writes=[rSM])
                b, rb = self.bank()
                Sc.add("pe", lambda h, XC=XC, b=b: h.matmul(self.PS[:, b, 0:32], self.triu[:], SM[:, 1, :],
                                                     start=True, stop=True), reads=[rSM, self.rC3], writes=[rb])
                Sc.add("pe", lambda h, XC=XC, b=b: h.matmul(self.PS[:, b, 32:64], self.onesf[:], SM[:, 1, :],
                                                     start=True, stop=True), reads=[rSM, self.rC4], writes=[rb])
                Sc.add("act", lambda h, XC=XC, b=b: h.activation(
                    SM[:, 2:4, :].rearrange("p a b -> p (a b)"), self.PS[:, b, 0:64], AF.Exp),
                    reads=[rb], writes=[rSM])
                for q4 in range(6):
                    b, rb = self.bank()
                    psb = self.PS[:, b, :].bitcast(BF16)
                    for i in range(4):
                        fch = q4 * 4 + i
                        Sc.add("pe", lambda h, XC=XC, psb=psb, i=i, fch=fch: h.transpose(
                            psb[:, i * 128:(i + 1) * 128], XC[:, fch, :], self.identb[:]),
                            reads=[rXC, self.rC], writes=[rb])
                    if q4 < 4:
                        Sc.add("act", lambda h, XC=XC, psb=psb, q4=q4: h.copy(XTM[:, q4 * 512:(q4 + 1) * 512], psb[:, 0:512]),
                               reads=[rb], writes=[rXTM])
                    else:
                        Sc.add("act", lambda h, XC=XC, psb=psb, q4=q4: h.copy(
                            BTM[:, (q4 - 4) * 512:(q4 - 3) * 512], psb[:, 0:512]), reads=[rb], writes=[rBTM])
                Sc.add("dve", lambda h, XC=XC: h.tensor_tensor(
                    XDT, XTM.rearrange("p (a b) -> p a b", b=64),
                    SM[:, 0, :].unsqueeze(2).to_broadcast([128, 32, 64]), ALU.mult),
                    reads=[rXTM, rSM], writes=[rXDT])
                Sc.add("dve", lambda h, XC=XC: h.tensor_tensor(
                    SR, self.triu[:].unsqueeze(1).to_broadcast([128, 32, 128]),
                    SM[:, 1, :].unsqueeze(2).to_broadcast([128, 32, 128]), ALU.mult),
                    reads=[rSM, self.rC3], writes=[rSR])
                for hq in range(8):
                    b, rb = self.bank()
                    Sc.add("pe", lambda h, XC=XC, b=b, hq=hq: h.matmul(
                        self.PS[:, b, :], self.slb[:], SR[:, hq * 4:(hq + 1) * 4, :].rearrange("p a b -> p (a b)"),
                        start=True, stop=True), reads=[rSR, self.rC3], writes=[rb])
                    Sc.add("act", lambda h, XC=XC, b=b, hq=hq: h.activation(
                        DEC[:, hq * 4:(hq + 1) * 4, :].rearrange("p a b -> p (a b)"), self.PS[:, b, :], AF.Exp),
                        reads=[rb], writes=[rDEC])
                Sc.add("dve", lambda h, XC=XC: h.tensor_tensor(
                    XDD, XDT, DEC[:, :, 127:128].to_broadcast([128, 32, 64]), ALU.mult),
                    reads=[rXDT, rDEC], writes=[rXDD])
                for half in range(2):
                    b, rb = self.bank()
                    for gg in range(4):
                        g = half * 4 + gg
                        Sc.add("pe", lambda h, XC=XC, b=b, gg=gg, g=g: h.matmul(
                            self.PS[:, b, gg * 128:(gg + 1) * 128], XC[:, 16 + g, :], XC[:, 24 + g, :],
                            start=True, stop=True), reads=[rXC], writes=[rb])
                    Sc.add("dve", lambda h, XC=XC, b=b, half=half: h.tensor_tensor(
                        CBm[:, half * 4:(half + 1) * 4, :], self.PS[:, b, :].rearrange("p (a b) -> p a b", a=4),
                        self.triu[:].unsqueeze(1).to_broadcast([128, 4, 128]), ALU.mult),
                        reads=[rb, self.rC3], writes=[rCBm])
                Sc.add("dve", lambda h, XC=XC: h.tensor_tensor(
                    DEC.rearrange("p (g r) l -> p g r l", r=4), DEC.rearrange("p (g r) l -> p g r l", r=4),
                    CBm.unsqueeze(2).to_broadcast([128, 8, 4, 128]), ALU.mult),
                    reads=[rDEC, rCBm], writes=[rDEC])
                bo, ro0, ro1 = self.bank2()
                bo2, ro2, ro3 = self.bank2()
                rO = [ro0, ro1, ro2, ro3]
                bos = [bo, bo + 1, bo2, bo2 + 1]
                for g in range(8):
                    Sc.add("pe", lambda h, XC=XC, g=g, bb=bos[g // 2]: h.matmul(
                        self.PS[:, bb, (g % 2) * 256:(g % 2 + 1) * 256], XC[:, 24 + g, :],
                        PREVb[:, g * 256:(g + 1) * 256], start=True, stop=True),
                        reads=[rXC, rPREVb], writes=[rO[g // 2]])
                for q in range(4):
                    Sc.add("dve", lambda h, XC=XC, q=q, bb=bos[q]: h.tensor_tensor(
                        T[:, q * 512:(q + 1) * 512].rearrange("p (a b) -> p a b", b=64),
                        self.PS[:, bb, :].rearrange("p (a b) -> p a b", b=64),
                        SM[:, 2, q * 8:(q + 1) * 8].unsqueeze(2).to_broadcast([128, 8, 64]), ALU.mult),
                        reads=[rO[q], rSM], writes=[rSR])
                by, ry0, ry1 = self.bank2()
                by2, ry2, ry3 = self.bank2()
                rY = [ry0, ry1, ry2, ry3]
                bys = [by, by + 1, by2, by2 + 1]
                for hh in range(32):
                    Sc.add("pe", lambda h, XC=XC, hh=hh, bb=bys[hh // 8]: h.matmul(
                        self.PS[:, bb, (hh % 8) * 64:(hh % 8 + 1) * 64], DEC[:, hh, :], XDT[:, hh, :],
                        start=True, stop=True), reads=[rDEC, rXDT], writes=[rY[hh // 8]])
                for q in range(4):
                    Sc.add("dve", lambda h, XC=XC, q=q, bb=bys[q]: h.tensor_tensor(
                        T[:, q * 512:(q + 1) * 512], T[:, q * 512:(q + 1) * 512], self.PS[:, bb, :], ALU.add),
                        reads=[rY[q], rSR], writes=[rSR])
                Sc.add("dve", lambda h, XC=XC: h.tensor_tensor(
                    XDT, XTM.rearrange("p (a b) -> p a b", b=64),
                    RB[:, 2, :].unsqueeze(2).to_broadcast([128, 32, 64]), ALU.mult),
                    reads=[rXTM, rRB, rXDT], writes=[rXDT])
                Sc.add("dve", lambda h, XC=XC: h.tensor_tensor(T, T, XDT.rearrange("p a b -> p (a b)"), ALU.add),
                       reads=[rSR, rXDT], writes=[rSR])
                Sc.add("dve", lambda h, XC=XC: h.tensor_tensor(T, T, Z, ALU.mult), reads=[rSR, rZ], writes=[rSR])
                bs_, rs0, rs1 = self.bank2()
                bs2, rs2, rs3 = self.bank2()
                rS = [rs0, rs1, rs2, rs3]
                bss = [bs_, bs_ + 1, bs2, bs2 + 1]
                for g in range(8):
                    Sc.add("pe", lambda h, XC=XC, g=g, bb=bss[g // 2]: h.matmul(
                        self.PS[:, bb, (g % 2) * 256:(g % 2 + 1) * 256], BTM[:, g * 128:(g + 1) * 128],
                        XDD[:, g * 4:(g + 1) * 4, :].rearrange("p a b -> p (a b)"), start=True, stop=True),
                        reads=[rBTM, rXDD], writes=[rS[g // 2]])
                Sc.add("dve", lambda h, XC=XC: h.tensor_tensor(
                    CARRY.rearrange("p (a b) -> p a b", b=64), CARRY.rearrange("p (a b) -> p a b", b=64),
                    SM[:, 3, :].unsqueeze(2).to_broadcast([128, 32, 64]), ALU.mult),
                    reads=[rCARRY, rSM], writes=[rCARRY])
                for q in range(4):
                    Sc.add("dve", lambda h, XC=XC, q=q, bb=bss[q]: h.tensor_tensor(
                        CARRY[:, q * 512:(q + 1) * 512], CARRY[:, q * 512:(q + 1) * 512], self.PS[:, bb, :], ALU.add),
                        reads=[rS[q], rCARRY], writes=[rCARRY])
                Sc.add("act", lambda h, XC=XC: h.copy(PREVb, CARRY), reads=[rCARRY], writes=[rPREVb])
                for g in range(8):
                    Sc.add("act", lambda h, XC=XC, g=g: h.activation(
                        XDD.rearrange("p a b -> p (a b)")[:, 0:256], T[:, g * 256:(g + 1) * 256], AF.Square,
                        accum_out=SM[:, 4, g:g + 1]), reads=[rSR], writes=[rXDD, rSM])
                Sc.add("act", lambda h, XC=XC: h.activation(SM[:, 5, 0:8], SM[:, 4, 0:8], AF.Sqrt, bias=EPS, scale=1.0 / 256),
                       reads=[rSM], writes=[rSM])
                Sc.add("dve", lambda h, XC=XC: h.reciprocal(SM[:, 6, 0:8], SM[:, 5, 0:8]), reads=[rSM], writes=[rSM])
                Sc.add("dve", lambda h, XC=XC: h.tensor_tensor(
                    T.rearrange("p (a b) -> p a b", b=256), T.rearrange("p (a b) -> p a b", b=256),
                    SM[:, 6, 0:8].unsqueeze(2).to_broadcast([128, 8, 256]), ALU.mult),
                    reads=[rSR, rSM], writes=[rSR])
                Sc.add("dve", lambda h, XC=XC: h.tensor_copy(XDD.rearrange("p a b -> p (a b)"), T),
                       reads=[rSR, rXDD], writes=[rXDD])

                YN = XDD.rearrange("p a b -> p (a b)")
                for q4 in range(4):
                    b, rb = self.bank()
                    psb = self.PS[:, b, :].bitcast(BF16)
                    for i in range(4):
                        fch = q4 * 4 + i
                        Sc.add("pe", lambda h, XC=XC, psb=psb, i=i, fch=fch: h.transpose(
                            psb[:, i * 128:(i + 1) * 128], YN[:, fch * 128:(fch + 1) * 128], self.identb[:]),
                            reads=[rXDD, self.rC], writes=[rb])
                    for i in range(4):
                        fch = q4 * 4 + i
                        Sc.add("act", lambda h, XC=XC, psb=psb, i=i, fch=fch, ci=ci: h.activation(
                            YT2[:, fch, ci * 128:(ci + 1) * 128], psb[:, i * 128:(i + 1) * 128], AF.Copy,
                            scale=NGc[:, fch:fch + 1]), reads=[rb, rNG], writes=[rYT2])
            for mg in range(4):
                Wo, rw = self.wload(w_out[:, :, mg * 256:(mg + 1) * 256], 4096)
                for mm in range(2):
                    b, rb = self.bank()
                    for fc in range(16):
                        Sc.add("pe", lambda h, b=b, fc=fc, mm=mm, Wo=Wo: h.matmul(
                            self.PS[:, b, 0:256], Wo[:, fc, mm * 128:(mm + 1) * 128], YT2[:, fc, :],
                            start=(fc == 0), stop=(fc == 15)), reads=[rw, rYT2], writes=[rb])
                    m = mg * 2 + mm
                    Sc.add("dve", lambda h, b=b, m=m, tp0=tp0: h.tensor_tensor(
                        self.XT[:, m, tp0:tp0 + 256], self.XT[:, m, tp0:tp0 + 256], self.PS[:, b, 0:256], ALU.add),
                        reads=[rb, self.rXT[tb]], writes=[self.rXT[tb]])

    def nsa(self, w):
        Sc = self.Sc
        self.begin_phase()
        o = [0]

        def al(shape, dt, name):
            n = 1
            for d_ in shape:
                n *= d_
            nb = n * (4 if dt == F32 else 2)
            nb = (nb + 31) // 32 * 32
            v = self.scr(o[0], shape, dt)
            o[0] += nb
            return v, self.pres(name)
        KS, rKS = al((2, 2048), BF16, "KS")
        KW, rKW = al((2, 2048), BF16, "KW")
        KC, rKC = al((2, 2048), BF16, "KC")
        VC, rVC = al((2, 2048), BF16, "VC")
        rVC = rKC
        VS1, rVS = al((16, 4, 65), BF16, "VS1")
        VW1, rVW = al((16, 4, 65), BF16, "VW1")
        KCT, rKCT = al((2, 128), BF16, "KCT")
        VC1, rVC1 = al((4, 65), BF16, "VC1")
        HID, rHID = al((2, 128), BF16, "HID")
        POS, rPOS = al((2, 32), BF16, "POS")
        POSR, rPOSR = al((128,), F32, "POSR")
        HB, rHB = al((4,), F32, "HB")
        W2S, rW2S = al((2, 2, 128), BF16, "W2S")
        OV1, rOV = al((33,), BF16, "OV1")
        EXb, rEX = al((2048,), BF16, "EX")
        QT, rQT = self.scr(16384 + 1056, (4, 8, 128), BF16), self.pres("QT")
        OTR4 = self.scr(16384 + 1056 + 8192, (2, 512), F32)
        rOTRs = [self.pres("OTR0"), self.pres("OTR1")]
        GT, rGT = al((48,), F32, "GT")
        MC, rMC = al((128,), BF16, "MC")
        E, rE0 = al((4, 512), BF16, "E")
        rE = [rE0, self.pres("E1"), self.pres("E2"), self.pres("E3")]
        IMP, rIMP = al((4, 32), F32, "IMP")
        SA, rSA = al((2, 32), F32, "SA")
        M8, rM8 = al((16,), F32, "M8")
        SELT, rSELT = al((4, 128), BF16, "SELT")
        NUM, rNUM = self.scr(16384, (4, 65), F32), self.pres("NUM")
        OT, rOT = al((8, 512), BF16, "OT")
        SA2, rSA2 = al((32,), F32, "SA2")
        F_, rF = al((3, 16), F32, "F")
        OTM, rOTM = al((1024,), BF16, "OTM")
        assert o[0] <= self.SCR_BYTES, o[0]
        w_in = w["w_in"].rearrange("(k p) n -> p k n", p=128)
        w_out = w["w_out"].rearrange("(c p) n -> p c n", p=128)

        def wload2(base, nper, a0=0, na=None):
            na = nper if na is None else na
            i, r = self.wring.next()
            dst = self.WR[:, i, 0:8 * na * 128].rearrange("p (k a u d) -> p k a u d", k=8, a=na, u=2)
            for u in range(2):
                for a in range(na):
                    c0 = base + u * nper * 64 + (a0 + a) * 64
                    Sc.add("pool", lambda h, u=u, a=a, c0=c0: h.dma_start(
                        out=dst[:, :, a, u, :], in_=w_in[:, :, c0:c0 + 64]), writes=[r], dma=r)
            return self.WR[:, i, 0:8 * na * 128].rearrange("p (k n) -> p k n", k=8), r
        Sc.add("pool", lambda h: h.dma_start(out=OV1[0:127, :], in_=w["ov1"]), writes=[rOV], dma=rOV)
        Sc.add("pool", lambda h: h.dma_start(out=EXb[0:32, :], in_=w["ex"]), writes=[rEX], dma=rEX)
        Sc.add("dve", lambda h: h.memset(VS1, 1.0), writes=[rVS])
        Sc.add("dve", lambda h: h.memset(VW1, 1.0), writes=[rVW])
        Sc.add("dve", lambda h: h.memset(VC1, 1.0), writes=[rVC1])
        for kv, nm in enumerate(("pos_k", "pos_v")):
            for u in range(2):
                Sc.add("sp", lambda h, kv=kv, nm=nm, u=u: h.dma_start(
                    out=POSR[kv * 32:(kv + 1) * 32, u * 64:(u + 1) * 64], in_=w[nm]), writes=[rPOSR], dma=rPOSR)
        b, rb = self.bank()
        Sc.add("pe", lambda h, b=b: h.transpose(self.PS[:, b, 0:64], POSR[0:64, :], self.ident[0:64, 0:64]),
               reads=[rPOSR, self.rC], writes=[rb])
        Sc.add("dve", lambda h, b=b: h.tensor_copy(POS.rearrange("p a b -> p (a b)"), self.PS[:, b, 0:64]),
               reads=[rb], writes=[rPOS])
        for kv, nm in enumerate(("k_w2", "v_w2")):
            for u in range(2):
                Sc.add("pool", lambda h, kv=kv, nm=nm, u=u: h.dma_start(
                    out=W2S[:, kv, :, u * 64:(u + 1) * 64], in_=w[nm].rearrange("(c p) d -> p c d", p=128)),
                    writes=[rW2S], dma=rW2S)
        for (base, dstT, rT) in ((1024, KC, rKC), (1280, VC, rVC), (1536, KS, rKS), (2048, KW, rKW)):
            Wg, rw = wload2(base, 2)
            for a in range(2):
                for tb in range(4):
                    b, rb = self.bank()
                    for k in range(8):
                        Sc.add("pe", lambda h, b=b, k=k, a=a, Wg=Wg, tb=tb: h.matmul(
                            self.PS[:, b, :], Wg[:, k, a * 128:(a + 1) * 128], self.HT[:, k, tb * 512:(tb + 1) * 512],
                            start=(k == 0), stop=(k == 7)), reads=[rw, self.rHT[tb]], writes=[rb])
                    Sc.add("act", lambda h, b=b, a=a, tb=tb, dstT=dstT: h.copy(
                        dstT[:, a, tb * 512:(tb + 1) * 512], self.PS[:, b, :]), reads=[rb], writes=[rT])
        for (base, dstV, rV) in ((1792, VS1, rVS), (2304, VW1, rVW)):
            Wg, rw = self.wload(w_in[:, :, base:base + 256], 2048)
            for kt in range(16):
                b, rb = self.bank()
                for k in range(8):
                    Sc.add("pe", lambda h, b=b, k=k, Wg=Wg, kt=kt: h.matmul(
                        self.PS[:, b, 0:256], self.HT[:, k, kt * 128:(kt + 1) * 128], Wg[:, k, :],
                        start=(k == 0), stop=(k == 7)), reads=[rw, self.rHT[kt // 4]], writes=[rb])
                Sc.add("act", lambda h, b=b, kt=kt, dstV=dstV: h.copy(
                    dstV[:, kt, :, 0:64], self.PS[:, b, 0:256].rearrange("p (g d) -> p g d", d=64)),
                    reads=[rb], writes=[rV])
        for kv, (srcT, rS, w1n) in enumerate(((KC, rKC, "k_w1"), (VC, rVC, "v_w1"))):
            w1v = w[w1n].rearrange("(l d) n -> d l n", d=64)
            W1h = []
            for lh in range(2):
                i, r = self.wring.next()
                dst = self.WR[:, i, :].rearrange("p (l n) -> p l n", n=256)
                for u in range(2):
                    Sc.add("pool", lambda h, u=u, lh=lh, dst=dst, w1v=w1v: h.dma_start(
                        out=dst[u * 64:(u + 1) * 64, :, :], in_=w1v[:, lh * 16:(lh + 1) * 16, :]),
                        writes=[r], dma=r)
                W1h.append((dst, r))
            bh, rbh = self.bank()
            for hc in range(2):
                for l in range(32):
                    W1, rw = W1h[l // 16]
                    Sc.add("pe", lambda h, bh=bh, hc=hc, l=l, W1=W1, kv=kv: h.matmul(
                        self.PS[:, bh, hc:hc + 1], W1[0:64, l % 16, hc * 128:(hc + 1) * 128],
                        POS[0:64, kv, l:l + 1], start=(l == 0), stop=(l == 31)),
                        reads=[rw, rPOS], writes=[rbh])
            Sc.add("dve", lambda h, bh=bh, kv=kv: h.tensor_copy(HB[:, kv * 2:kv * 2 + 2], self.PS[:, bh, 0:2]),
                   reads=[rbh], writes=[rHB])
            for g in range(4):
                u, a = g // 2, g % 2
                for hc in range(2):
                    b, rb = self.bank()
                    for l in range(32):
                        W1, rw = W1h[l // 16]
                        Sc.add("pe", lambda h, b=b, hc=hc, l=l, W1=W1, u=u, a=a, srcT=srcT: h.matmul(
                            self.PS[:, b, 0:127], W1[u * 64:(u + 1) * 64, l % 16, hc * 128:(hc + 1) * 128],
                            srcT[u * 64:(u + 1) * 64, a, l:l + 16 * 126 + 1:16],
                            start=(l == 0), stop=(l == 31)), reads=[rw, rS], writes=[rb])
                    Sc.add("act", lambda h, b=b, hc=hc, kv=kv: h.activation(
                        HID[:, hc, 0:127], self.PS[:, b, 0:127], AF.Gelu, bias=HB[:, kv * 2 + hc:kv * 2 + hc + 1]),
                        reads=[rb, rHB], writes=[rHID])
                b, rb = self.bank()
                if kv == 0:
                    for hc in range(2):
                        Sc.add("pe", lambda h, b=b, hc=hc: h.matmul(
                            self.PS[:, b, 0:127], W2S[:, 0, hc, :], HID[:, hc, 0:127],
                            start=(hc == 0), stop=(hc == 1)), reads=[rW2S, rHID], writes=[rb])
                    Sc.add("act", lambda h, b=b, u=u, a=a: h.copy(
                        KCT[u * 64:(u + 1) * 64, a, 0:127], self.PS[u * 64:(u + 1) * 64, b, 0:127]),
                        reads=[rb], writes=[rKCT])
                else:
                    for hc in range(2):
                        Sc.add("pe", lambda h, b=b, hc=hc: h.matmul(
                            self.PS[0:127, b, 0:64], HID[:, hc, 0:127], W2S[:, 1, hc, 0:64],
                            start=(hc == 0), stop=(hc == 1)), reads=[rW2S, rHID], writes=[rb])
                    Sc.add("act", lambda h, b=b, g=g: h.copy(VC1[0:127, g, 0:64], self.PS[0:127, b, 0:64]),
                           reads=[rb], writes=[rVC1])
        for r_al in (rQT, rNUM, rOTRs[0], rOTRs[1]):
            for dct in (rKC.w, rKC.r):
                for ch, sq in dct.items():
                    if r_al.w.get(ch, 0) < sq:
                        r_al.w[ch] = sq
        self.bank_lo = 5
        self.psi = 0
        accb = [(g, self.rPS[g]) for g in range(4)]
        ecnt = [0]

        def scores(KT, rK, g, ksl, nk, qi, masks):
            u, a = g // 2, g % 2
            b, rb = self.bank()
            Sc.add("pe", lambda h, b=b: h.matmul(
                self.PS[0:nk, b, :], KT[u * 64:(u + 1) * 64, a, ksl],
                QT[u * 64:(u + 1) * 64, qi, a * 4:(a + 1) * 4, :].rearrange("p a q -> p (a q)"),
                start=True, stop=True), reads=[rK, rQT], writes=[rb])
            e = ecnt[0] % 4
            ecnt[0] += 1
            Sc.add("act", lambda h, b=b, e=e: h.activation(E[0:nk, e, :], self.PS[0:nk, b, :], AF.Exp),
                   reads=[rb], writes=[rE[e]])
            for (mk, rm) in masks:
                Sc.add("dve", lambda h, e=e, mk=mk: h.tensor_tensor(
                    E[0:nk, e, :].rearrange("p (a q) -> p a q", a=4),
                    E[0:nk, e, :].rearrange("p (a q) -> p a q", a=4),
                    mk.unsqueeze(1).to_broadcast([nk, 4, 128]), ALU.mult),
                    reads=[rE[e]] + rm, writes=[rE[e]])
            return e

        def pv(e, nk, V1, rV, g, first, last):
            bb, rbb = accb[g]
            Sc.add("pe", lambda h, e=e, bb=bb: h.matmul(
                self.PS[0:65, bb, :], V1, E[0:nk, e, :], start=first, stop=last),
                reads=[rE[e], rV], writes=[rbb])

        def finish_a(g):
            bb, rbb = accb[g]
            Sc.add("act", lambda h, bb=bb, g=g: h.copy(OTR4[0:65, g % 2, :], self.PS[0:65, bb, :]),
                   reads=[rbb], writes=[rOTRs[g % 2]])

        def finish(x_, g, first_branch):
            OTR, rOTR = OTR4[:, g % 2, :], rOTRs[g % 2]
            bt, rbt = self.bank()
            for r_ in range(4):
                Sc.add("pe", lambda h, bt=bt, r_=r_: h.transpose(
                    self.PS[:, bt, r_ * 65:(r_ + 1) * 65], OTR[0:65, r_ * 128:(r_ + 1) * 128],
                    self.ident[0:65, 0:65]), reads=[rOTR, self.rC], writes=[rbt])
            Sc.add("dve", lambda h, bt=bt: h.tensor_copy(
                NUM.rearrange("p a b -> p (a b)"), self.PS[:, bt, 0:260]), reads=[rbt], writes=[rNUM])
            self.dbg("OTR", OTR[0:65, :], [rOTR])
            self.dbg("NUM", NUM, [rNUM])
            g4 = slice(g * 4, g * 4 + 4)
            Sc.add("dve", lambda h: h.tensor_scalar(F_[:, 0, g4], NUM[:, :, 64], 1e-30, None, ALU.max),
                   reads=[rNUM], writes=[rF])
            Sc.add("dve", lambda h: h.reciprocal(F_[:, 1, g4], F_[:, 0, g4]), reads=[rF], writes=[rF])
            Sc.add("dve", lambda h, g=g, x_=x_: h.tensor_tensor(
                F_[:, 2, g4], F_[:, 1, g4],
                GT.rearrange("p (h x) -> p x h", x=3)[:, x_, g * 4:(g + 1) * 4], ALU.mult),
                reads=[rF, rGT], writes=[rF])
            self.dbg("F", F_, [rF])
            self.dbg("GT", GT, [rGT])
            ov = OTM[:, g * 256:(g + 1) * 256].rearrange("p (h d) -> p h d", d=64)
            fb = F_[:, 2, g4].unsqueeze(2).to_broadcast([128, 4, 64])
            if first_branch:
                Sc.add("dve", lambda h: h.tensor_tensor(ov, NUM[:, :, 0:64], fb, ALU.mult),
                       reads=[rNUM, rF], writes=[rOTM])
            else:
                Sc.add("dve", lambda h: h.tensor_tensor(NUM[:, :, 0:64], NUM[:, :, 0:64], fb, ALU.mult),
                       reads=[rNUM, rF], writes=[rNUM])
                Sc.add("dve", lambda h: h.tensor_tensor(ov, ov, NUM[:, :, 0:64], ALU.add),
                       reads=[rNUM, rOTM], writes=[rOTM])

        for tb in range(4):
            rH = self.rHT[tb]
            for jh in range(2):
                Wq, rw = wload2(0, 8, jh * 4, 4)
                for jj in range(4):
                    b, rb = self.bank()
                    for k in range(8):
                        Sc.add("pe", lambda h, b=b, k=k, jj=jj, Wq=Wq, tb=tb: h.matmul(
                            self.PS[:, b, :], Wq[:, k, jj * 128:(jj + 1) * 128],
                            self.HT[:, k, tb * 512:(tb + 1) * 512], start=(k == 0), stop=(k == 7)),
                            reads=[rw, rH], writes=[rb])
                    Sc.add("act", lambda h, b=b, j=jh * 4 + jj: h.activation(
                        QT[:, :, j, :], self.PS[:, b, :].rearrange("p (a q) -> p a q", a=4),
                        AF.Copy, scale=0.125), reads=[rb], writes=[rQT])
            Wg_, rwg = self.wload(w_in[:, :, 2560:2608], 384)
            for i in range(tb * 4, tb * 4 + 4):
                t0 = i * 128
                qsl = slice((i % 4) * 128, (i % 4 + 1) * 128)
                b, rb = self.bank()
                for k in range(8):
                    Sc.add("pe", lambda h, b=b, k=k, t0=t0, Wg_=Wg_: h.matmul(
                        self.PS[:, b, 0:48], self.HT[:, k, t0:t0 + 128], Wg_[:, k, :],
                        start=(k == 0), stop=(k == 7)), reads=[rwg, rH], writes=[rb])
                Sc.add("act", lambda h, b=b: h.activation(GT, self.PS[:, b, 0:48], AF.Sigmoid),
                       reads=[rb], writes=[rGT])
                Sc.add("pool", lambda h, t0=t0: h.dma_start(out=MC[0:127, :], in_=w["maskc"][:, t0:t0 + 128]),
                       writes=[rMC], dma=rMC)
                Sc.add("sp", lambda h, t0=t0: h.dma_start(out=SA[:, 0, :], in_=w["selA"][t0:t0 + 128, :]),
                       writes=[rSA], dma=rSA)
                Sc.add("sp", lambda h, t0=t0: h.dma_start(out=SA[:, 1, :], in_=w["selB"][t0:t0 + 128, :]),
                       writes=[rSA], dma=rSA)
                def run_items(items, extra2=None, pre1=None):
                    LOOK = 2
                    slots = []
                    for n in range(len(items) + LOOK):
                        if n < len(items):
                            if pre1 is not None:
                                pre1(n)
                            KT_, rK_, g_, ksl_, nk_, masks_, V1_, rV_, fi_, la_ = items[n]
                            slots.append(scores(KT_, rK_, g_, ksl_, nk_, i % 4, masks_))
                        if n >= LOOK:
                            KT_, rK_, g_, ksl_, nk_, masks_, V1_, rV_, fi_, la_ = items[n - LOOK]
                            e_ = slots[n - LOOK]
                            pv(e_, nk_, V1_, rV_, g_, fi_, la_)
                            if extra2 is not None:
                                extra2(g_, e_)

                bis = {}

                def imp_mm(g, e):
                    bi, rbi = 4, self.rPS[4]
                    bis[g] = (bi, rbi)
                    for r_ in range(4):
                        Sc.add("pe", lambda h, e=e, r_=r_, bi=bi, g=g: h.matmul(
                            self.PS[:, bi, g * 128 + r_ * 32:g * 128 + (r_ + 1) * 32],
                            E[0:127, e, r_ * 128:(r_ + 1) * 128],
                            OV1[0:127, 0:32], start=True, stop=True), reads=[rE[e], rOV], writes=[rbi])
                run_items([(KCT, rKCT, g, slice(0, 127), 127, [(MC[0:127, :], [rMC])],
                            VC1[0:127, g, :], rVC1, True, True) for g in range(4)], imp_mm)
                finish_a(0)
                for g in range(4):
                    if g < 3:
                        finish_a(g + 1)
                    finish(0, g, True)
                for g in range(4):
                    bi, rbi = bis[g]
                    c0 = g * 128
                    Sc.add("dve", lambda h, bi=bi, g=g, c0=c0: h.tensor_scalar(
                        IMP[:, g, :], self.PS[:, bi, c0:c0 + 32], F_[:, 1, g * 4:g * 4 + 1], None, ALU.mult),
                        reads=[rbi, rF], writes=[rIMP])
                    for r_ in range(1, 4):
                        Sc.add("dve", lambda h, bi=bi, g=g, r_=r_, c0=c0: h.scalar_tensor_tensor(
                            IMP[:, g, :], self.PS[:, bi, c0 + r_ * 32:c0 + r_ * 32 + 32],
                            F_[:, 1, g * 4 + r_:g * 4 + r_ + 1],
                            IMP[:, g, :], ALU.mult, ALU.add), reads=[rbi, rF, rIMP], writes=[rIMP])
                    Sc.add("dve", lambda h, g=g: h.tensor_tensor(IMP[:, g, :], IMP[:, g, :], SA[:, 0, :], ALU.mult),
                           reads=[rIMP, rSA], writes=[rIMP])
                    Sc.add("dve", lambda h, g=g: h.tensor_tensor(IMP[:, g, :], IMP[:, g, :], SA[:, 1, :], ALU.add),
                           reads=[rIMP, rSA], writes=[rIMP])
                    Sc.add("dve", lambda h, g=g: h.max(M8[:, 8:16], IMP[:, g, :]), reads=[rIMP], writes=[rM8])
                    Sc.add("dve", lambda h, g=g: h.match_replace(SA2, M8[:, 8:16], IMP[:, g, :], -3e38),
                           reads=[rIMP, rM8], writes=[rSA2])
                    Sc.add("dve", lambda h: h.max(M8[:, 8:16], SA2), reads=[rSA2], writes=[rM8])
                    Sc.add("dve", lambda h: h.tensor_reduce(M8[:, 0:1], M8[:, 8:16], AX.X, ALU.min),
                           reads=[rM8], writes=[rM8])
                    Sc.add("dve", lambda h, g=g: h.tensor_scalar(
                        IMP[:, g, :], IMP[:, g, :], M8[:, 0:1], None, ALU.is_ge), reads=[rIMP, rM8], writes=[rIMP])
                kts = [kt for kt in range(i - 4, i + 1) if kt >= 0]
                items = []
                for kt in kts:
                    for g in range(4):
                        masks = []
                        if kt == i:
                            masks.append((self.triub[:], [self.rC3]))
                        if kt == i - 4:
                            masks.append((self.slb[:], [self.rC3]))
                        items.append((KW, rKW, g, slice(kt * 128, (kt + 1) * 128), 128, masks,
                                      VW1[:, kt, g, :], rVW, kt == kts[0], kt == kts[-1]))
                run_items(items)
                bt, rbt = self.bank()
                for g in range(4):
                    Sc.add("pe", lambda h, bt=bt, g=g: h.transpose(
                        self.PS[0:32, bt, g * 128:(g + 1) * 128], IMP[:, g, :], self.ident[:]),
                        reads=[rIMP, self.rC], writes=[rbt])
                Sc.add("act", lambda h, bt=bt: h.copy(
                    SELT[0:32, :, :].rearrange("p g q -> p (g q)"), self.PS[0:32, bt, :]),
                    reads=[rbt], writes=[rSELT])
                finish_a(0)
                for g in range(4):
                    if g < 3:
                        finish_a(g + 1)
                    finish(2, g, False)
                items = []
                for kt in range(i + 1):
                    bm, rbm = 4, self.rPS[4]
                    for g in range(4):
                        masks = [(self.PS[:, bm, g * 128:(g + 1) * 128], [rbm])]
                        if kt == i:
                            masks.append((self.triub[:], [self.rC3]))
                        items.append((KS, rKS, g, slice(kt * 128, (kt + 1) * 128), 128, masks,
                                      VS1[:, kt, g, :], rVS, kt == 0, kt == i))

                def mask_mm(n):
                    if n % 4 == 0:
                        kt = n // 4
                        Sc.add("pe", lambda h, kt=kt: h.matmul(
                            self.PS[:, 4, :], EXb[0:32, kt * 128:(kt + 1) * 128],
                            SELT[0:32, :, :].rearrange("p g q -> p (g q)"),
                            start=True, stop=True), reads=[rEX, rSELT], writes=[self.rPS[4]])
                run_items(items, pre1=mask_mm)
                finish_a(0)
                for g in range(4):
                    if g < 3:
                        finish_a(g + 1)
                    finish(1, g, False)
                for q4 in range(2):
                    b, rb = self.bank()
                    psb = self.PS[:, b, :].bitcast(BF16)
                    for ii in range(4):
                        fch = q4 * 4 + ii
                        Sc.add("pe", lambda h, psb=psb, ii=ii, fch=fch: h.transpose(
                            psb[:, ii * 128:(ii + 1) * 128], OTM[:, fch * 128:(fch + 1) * 128], self.identb[:]),
                            reads=[rOTM, self.rC], writes=[rb])
                    Sc.add("act", lambda h, psb=psb, q4=q4, qsl=qsl: h.copy(
                        OT[:, q4 * 4:(q4 + 1) * 4, qsl], psb[:, 0:512].rearrange("p (a t) -> p a t", a=4)),
                        reads=[rb], writes=[rOT])
            for mg in range(2):
                Wo, rw = self.wload(w_out[:, :, mg * 512:(mg + 1) * 512], 4096)
                for mm in range(4):
                    b, rb = self.bank()
                    for fc in range(8):
                        Sc.add("pe", lambda h, b=b, fc=fc, mm=mm, Wo=Wo: h.matmul(
                            self.PS[:, b, :], Wo[:, fc, mm * 128:(mm + 1) * 128], OT[:, fc, :],
                            start=(fc == 0), stop=(fc == 7)), reads=[rw, rOT], writes=[rb])
                    self.xt_add(mg * 4 + mm, tb, b, rb)
        self.bank_lo = 0
        self.psi = 0


def consts():
    return {"c_ident": np.eye(128, dtype=np.float32),
            "c_tril": np.tril(np.ones((128, 128), dtype=np.float32)),
            "c_triu": np.triu(np.ones((128, 128), dtype=np.float32)),
            "c_sl": np.tril(np.ones((128, 128), dtype=np.float32), -1), **_nsa_consts()}


def _nsa_consts():
    t = np.arange(2048)
    n = np.arange(127)
    cmp_start = n * 16
    cmp_end = cmp_start + 31
    maskc = (cmp_end[:, None] <= t[None, :]).astype(np.float32)
    sel_start = np.arange(32) * 64
    ov = ((cmp_start[:, None] <= sel_start[None, :] + 63) & (cmp_end[:, None] >= sel_start[None, :]))
    ov1 = np.concatenate([ov.astype(np.float32), np.ones((127, 1), np.float32)], axis=1)
    j = np.arange(32)[None, :]
    cur = (t // 64)[:, None]
    forced = (j == 0) | (j == cur) | (j == cur - 1)
    future = sel_start[None, :] > t[:, None]
    selA = (~forced & ~future).astype(np.float32)
    selB = np.where(forced, 1e6, np.where(future, -1e6, 0.0)).astype(np.float32)
    ex = (np.arange(2048)[None, :] // 64 == np.arange(32)[:, None]).astype(np.float32)
    return {"c_maskc": maskc, "c_ov1": ov1, "c_selA": selA, "c_selB": selB, "c_ex": ex}


_INPUT_NAMES = (
    "x", "norm_gains", "final_norm", "ff_w1", "ff_w2",
    "l0_sg_w_in", "l0_sg_vnorm_g", "l0_sg_vnorm_b", "l0_sg_w_s", "l0_sg_b_s", "l0_sg_w_out",
    "l1_ssm_w_in", "l1_ssm_conv_w", "l1_ssm_conv_b", "l1_ssm_dt_bias", "l1_ssm_a_log", "l1_ssm_d_skip",
    "l1_ssm_norm_g", "l1_ssm_w_out",
    "l2_nsa_w_in", "l2_nsa_cmp_pos_k", "l2_nsa_cmp_pos_v", "l2_nsa_cmp_k_w1", "l2_nsa_cmp_k_w2",
    "l2_nsa_cmp_v_w1", "l2_nsa_cmp_v_w2", "l2_nsa_w_out",
    "l3_sg_w_in", "l3_sg_vnorm_g", "l3_sg_vnorm_b", "l3_sg_w_s", "l3_sg_b_s", "l3_sg_w_out",
)
ALL_STAGES = [("sg", 0), ("ffn", 0), ("ssm", 1), ("ffn", 1), ("nsa", 2), ("ffn", 2), ("sg", 3), ("ffn", 3)]


def run(inputs, n_cores, nseq, stages, final=True, trace=False, debug=False):
    p = Prog(nseq, stages, final)
    p.debug_on = debug
    nc = p.build()
    names = list(p.dram.keys())
    cs = consts()
    in_maps = []
    for c in range(n_cores):
        m = {}
        for n in names:
            if n == "x":
                m[n] = np.ascontiguousarray(inputs["x"][c * nseq:(c + 1) * nseq])
            elif n in cs:
                m[n] = cs[n]
            else:
                m[n] = np.ascontiguousarray(inputs[n])
        in_maps.append(m)
    res = run_bass_kernel_spmd(nc, in_maps, core_ids=list(range(n_cores)), trace=trace)
    outs = np.concatenate([r["out"] for r in res.results], axis=0)
    res.dbg = {k: res.results[0]["dbg_" + k] for k in p.dbgs}
    return outs, res


def kernel(**inputs):
    outs, _ = run(inputs, 8, 4, ALL_STAGES, True)
    return outs.astype(np.float32, copy=False)
```

```python
import contextlib
import numpy as np
import concourse.bass as bass
import concourse.mybir as mybir
from concourse.bass_utils import run_bass_kernel_spmd

F32 = mybir.dt.float32
BF16 = mybir.dt.bfloat16
AF = mybir.ActivationFunctionType
ALU = mybir.AluOpType
AX = mybir.AxisListType

D = 1024
S = 2048
DFF = 4096
EPS = 1e-5
COMPUTE = ("pe", "act", "dve", "pool")


class Res:
    __slots__ = ("name", "w", "r", "dma_n")

    def __init__(self, name):
        self.name = name
        self.w = {}
        self.r = {}
        self.dma_n = 0


class Op:
    __slots__ = ("eng", "fn", "deps", "chan", "seq", "signal", "sigval", "isdma")


class Sched:
    def __init__(self, nc, es):
        self.nc = nc
        self.es = es
        self.ops = {e: [] for e in ("pe", "act", "dve", "pool", "sp")}
        self.cops = {e: [] for e in COMPUTE}
        self.sems = {e: es.enter_context(nc.semaphore("s_" + e)) for e in COMPUTE}
        self.dma_sems = {}
        self.dma_cnt = {}
        self.nres = 0

    def res(self, name=""):
        self.nres += 1
        return Res(name or ("r%d" % self.nres))

    def add(self, eng, fn, reads=(), writes=(), dma=None):
        deps = {}
        raw_same = 0
        for r in reads:
            for ch, s in r.w.items():
                if deps.get(ch, 0) < s:
                    deps[ch] = s
                if ch == eng and s > raw_same:
                    raw_same = s
        for r in writes:
            for ch, s in r.w.items():
                if deps.get(ch, 0) < s:
                    deps[ch] = s
            for ch, s in r.r.items():
                if deps.get(ch, 0) < s:
                    deps[ch] = s
        op = Op()
        op.eng = eng
        op.fn = fn
        op.signal = False
        op.sigval = 0
        op.isdma = dma is not None
        if fn is None:
            chan, seq = None, 0
        elif dma is not None:
            chan = ("dma", dma.name)
            if chan not in self.dma_sems:
                self.dma_sems[chan] = self.es.enter_context(
                    self.nc.semaphore("d%d" % len(self.dma_sems)))
                self.dma_cnt[chan] = 0
            self.dma_cnt[chan] += 1
            seq = self.dma_cnt[chan]
        else:
            chan = eng
            self.cops[eng].append(op)
            seq = len(self.cops[eng])
        if eng in COMPUTE and eng in deps and not op.isdma:
            if eng == "pe" or raw_same == 0:
                del deps[eng]
            else:
                deps[eng] = raw_same
        op.deps = deps
        op.chan = chan
        op.seq = seq
        if fn is not None:
            for r in reads:
                if r.r.get(chan, 0) < seq:
                    r.r[chan] = seq
            for r in writes:
                r.w = {chan: seq}
                r.r = {}
        self.ops[eng].append(op)
        return op

    def emit(self):
        for lst in self.ops.values():
            for op in lst:
                for ch, s in op.deps.items():
                    if ch in self.cops:
                        self.cops[ch][s - 1].signal = True
        for ch in COMPUTE:
            n = 0
            for op in self.cops[ch]:
                if op.signal:
                    n += 1
                    op.sigval = n

        def run(eng, h):
            seen = {}
            for op in self.ops[eng]:
                for ch, s in op.deps.items():
                    if ch in self.cops:
                        sem = self.sems[ch]
                        val = self.cops[ch][s - 1].sigval
                    else:
                        sem = self.dma_sems[ch]
                        val = 16 * s
                    if seen.get(ch, 0) < val:
                        h.wait_ge(sem, val)
                        seen[ch] = val
                if op.fn is None:
                    continue
                inst = op.fn(h)
                if op.isdma:
                    inst.then_inc(self.dma_sems[op.chan], 16)
                elif op.signal:
                    inst.then_inc(self.sems[op.chan], 1)

        with self.nc.Block() as block:
            @block.sync
            def _(h):
                run("sp", h)

            @block.gpsimd
            def _(h):
                run("pool", h)

            @block.scalar
            def _(h):
                run("act", h)

            @block.vector
            def _(h):
                run("dve", h)

            @block.tensor
            def _(h):
                run("pe", h)


class Ring:
    def __init__(self, S_, tensor, n):
        self.t = tensor
        self.n = n
        self.res = [S_.res("ring%d" % i) for i in range(n)]
        self.i = 0

    def next(self):
        i = self.i % self.n
        self.i += 1
        return i, self.res[i]


class Prog:
    def __init__(self, nseq, stages, final=True):
        self.nseq = nseq
        self.stages = stages
        self.final = final
        self.nc = bass.Bass("TRN2", target_bir_lowering=False)
        self.es = contextlib.ExitStack()
        self.dram = {}
        self.dbgs = {}
        self.dbg_res = []

    def din(self, name, shape):
        t = self.nc.dram_tensor(name, list(shape), F32, kind="ExternalInput").ap()
        self.dram[name] = t
        return t

    def sb(self, name, shape, dt=F32):
        return self.es.enter_context(self.nc.sbuf_tensor(name, list(shape), dt))

    def scr(self, off, shape, dt):
        n = 1
        for d in shape:
            n *= d
        nb = n * (4 if dt == F32 else 2)
        assert off % 4 == 0 and off + nb <= self.SCR_BYTES, (off, nb)
        v = self.SCR[:, off // 2:(off + nb) // 2]
        if dt == F32:
            v = v.bitcast(F32)
        if len(shape) == 2:
            v = v.rearrange("p (a b) -> p a b", a=shape[0])
        elif len(shape) == 3:
            v = v.rearrange("p (a b c) -> p a b c", a=shape[0], b=shape[1])
        return v

    def dbg(self, name, ap, reads, dt=F32):
        if not getattr(self, "debug_on", False) or name in self.dbgs:
            return
        t = self.nc.dram_tensor("dbg_" + name, list(ap.shape), dt, kind="ExternalOutput").ap()
        self.dbgs[name] = t
        r = self.Sc.res("dbg_" + name)
        self.dbg_res.append(r)
        self.Sc.add("sp", lambda h: h.dma_start(out=t, in_=ap), reads=list(reads), writes=[r], dma=r)

    def begin_phase(self):
        merged = {}
        for o in self.phase_res:
            for dct in (o.w, o.r):
                for ch, sq in dct.items():
                    if merged.get(ch, 0) < sq:
                        merged[ch] = sq
        self.phase_merged = merged
        self.phase_res = []

    def pres(self, name=""):
        r = self.Sc.res(name)
        r.w = dict(self.phase_merged)
        self.phase_res.append(r)
        return r

    def build(self):
        nc, es = self.nc, self.es
        nseq = self.nseq
        x = self.din("x", (nseq, S, D))
        out = nc.dram_tensor("out", [nseq, S, D], F32, kind="ExternalOutput").ap()
        norm_gains = self.din("norm_gains", (4, 2, D))
        final_norm = self.din("final_norm", (D,))
        ff_w1 = self.din("ff_w1", (4, D, DFF))
        ff_w2 = self.din("ff_w2", (4, DFF, D))
        ident_d = self.din("c_ident", (128, 128))
        tril_d = self.din("c_tril", (128, 128))
        kinds = set(k for k, _ in self.stages)
        lw = {}
        for kind, li in self.stages:
            if kind == "sg":
                p = "l%d_sg_" % li
                lw[li] = dict(
                    w_in=self.din(p + "w_in", (D, 4096)), g=self.din(p + "vnorm_g", (2048,)),
                    b=self.din(p + "vnorm_b", (2048,)), w_s=self.din(p + "w_s", (8, 128, 128)),
                    b_s=self.din(p + "b_s", (8, 128)), w_out=self.din(p + "w_out", (2048, D)))

            elif kind == "ssm":
                p = "l%d_ssm_" % li
                lw[li] = dict(
                    w_in=self.din(p + "w_in", (D, 6176)), conv_w=self.din(p + "conv_w", (4, 4096)),
                    conv_b=self.din(p + "conv_b", (4096,)), dt_bias=self.din(p + "dt_bias", (32,)),
                    a_log=self.din(p + "a_log", (32,)), d_skip=self.din(p + "d_skip", (32,)),
                    norm_g=self.din(p + "norm_g", (2048,)), w_out=self.din(p + "w_out", (2048, D)))
            elif kind == "nsa":
                p = "l%d_nsa_" % li
                lw[li] = dict(
                    w_in=self.din(p + "w_in", (D, 2608)), pos_k=self.din(p + "cmp_pos_k", (32, 64)),
                    pos_v=self.din(p + "cmp_pos_v", (32, 64)), k_w1=self.din(p + "cmp_k_w1", (2048, 256)),
                    k_w2=self.din(p + "cmp_k_w2", (256, 64)), v_w1=self.din(p + "cmp_v_w1", (2048, 256)),
                    v_w2=self.din(p + "cmp_v_w2", (256, 64)), w_out=self.din(p + "w_out", (D, D)),
                    maskc=self.din("c_maskc", (127, 2048)), ov1=self.din("c_ov1", (127, 33)),
                    selA=self.din("c_selA", (2048, 32)), selB=self.din("c_selB", (2048, 32)),
                    ex=self.din("c_ex", (32, 2048)))
        triu_d = self.din("c_triu", (128, 128))
        sl_d = self.din("c_sl", (128, 128))

        self.Sc = Sc = Sched(nc, es)
        self.phase_res = []
        self.phase_merged = {}
        self.XT = self.sb("XT", (128, 8, S), F32)
        self.HT = self.sb("HT", (128, 8, S), BF16)
        self.rXT = [Sc.res("XT%d" % i) for i in range(4)]
        self.rHT = [Sc.res("HT%d" % i) for i in range(4)]
        self.PS = es.enter_context(nc.psum_tensor("PS", [128, 8, 512], F32))
        self.rPS = [Sc.res("ps%d" % i) for i in range(8)]
        self.psi = 0
        self.SCR_BYTES = 78848
        self.SCR = self.sb("SCR", (128, self.SCR_BYTES // 2), BF16)
        self.ident = self.sb("ident", (128, 128), F32)
        self.tril = self.sb("tril", (128, 128), F32)
        self.identb = self.sb("identb", (128, 128), BF16)
        self.onesb = self.sb("onesb", (128, 128), BF16)
        self.gains = self.sb("gains", (128, 8, 8), F32)
        rC = self.rC = Sc.res("consts")
        Sc.add("sp", lambda h: h.dma_start(out=self.ident[:], in_=ident_d[:, :]), writes=[rC], dma=rC)
        rC2 = self.rC2 = Sc.res("consts2")
        Sc.add("sp", lambda h: h.dma_start(out=self.tril[:], in_=tril_d[:, :]), writes=[rC2], dma=rC2)
        self.triu = self.sb("triu", (128, 128), F32)
        self.triub = self.sb("triub", (128, 128), BF16)
        self.slb = self.sb("slb", (128, 128), BF16)
        self.onesf = self.sb("onesf", (128, 128), F32)
        rC3 = self.rC3 = Sc.res("consts3")
        Sc.add("sp", lambda h: h.dma_start(out=self.triu[:], in_=triu_d[:, :]), writes=[rC3], dma=rC3)
        rC4 = Sc.res("consts4")
        Sc.add("sp", lambda h: h.dma_start(out=self.onesf[:], in_=sl_d[:, :]), writes=[rC4], dma=rC4)
        Sc.add("dve", lambda h: h.tensor_copy(self.slb[:], self.onesf[:]), reads=[rC4], writes=[rC3])
        Sc.add("dve", lambda h: h.tensor_copy(self.triub[:], self.triu[:]), reads=[rC3], writes=[rC3])
        Sc.add("dve", lambda h: h.memset(self.onesf[:], 1.0), reads=[rC3], writes=[rC4])
        self.rC4 = rC4
        rG = self.rG = Sc.res("gains")
        self.graw = self.sb("graw", (64, 128), F32)
        Sc.add("sp", lambda h: h.dma_start(
            out=self.graw[:], in_=norm_gains.rearrange("l j (c p) -> (l j c) p", p=128)),
            writes=[rG], dma=rG)
        bg, rbg = self.bank()
        Sc.add("pe", lambda h: h.transpose(self.PS[:, bg, 0:64], self.graw[:], self.ident[0:64, 0:64]),
               reads=[rG, rC], writes=[rbg])
        Sc.add("dve", lambda h: h.tensor_copy(
            self.gains[:, :, :].rearrange("p a b -> p (a b)"), self.PS[:, bg, 0:64]),
            reads=[rbg], writes=[rG])
        self.final_norm_d = final_norm
        Sc.add("dve", lambda h: h.tensor_copy(self.identb[:], self.ident[:]), reads=[rC], writes=[rC])
        Sc.add("dve", lambda h: h.memset(self.onesb[:], 1.0 / 1024.0), writes=[rC])

        self.WR = self.sb("WR", (128, 3, 4096), BF16)
        self.wring = Ring(Sc, self.WR, 3)
        self.SQ = self.sb("SQ", (128, 2, 512), BF16)
        self.rSQ = [Sc.res("SQ0"), Sc.res("SQ1")]
        self.RS = self.sb("RS", (128, 2, 512), F32)
        self.rRS = [Sc.res("RS0"), Sc.res("RS1")]
        self.ST = self.sb("ST", (128, 64), F32)
        self.cnt = 0

        for sq in range(nseq):
            self.load_x(x, sq)
            for st in self.stages:
                kind, li = st
                if kind == "ffn":
                    self.rmsnorm(li * 2 + 1)
                    self.ffn(ff_w1[li], ff_w2[li])
                elif kind == "sg":
                    self.rmsnorm(li * 2)
                    self.sgmlp(lw[li])
                elif kind == "ssm":
                    self.rmsnorm(li * 2)
                    self.ssm(lw[li])
                elif kind == "nsa":
                    self.rmsnorm(li * 2)
                    self.nsa(lw[li])
                else:
                    raise ValueError(kind)
            self.store_x(out, sq)
        Sc.add("sp", None, writes=self.rXS + self.dbg_res)
        Sc.emit()
        return nc

    def bank(self):
        lo = getattr(self, "bank_lo", 0)
        b = lo + self.psi % (8 - lo)
        self.psi += 1
        return b, self.rPS[b]

    def bank2(self):
        if self.psi % 2:
            self.psi += 1
        b = self.psi % 8
        self.psi += 2
        return b, self.rPS[b], self.rPS[b + 1]

    def load_x(self, x, sq):
        Sc = self.Sc
        self.begin_phase()
        self.XS = self.scr(0, (2, D), F32)
        self.rXS = [self.pres("XS0"), self.pres("XS1")]
        for t in range(16):
            sl = t % 2
            rs = self.rXS[sl]
            Sc.add("sp", lambda h, sl=sl, t=t: h.dma_start(
                out=self.XS[:, sl, :], in_=x[sq, t * 128:(t + 1) * 128, :]), writes=[rs], dma=rs)
            b0, r0, r1 = self.bank2()
            b1 = b0 + 1
            for k in range(8):
                bb, rb = (b0, r0) if k < 4 else (b1, r1)
                Sc.add("pe", lambda h, sl=sl, k=k, bb=bb: h.transpose(
                    self.PS[:, bb, (k % 4) * 128:(k % 4 + 1) * 128],
                    self.XS[:, sl, k * 128:(k + 1) * 128], self.ident[:]),
                    reads=[rs, self.rC], writes=[rb])
            tb = t // 4
            Sc.add("act", lambda h, b0=b0, t=t: h.copy(
                self.XT[:, :, t * 128:(t + 1) * 128],
                self.PS[:, b0:b0 + 2, :].rearrange("p b (k t) -> p (b k) t", t=128)),
                reads=[r0, r1], writes=[self.rXT[tb]])

    def store_x(self, out, sq):
        Sc = self.Sc
        self.begin_phase()
        self.XS = self.scr(0, (2, D), F32)
        self.rXS = [self.pres("XS0"), self.pres("XS1")]
        JK = self.scr(8192, (D,), F32)
        rJK = self.pres("JK")
        rST = self.pres("ST")
        GF = self.scr(12288, (D,), F32)
        rGF = self.pres("gfin")
        Sc.add("sp", lambda h: h.dma_start(out=GF, in_=self.final_norm_d.partition_broadcast(128)),
               writes=[rGF], dma=rGF)
        for t in range(16):
            tb = t // 4
            b0, r0, r1 = self.bank2()
            b1 = b0 + 1
            for k in range(8):
                bb, rb = (b0, r0) if k < 4 else (b1, r1)
                Sc.add("pe", lambda h, k=k, bb=bb, t=t: h.transpose(
                    self.PS[:, bb, (k % 4) * 128:(k % 4 + 1) * 128],
                    self.XT[:, k, t * 128:(t + 1) * 128], self.ident[:]),
                    reads=[self.rXT[tb], self.rC], writes=[rb])
            sl = t % 2
            rs = self.rXS[sl]
            psv = self.PS[:, b0:b0 + 2, :].rearrange("p b f -> p (b f)")
            if self.final:
                Sc.add("act", lambda h, psv=psv: h.activation(
                    JK, psv, AF.Square, accum_out=self.ST[:, 0:1]),
                    reads=[r0, r1], writes=[rJK, rST])
                Sc.add("act", lambda h: h.activation(
                    self.ST[:, 1:2], self.ST[:, 0:1], AF.Sqrt, bias=EPS, scale=1.0 / D),
                    reads=[rST], writes=[rST])
                Sc.add("dve", lambda h: h.reciprocal(self.ST[:, 2:3], self.ST[:, 1:2]),
                       reads=[rST], writes=[rST])
                Sc.add("dve", lambda h, psv=psv, sl=sl: h.scalar_tensor_tensor(
                    self.XS[:, sl, :], psv, self.ST[:, 2:3], GF, ALU.mult, ALU.mult),
                    reads=[r0, r1, rST, rGF], writes=[rs])
            else:
                Sc.add("dve", lambda h, psv=psv, sl=sl: h.tensor_copy(self.XS[:, sl, :], psv),
                       reads=[r0, r1], writes=[rs])
            Sc.add("sp", lambda h, sl=sl, t=t: h.dma_start(
                out=out[sq, t * 128:(t + 1) * 128, :], in_=self.XS[:, sl, :]), reads=[rs], dma=rs)

    def rmsnorm(self, gi):
        Sc = self.Sc
        for tb in range(4):
            tsl = slice(tb * 512, (tb + 1) * 512)
            b, rb = self.bank()
            for k in range(8):
                sl = self.cnt % 2
                self.cnt += 1
                Sc.add("act", lambda h, sl=sl, k=k, tsl=tsl: h.activation(
                    self.SQ[:, sl, :], self.XT[:, k, tsl], AF.Square),
                    reads=[self.rXT[tb]], writes=[self.rSQ[sl]])
                Sc.add("pe", lambda h, sl=sl, k=k, b=b: h.matmul(
                    self.PS[:, b, :], self.onesb[:], self.SQ[:, sl, :], start=(k == 0), stop=(k == 7)),
                    reads=[self.rSQ[sl], self.rC], writes=[rb])
            rsl = tb % 2
            Sc.add("act", lambda h, rsl=rsl, b=b: h.activation(
                self.RS[:, rsl, :], self.PS[:, b, :], AF.Sqrt, bias=EPS),
                reads=[rb], writes=[self.rRS[rsl]])
            Sc.add("dve", lambda h, rsl=rsl: h.reciprocal(self.RS[:, rsl, :], self.RS[:, rsl, :]),
                   reads=[self.rRS[rsl]], writes=[self.rRS[rsl]])
            for k in range(8):
                Sc.add("dve", lambda h, rsl=rsl, k=k, tsl=tsl: h.scalar_tensor_tensor(
                    self.HT[:, k, tsl], self.XT[:, k, tsl], self.gains[:, gi, k:k + 1], self.RS[:, rsl, :],
                    ALU.mult, ALU.mult),
                    reads=[self.rXT[tb], self.rG, self.rRS[rsl]], writes=[self.rHT[tb]])

    def wload(self, src_ap, nelem):
        Sc = self.Sc
        i, r = self.wring.next()
        shp = src_ap.shape
        dst = self.WR[:, i, 0:nelem]
        if len(shp) == 3:
            dst = dst.rearrange("p (a b) -> p a b", a=shp[1])
        Sc.add("pool", lambda h: h.dma_start(out=dst, in_=src_ap), writes=[r], dma=r)
        return dst, r

    def xt_add(self, m, tb, b, rb):
        tsl = slice(tb * 512, (tb + 1) * 512)
        self.Sc.add("dve", lambda h: h.tensor_tensor(
            self.XT[:, m, tsl], self.XT[:, m, tsl], self.PS[:, b, :], ALU.add),
            reads=[rb, self.rXT[tb]], writes=[self.rXT[tb]])

    def ffn(self, w1, w2):
        Sc = self.Sc
        self.begin_phase()
        AT = self.scr(0, (2, 4, 512), BF16)
        rAT = [self.pres("AT0"), self.pres("AT1")]
        RL = self.scr(8192, (2, 512), F32)
        rRL = [self.pres("RL0"), self.pres("RL1")]
        w1v = w1.rearrange("(k p) n -> p k n", p=128)
        w2v = w2.rearrange("(c p) n -> p c n", p=128)
        pend = None
        it = 0
        for q in range(8):
            W1, r1 = self.wload(w1v[:, :, q * 512:(q + 1) * 512], 4096)
            W2, r2 = self.wload(w2v[:, q * 4:(q + 1) * 4, :], 4096)
            for tb in range(4):
                tsl = slice(tb * 512, (tb + 1) * 512)
                asl = it % 2
                it += 1
                for c in range(4):
                    b, rb = self.bank()
                    for k in range(8):
                        Sc.add("pe", lambda h, b=b, k=k, c=c, W1=W1, tsl=tsl: h.matmul(
                            self.PS[:, b, :], W1[:, k, c * 128:(c + 1) * 128], self.HT[:, k, tsl],
                            start=(k == 0), stop=(k == 7)),
                            reads=[r1, self.rHT[tb]], writes=[rb])
                    rl = self.cnt % 2
                    self.cnt += 1
                    Sc.add("act", lambda h, b=b, rl=rl: h.activation(
                        RL[:, rl, :], self.PS[:, b, :], AF.Relu),
                        reads=[rb], writes=[rRL[rl]])
                    Sc.add("dve", lambda h, c=c, asl=asl, rl=rl: h.tensor_tensor(
                        AT[:, asl, c, :], RL[:, rl, :], RL[:, rl, :], ALU.mult),
                        reads=[rRL[rl]], writes=[rAT[asl]])
                if pend is not None:
                    self.ffn2(AT, rAT, *pend)
                pend = (W2, r2, asl, tb)
        self.ffn2(AT, rAT, *pend)

    def ffn2(self, AT, rAT, W2, r2, asl, tb):
        Sc = self.Sc
        for m in range(8):
            b, rb = self.bank()
            for c in range(4):
                Sc.add("pe", lambda h, b=b, c=c, m=m: h.matmul(
                    self.PS[:, b, :], W2[:, c, m * 128:(m + 1) * 128], AT[:, asl, c, :],
                    start=(c == 0), stop=(c == 3)),
                    reads=[r2, rAT[asl]], writes=[rb])
            self.xt_add(m, tb, b, rb)

    def sgmlp(self, w):
        Sc = self.Sc
        self.begin_phase()
        UT2 = self.scr(0, (2, 16, 512), BF16)
        VB = self.scr(32768, (4, 2048), BF16)
        GB = self.scr(49152, (2, 2048), BF16)
        BS = self.scr(57344, (8, 128), F32)
        WST = self.scr(61440, (8, 128), BF16)
        T1 = self.scr(63488, (2, 512), F32)
        JK = self.scr(67584, (512,), BF16)
        WSR = self.scr(68608, (8, 128), F32)
        rUT2 = [[self.pres("UT%d_%d" % (j, i)) for i in range(16)] for j in range(2)]
        rVB = [self.pres("VB%d" % i) for i in range(4)]
        rGB, rBS, rWST, rWSR = self.pres("GB"), self.pres("BS"), self.pres("WST"), self.pres("WSR")
        rT1 = [self.pres("T1a"), self.pres("T1b")]
        rST = self.pres("STsg")
        rJK = self.pres("JKsg")
        Sc.add("pool", lambda h: h.dma_start(out=GB[:, 0, :], in_=w["g"].partition_broadcast(128)),
               writes=[rGB], dma=rGB)
        rGB2 = self.pres("GB2")
        Sc.add("pool", lambda h: h.dma_start(out=GB[:, 1, :], in_=w["b"].partition_broadcast(128)),
               writes=[rGB2], dma=rGB2)
        Sc.add("sp", lambda h: h.dma_start(
            out=BS.rearrange("p g t -> p (g t)"),
            in_=w["b_s"].rearrange("g t -> (g t)").partition_broadcast(128)), writes=[rBS], dma=rBS)
        Sc.add("sp", lambda h: h.dma_start(out=WSR, in_=w["w_s"].rearrange("g t s -> t g s")),
               writes=[rWSR], dma=rWSR)
        Sc.add("dve", lambda h: h.tensor_tensor(
            WSR, WSR, self.tril[:].unsqueeze(1).to_broadcast([128, 8, 128]), ALU.mult),
            reads=[rWSR, self.rC2], writes=[rWSR])
        for half in range(2):
            b0, rb = self.bank()
            for gg in range(4):
                g = half * 4 + gg
                Sc.add("pe", lambda h, g=g, gg=gg, b0=b0: h.transpose(
                    self.PS[:, b0, gg * 128:(gg + 1) * 128], WSR[:, g, :], self.ident[:]),
                    reads=[rWSR, self.rC], writes=[rb])
            Sc.add("act", lambda h, half=half, b0=b0: h.copy(
                WST[:, half * 4:(half + 1) * 4, :].rearrange("p g t -> p (g t)"), self.PS[:, b0, :]),
                reads=[rb], writes=[rWST])
        w_in = w["w_in"].rearrange("(k p) n -> p k n", p=128)
        w_out = w["w_out"].rearrange("(c p) n -> p c n", p=128)
        def u_phase(tb):
            tsl = slice(tb * 512, (tb + 1) * 512)
            UT, rUT = UT2[:, tb % 2], rUT2[tb % 2]
            for gu in range(4):
                Wg, rw = self.wload(w_in[:, :, gu * 512:(gu + 1) * 512], 4096)
                for fc in range(4):
                    b, rb = self.bank()
                    for k in range(8):
                        Sc.add("pe", lambda h, b=b, k=k, fc=fc, Wg=Wg, tsl=tsl: h.matmul(
                            self.PS[:, b, :], Wg[:, k, fc * 128:(fc + 1) * 128], self.HT[:, k, tsl],
                            start=(k == 0), stop=(k == 7)),
                            reads=[rw, self.rHT[tb]], writes=[rb])
                    fi = gu * 4 + fc
                    Sc.add("act", lambda h, b=b, fi=fi, UT=UT: h.activation(
                        UT[:, fi, :], self.PS[:, b, :], AF.Gelu),
                        reads=[rb], writes=[rUT[fi]])

        u_phase(0)
        for tb in range(4):
            tsl = slice(tb * 512, (tb + 1) * 512)
            UT, rUT = UT2[:, tb % 2], rUT2[tb % 2]
            for gv in range(4):
                Wg, rw = self.wload(w_in[:, :, 2048 + gv * 512:2048 + (gv + 1) * 512], 4096)
                for tc in range(4):
                    b, rb = self.bank()
                    t0 = tb * 512 + tc * 128
                    for k in range(8):
                        Sc.add("pe", lambda h, b=b, k=k, Wg=Wg, t0=t0: h.matmul(
                            self.PS[:, b, :], self.HT[:, k, t0:t0 + 128], Wg[:, k, :],
                            start=(k == 0), stop=(k == 7)),
                            reads=[rw, self.rHT[tb]], writes=[rb])
                    Sc.add("act", lambda h, b=b, tc=tc, gv=gv: h.activation(
                        VB[:, tc, gv * 512:(gv + 1) * 512], self.PS[:, b, :], AF.Gelu,
                        accum_out=self.ST[:, tc * 4 + gv:tc * 4 + gv + 1]),
                        reads=[rb], writes=[rVB[tc], rST])
                    Sc.add("act", lambda h, tc=tc, gv=gv: h.activation(
                        JK, VB[:, tc, gv * 512:(gv + 1) * 512], AF.Square,
                        accum_out=self.ST[:, 16 + tc * 4 + gv:16 + tc * 4 + gv + 1]),
                        reads=[rVB[tc]], writes=[rJK, rST])
            self.dbg("VBraw", VB, rVB, BF16)
            self.dbg("STraw", self.ST[:, :], [rST])
            Sc.add("dve", lambda h: h.reduce_sum(
                self.ST[:, 32:40], self.ST[:, 0:32].rearrange("p (a b) -> p a b", b=4), AX.X),
                reads=[rST], writes=[rST])
            Sc.add("dve", lambda h: h.tensor_scalar(
                self.ST[:, 32:40], self.ST[:, 32:40], 1.0 / 2048.0, None, ALU.mult),
                reads=[rST], writes=[rST])
            Sc.add("dve", lambda h: h.tensor_tensor(
                self.ST[:, 44:48], self.ST[:, 32:36], self.ST[:, 32:36], ALU.mult),
                reads=[rST], writes=[rST])
            Sc.add("dve", lambda h: h.tensor_tensor(
                self.ST[:, 48:52], self.ST[:, 36:40], self.ST[:, 44:48], ALU.subtract),
                reads=[rST], writes=[rST])
            Sc.add("act", lambda h: h.activation(
                self.ST[:, 52:56], self.ST[:, 48:52], AF.Sqrt, bias=EPS),
                reads=[rST], writes=[rST])
            Sc.add("dve", lambda h: h.reciprocal(self.ST[:, 40:44], self.ST[:, 52:56]),
                   reads=[rST], writes=[rST])
            if tb < 3:
                u_phase(tb + 1)
            for tc in range(4):
                Sc.add("dve", lambda h, tc=tc: h.tensor_scalar(
                    VB[:, tc, :], VB[:, tc, :], self.ST[:, 32 + tc:33 + tc], self.ST[:, 40 + tc:41 + tc],
                    ALU.subtract, ALU.mult), reads=[rVB[tc], rST], writes=[rVB[tc]])
                Sc.add("dve", lambda h, tc=tc: h.tensor_tensor(
                    VB[:, tc, :], VB[:, tc, :], GB[:, 0, :], ALU.mult),
                    reads=[rVB[tc], rGB], writes=[rVB[tc]])
                Sc.add("dve", lambda h, tc=tc: h.tensor_tensor(
                    VB[:, tc, :], VB[:, tc, :], GB[:, 1, :], ALU.add),
                    reads=[rVB[tc], rGB2], writes=[rVB[tc]])
            self.dbg("VBn", VB, rVB, BF16)
            self.dbg("STn", self.ST[:, :], [rST])
            self.dbg("WST", WST, [rWST], BF16)
            for fi in range(16):
                g = fi // 2
                b, rb = self.bank()
                for tc in range(4):
                    Sc.add("pe", lambda h, b=b, tc=tc, fi=fi, g=g: h.matmul(
                        self.PS[:, b, tc * 128:(tc + 1) * 128], VB[:, tc, fi * 128:(fi + 1) * 128],
                        WST[:, g, :], start=True, stop=True),
                        reads=[rVB[tc], rWST], writes=[rb])
                tl = fi % 2
                Sc.add("dve", lambda h, b=b, g=g, tl=tl: h.tensor_tensor(
                    T1[:, tl, :].rearrange("p (a t) -> p a t", a=4),
                    self.PS[:, b, :].rearrange("p (a t) -> p a t", a=4),
                    BS[:, g, :].unsqueeze(1).to_broadcast([128, 4, 128]), ALU.add),
                    reads=[rb, rBS], writes=[rT1[tl]])
                Sc.add("dve", lambda h, fi=fi, tl=tl, UT=UT: h.tensor_tensor(
                    UT[:, fi, :], UT[:, fi, :], T1[:, tl, :], ALU.mult),
                    reads=[rT1[tl], rUT[fi]], writes=[rUT[fi]])
            self.dbg("G", UT, rUT, BF16)
            for mg in range(4):
                Wo, rw = self.wload(w_out[:, :, mg * 256:(mg + 1) * 256], 4096)
                for mm in range(2):
                    b, rb = self.bank()
                    for fc in range(16):
                        Sc.add("pe", lambda h, b=b, fc=fc, mm=mm, Wo=Wo, UT=UT: h.matmul(
                            self.PS[:, b, :], Wo[:, fc, mm * 128:(mm + 1) * 128], UT[:, fc, :],
                            start=(fc == 0), stop=(fc == 15)),
                            reads=[rw, rUT[fc]], writes=[rb])
                    self.xt_add(mg * 2 + mm, tb, b, rb)


    def ssm(self, w):
        Sc = self.Sc
        self.begin_phase()
        o = [0]

        def al(shape, dt, name):
            n = 1
            for d_ in shape:
                n *= d_
            nb = n * (4 if dt == F32 else 2)
            nb = (nb + 31) // 32 * 32
            v = self.scr(o[0], shape, dt)
            o[0] += nb
            return v, self.pres(name)
        XC2, rXC = al((32, 256), BF16, "XC")
        Z, rZ = al((2048,), BF16, "Z")
        XTM, rXTM = al((2048,), BF16, "XTM")
        BTM, rBTM = al((1024,), BF16, "BTM")
        XDT, rXDT = al((32, 64), BF16, "XDT")
        XDD, rXDD = al((32, 64), BF16, "XDD")
        SR, rSR = al((32, 128), BF16, "SR")
        DEC, rDEC = al((32, 128), BF16, "DEC")
        CBm, rCBm = al((8, 128), BF16, "CBm")
        CARRY, rCARRY = al((2048,), F32, "CARRY")
        PREVb, rPREVb = al((2048,), BF16, "PREVb")
        NGc, rNG = al((16,), F32, "NGc")
        YT2, rYT2 = al((16, 256), BF16, "YT2")
        CW, rCW = al((5, 32), F32, "CW")
        RB, rRB = al((3, 32), F32, "RB")
        SM, rSM = al((8, 32), F32, "SM")
        ACC, rACC = al((2, 256), F32, "ACC")
        yt_off = o[0] - 8192
        CWR, rCWR = self.scr(yt_off, (128,), F32), rYT2
        CBR, rCBR = self.scr(yt_off + 512, (128,), F32), rYT2
        NGR = self.scr(yt_off + 1024, (128,), F32)
        sr_off = 16384 + 4096 + 4096 + 2048 + 4096 + 4096
        T = self.scr(sr_off, (2048,), F32)
        assert o[0] <= self.SCR_BYTES, o[0]
        w_in = w["w_in"].rearrange("(k p) n -> p k n", p=128)
        w_out = w["w_out"].rearrange("(c p) n -> p c n", p=128)
        Sc.add("sp", lambda h: h.dma_start(out=CWR, in_=w["conv_w"].rearrange("j (f p) -> (j f) p", p=128)),
               writes=[rCWR], dma=rCWR)
        Sc.add("sp", lambda h: h.dma_start(out=CBR[0:32, :], in_=w["conv_b"].rearrange("(f p) -> f p", p=128)),
               writes=[rCBR], dma=rCBR)
        b, rb = self.bank()
        Sc.add("pe", lambda h, b=b: h.transpose(self.PS[:, b, 0:128], CWR, self.ident[:]),
               reads=[rCWR, self.rC], writes=[rb])
        Sc.add("pe", lambda h, b=b: h.transpose(self.PS[:, b, 128:160], CBR[0:32, :], self.ident[0:32, 0:32]),
               reads=[rCBR, self.rC], writes=[rb])
        Sc.add("dve", lambda h, b=b: h.tensor_copy(CW.rearrange("p j f -> p (j f)"), self.PS[:, b, 0:160]),
               reads=[rb], writes=[rCW])
        for i, nm in enumerate(("dt_bias", "a_log", "d_skip")):
            Sc.add("sp", lambda h, i=i, nm=nm: h.dma_start(out=RB[:, i, :], in_=w[nm].partition_broadcast(128)),
                   writes=[rRB], dma=rRB)
        Sc.add("act", lambda h: h.activation(RB[:, 1, :], RB[:, 1, :], AF.Exp), reads=[rRB], writes=[rRB])
        Sc.add("dve", lambda h: h.tensor_scalar(RB[:, 1, :], RB[:, 1, :], -1.0, None, ALU.mult),
               reads=[rRB], writes=[rRB])
        Sc.add("sp", lambda h: h.dma_start(out=NGR[0:16, :], in_=w["norm_g"].rearrange("(f p) -> f p", p=128)),
               writes=[rYT2], dma=rYT2)
        b, rb = self.bank()
        Sc.add("pe", lambda h, b=b: h.transpose(self.PS[:, b, 0:16], NGR[0:16, :], self.ident[0:16, 0:16]),
               reads=[rYT2, self.rC], writes=[rb])
        Sc.add("dve", lambda h, b=b: h.tensor_copy(NGc, self.PS[:, b, 0:16]), reads=[rb], writes=[rNG])
        Sc.add("dve", lambda h: h.memset(CARRY, 0.0), writes=[rCARRY])
        Sc.add("dve", lambda h: h.memset(PREVb, 0.0), writes=[rPREVb])

        for cp in range(8):
            tp0 = cp * 256
            tb = tp0 // 512
            rH = self.rHT[tb]
            rHp = self.rHT[(tp0 - 3) // 512] if cp > 0 else rH
            for fg in range(8):
                Wg, rw = self.wload(w_in[:, :, 2048 + fg * 512:2048 + (fg + 1) * 512], 4096)
                for fc in range(4):
                    fch = fg * 4 + fc
                    b, rb = self.bank()
                    if cp == 0:
                        Sc.add("dve", lambda h, b=b: h.memset(self.PS[:, b, 0:3], 0.0), writes=[rb])
                        for k in range(8):
                            Sc.add("pe", lambda h, b=b, k=k, fc=fc, Wg=Wg: h.matmul(
                                self.PS[:, b, 3:259], Wg[:, k, fc * 128:(fc + 1) * 128], self.HT[:, k, 0:256],
                                start=(k == 0), stop=(k == 7)), reads=[rw, rH], writes=[rb])
                    else:
                        for k in range(8):
                            Sc.add("pe", lambda h, b=b, k=k, fc=fc, Wg=Wg, tp0=tp0: h.matmul(
                                self.PS[:, b, 0:259], Wg[:, k, fc * 128:(fc + 1) * 128],
                                self.HT[:, k, tp0 - 3:tp0 + 256],
                                start=(k == 0), stop=(k == 7)), reads=[rw, rH, rHp], writes=[rb])
                    a = fch % 2
                    Sc.add("dve", lambda h, b=b, fch=fch, a=a: h.tensor_scalar(
                        ACC[:, a, :], self.PS[:, b, 0:256], CW[:, 0, fch:fch + 1], CW[:, 4, fch:fch + 1],
                        ALU.mult, ALU.add), reads=[rb, rCW], writes=[rACC])
                    for j in range(1, 4):
                        Sc.add("dve", lambda h, b=b, fch=fch, a=a, j=j: h.scalar_tensor_tensor(
                            ACC[:, a, :], self.PS[:, b, j:j + 256], CW[:, j, fch:fch + 1], ACC[:, a, :],
                            ALU.mult, ALU.add), reads=[rb, rCW, rACC], writes=[rACC])
                    Sc.add("act", lambda h, fch=fch, a=a: h.activation(XC2[:, fch, :], ACC[:, a, :], AF.Silu),
                           reads=[rACC], writes=[rXC])
            for ci in range(2):
                c = cp * 2 + ci
                t0 = c * 128
                XC = XC2[:, :, ci * 128:(ci + 1) * 128]
                for gz in range(4):
                    Wg, rw = self.wload(w_in[:, :, gz * 512:(gz + 1) * 512], 4096)
                    b, rb = self.bank()
                    for k in range(8):
                        Sc.add("pe", lambda h, XC=XC, b=b, k=k, Wg=Wg, t0=t0: h.matmul(
                            self.PS[:, b, :], self.HT[:, k, t0:t0 + 128], Wg[:, k, :],
                            start=(k == 0), stop=(k == 7)), reads=[rw, rH], writes=[rb])
                    Sc.add("act", lambda h, XC=XC, b=b, gz=gz: h.activation(
                        Z[:, gz * 512:(gz + 1) * 512], self.PS[:, b, :], AF.Silu), reads=[rb], writes=[rZ])
                Wd, rw = self.wload(w_in[:, :, 6144:6176], 256)
                b, rb = self.bank()
                for k in range(8):
                    Sc.add("pe", lambda h, XC=XC, b=b, k=k, Wd=Wd, t0=t0: h.matmul(
                        self.PS[:, b, 0:32], self.HT[:, k, t0:t0 + 128], Wd[:, k, :],
                        start=(k == 0), stop=(k == 7)), reads=[rw, rH], writes=[rb])
                Sc.add("dve", lambda h, XC=XC, b=b: h.tensor_tensor(SM[:, 0, :], self.PS[:, b, 0:32], RB[:, 0, :], ALU.add),
                       reads=[rb, rRB], writes=[rSM])
                Sc.add("act", lambda h, XC=XC: h.activation(SM[:, 0, :], SM[:, 0, :], AF.Exp), reads=[rSM], writes=[rSM])
                Sc.add("act", lambda h, XC=XC: h.activation(SM[:, 0, :], SM[:, 0, :], AF.Ln, bias=1.0),
                       reads=[rSM], writes=[rSM])
                Sc.add("dve", lambda h, XC=XC: h.tensor_tensor(SM[:, 1, :], SM[:, 0, :], RB[:, 1, :], ALU.mult),
                       reads=[rSM, rRB], writes=[rSM])
                b, rb = self.bank()
                Sc.add("pe", lambda h, XC=XC, b=b: h.matmul(self.PS[:, b, 0:32], self.triu[:], SM[:, 1, :],
                                                     start=True, stop=True), reads=[rSM, self.rC3], writes=[rb])
                Sc.add("pe", lambda h, XC=XC, b=b: h.matmul(self.PS[:, b, 32:64], self.onesf[:], SM[:, 1, :],
                                                     start=True, stop=True), reads=[rSM, self.rC4], writes=[rb])
                Sc.add("act", lambda h, XC=XC, b=b: h.activation(
                    SM[:, 2:4, :].rearrange("p a b -> p (a b)"), self.PS[:, b, 0:64], AF.Exp),
                    reads=[rb], writes=[rSM])
                for q4 in range(6):
                    b, rb = self.bank()
                    psb = self.PS[:, b, :].bitcast(BF16)
                    for i in range(4):
                        fch = q4 * 4 + i
                        Sc.add("pe", lambda h, XC=XC, psb=psb, i=i, fch=fch: h.transpose(
                            psb[:, i * 128:(i + 1) * 128], XC[:, fch, :], self.identb[:]),
                            reads=[rXC, self.rC], writes=[rb])
                    if q4 < 4:
                        Sc.add("act", lambda h, XC=XC, psb=psb, q4=q4: h.copy(XTM[:, q4 * 512:(q4 + 1) * 512], psb[:, 0:512]),
                               reads=[rb], writes=[rXTM])
                    else:
                        Sc.add("act", lambda h, XC=XC, psb=psb, q4=q4: h.copy(
                            BTM[:, (q4 - 4) * 512:(q4 - 3) * 512], psb[:, 0:512]), reads=[rb], writes=[rBTM])
                Sc.add("dve", lambda h, XC=XC: h.tensor_tensor(
                    XDT, XTM.rearrange("p (a b) -> p a b", b=64),
                    SM[:, 0, :].unsqueeze(2).to_broadcast([128, 32, 64]), ALU.mult),
                    reads=[rXTM, rSM], writes=[rXDT])
                Sc.add("dve", lambda h, XC=XC: h.tensor_tensor(
                    SR, self.triu[:].unsqueeze(1).to_broadcast([128, 32, 128]),
                    SM[:, 1, :].unsqueeze(2).to_broadcast([128, 32, 128]), ALU.mult),
                    reads=[rSM, self.rC3], writes=[rSR])
                for hq in range(8):
                    b, rb = self.bank()
                    Sc.add("pe", lambda h, XC=XC, b=b, hq=hq: h.matmul(
                        self.PS[:, b, :], self.slb[:], SR[:, hq * 4:(hq + 1) * 4, :].rearrange("p a b -> p (a b)"),
                        start=True, stop=True), reads=[rSR, self.rC3], writes=[rb])
                    Sc.add("act", lambda h, XC=XC, b=b, hq=hq: h.activation(
                        DEC[:, hq * 4:(hq + 1) * 4, :].rearrange("p a b -> p (a b)"), self.PS[:, b, :], AF.Exp),
                        reads=[rb], writes=[rDEC])
                Sc.add("dve", lambda h, XC=XC: h.tensor_tensor(
                    XDD, XDT, DEC[:, :, 127:128].to_broadcast([128, 32, 64]), ALU.mult),
                    reads=[rXDT, rDEC], writes=[rXDD])
                for half in range(2):
                    b, rb = self.bank()
                    for gg in range(4):
                        g = half * 4 + gg
                        Sc.add("pe", lambda h, XC=XC, b=b, gg=gg, g=g: h.matmul(
                            self.PS[:, b, gg * 128:(gg + 1) * 128], XC[:, 16 + g, :], XC[:, 24 + g, :],
                            start=True, stop=True), reads=[rXC], writes=[rb])
                    Sc.add("dve", lambda h, XC=XC, b=b, half=half: h.tensor_tensor(
                        CBm[:, half * 4:(half + 1) * 4, :], self.PS[:, b, :].rearrange("p (a b) -> p a b", a=4),
                        self.triu[:].unsqueeze(1).to_broadcast([128, 4, 128]), ALU.mult),
                        reads=[rb, self.rC3], writes=[rCBm])
                Sc.add("dve", lambda h, XC=XC: h.tensor_tensor(
                    DEC.rearrange("p (g r) l -> p g r l", r=4), DEC.rearrange("p (g r) l -> p g r l", r=4),
                    CBm.unsqueeze(2).to_broadcast([128, 8, 4, 128]), ALU.mult),
                    reads=[rDEC, rCBm], writes=[rDEC])
                bo, ro0, ro1 = self.bank2()
                bo2, ro2, ro3 = self.bank2()
                rO = [ro0, ro1, ro2, ro3]
                bos = [bo, bo + 1, bo2, bo2 + 1]
                for g in range(8):
                    Sc.add("pe", lambda h, XC=XC, g=g, bb=bos[g // 2]: h.matmul(
                        self.PS[:, bb, (g % 2) * 256:(g % 2 + 1) * 256], XC[:, 24 + g, :],
                        PREVb[:, g * 256:(g + 1) * 256], start=True, stop=True),
                        reads=[rXC, rPREVb], writes=[rO[g // 2]])
                for q in range(4):
                    Sc.add("dve", lambda h, XC=XC, q=q, bb=bos[q]: h.tensor_tensor(
                        T[:, q * 512:(q + 1) * 512].rearrange("p (a b) -> p a b", b=64),
                        self.PS[:, bb, :].rearrange("p (a b) -> p a b", b=64),
                        SM[:, 2, q * 8:(q + 1) * 8].unsqueeze(2).to_broadcast([128, 8, 64]), ALU.mult),
                        reads=[rO[q], rSM], writes=[rSR])
                by, ry0, ry1 = self.bank2()
                by2, ry2, ry3 = self.bank2()
                rY = [ry0, ry1, ry2, ry3]
                bys = [by, by + 1, by2, by2 + 1]
                for hh in range(32):
                    Sc.add("pe", lambda h, XC=XC, hh=hh, bb=bys[hh // 8]: h.matmul(
                        self.PS[:, bb, (hh % 8) * 64:(hh % 8 + 1) * 64], DEC[:, hh, :], XDT[:, hh, :],
                        start=True, stop=True), reads=[rDEC, rXDT], writes=[rY[hh // 8]])
                for q in range(4):
                    Sc.add("dve", lambda h, XC=XC, q=q, bb=bys[q]: h.tensor_tensor(
                        T[:, q * 512:(q + 1) * 512], T[:, q * 512:(q + 1) * 512], self.PS[:, bb, :], ALU.add),
                        reads=[rY[q], rSR], writes=[rSR])
                Sc.add("dve", lambda h, XC=XC: h.tensor_tensor(
                    XDT, XTM.rearrange("p (a b) -> p a b", b=64),
                    RB[:, 2, :].unsqueeze(2).to_broadcast([128, 32, 64]), ALU.mult),
                    reads=[rXTM, rRB, rXDT], writes=[rXDT])
                Sc.add("dve", lambda h, XC=XC: h.tensor_tensor(T, T, XDT.rearrange("p a b -> p (a b)"), ALU.add),
                       reads=[rSR, rXDT], writes=[rSR])
                Sc.add("dve", lambda h, XC=XC: h.tensor_tensor(T, T, Z, ALU.mult), reads=[rSR, rZ], writes=[rSR])
                bs_, rs0, rs1 = self.bank2()
                bs2, rs2, rs3 = self.bank2()
                rS = [rs0, rs1, rs2, rs3]
                bss = [bs_, bs_ + 1, bs2, bs2 + 1]
                for g in range(8):
                    Sc.add("pe", lambda h, XC=XC, g=g, bb=bss[g // 2]: h.matmul(
                        self.PS[:, bb, (g % 2) * 256:(g % 2 + 1) * 256], BTM[:, g * 128:(g + 1) * 128],
                        XDD[:, g * 4:(g + 1) * 4, :].rearrange("p a b -> p (a b)"), start=True, stop=True),
                        reads=[rBTM, rXDD], writes=[rS[g // 2]])
                Sc.add("dve", lambda h, XC=XC: h.tensor_tensor(
                    CARRY.rearrange("p (a b) -> p a b", b=64), CARRY.rearrange("p (a b) -> p a b", b=64),
                    SM[:, 3, :].unsqueeze(2).to_broadcast([128, 32, 64]), ALU.mult),
                    reads=[rCARRY, rSM], writes=[rCARRY])
                for q in range(4):
                    Sc.add("dve", lambda h, XC=XC, q=q, bb=bss[q]: h.tensor_tensor(
                        CARRY[:, q * 512:(q + 1) * 512], CARRY[:, q * 512:(q + 1) * 512], self.PS[:, bb, :], ALU.add),
                        reads=[rS[q], rCARRY], writes=[rCARRY])
                Sc.add("act", lambda h, XC=XC: h.copy(PREVb, CARRY), reads=[rCARRY], writes=[rPREVb])
                for g in range(8):
                    Sc.add("act", lambda h, XC=XC, g=g: h.activation(
                        XDD.rearrange("p a b -> p (a b)")[:, 0:256], T[:, g * 256:(g + 1) * 256], AF.Square,
                        accum_out=SM[:, 4, g:g + 1]), reads=[rSR], writes=[rXDD, rSM])
                Sc.add("act", lambda h, XC=XC: h.activation(SM[:, 5, 0:8], SM[:, 4, 0:8], AF.Sqrt, bias=EPS, scale=1.0 / 256),
                       reads=[rSM], writes=[rSM])
                Sc.add("dve", lambda h, XC=XC: h.reciprocal(SM[:, 6, 0:8], SM[:, 5, 0:8]), reads=[rSM], writes=[rSM])
                Sc.add("dve", lambda h, XC=XC: h.tensor_tensor(
                    T.rearrange("p (a b) -> p a b", b=256), T.rearrange("p (a b) -> p a b", b=256),
                    SM[:, 6, 0:8].unsqueeze(2).to_broadcast([128, 8, 256]), ALU.mult),
                    reads=[rSR, rSM], writes=[rSR])
                Sc.add("dve", lambda h, XC=XC: h.tensor_copy(XDD.rearrange("p a b -> p (a b)"), T),
                       reads=[rSR, rXDD], writes=[rXDD])

                YN = XDD.rearrange("p a b -> p (a b)")
                for q4 in range(4):
                    b, rb = self.bank()
                    psb = self.PS[:, b, :].bitcast(BF16)
                    for i in range(4):
                        fch = q4 * 4 + i
                        Sc.add("pe", lambda h, XC=XC, psb=psb, i=i, fch=fch: h.transpose(
                            psb[:, i * 128:(i + 1) * 128], YN[:, fch * 128:(fch + 1) * 128], self.identb[:]),
                            reads=[rXDD, self.rC], writes=[rb])
                    for i in range(4):
                        fch = q4 * 4 + i
                        Sc.add("act", lambda h, XC=XC, psb=psb, i=i, fch=fch, ci=ci: h.activation(
                            YT2[:, fch, ci * 128:(ci + 1) * 128], psb[:, i * 128:(i + 1) * 128], AF.Copy,
                            scale=NGc[:, fch:fch + 1]), reads=[rb, rNG], writes=[rYT2])
            for mg in range(4):
                Wo, rw = self.wload(w_out[:, :, mg * 256:(mg + 1) * 256], 4096)
                for mm in range(2):
                    b, rb = self.bank()
                    for fc in range(16):
                        Sc.add("pe", lambda h, b=b, fc=fc, mm=mm, Wo=Wo: h.matmul(
                            self.PS[:, b, 0:256], Wo[:, fc, mm * 128:(mm + 1) * 128], YT2[:, fc, :],
                            start=(fc == 0), stop=(fc == 15)), reads=[rw, rYT2], writes=[rb])
                    m = mg * 2 + mm
                    Sc.add("dve", lambda h, b=b, m=m, tp0=tp0: h.tensor_tensor(
                        self.XT[:, m, tp0:tp0 + 256], self.XT[:, m, tp0:tp0 + 256], self.PS[:, b, 0:256], ALU.add),
                        reads=[rb, self.rXT[tb]], writes=[self.rXT[tb]])

    def nsa(self, w):
        Sc = self.Sc
        self.begin_phase()
        o = [0]

        def al(shape, dt, name):
            n = 1
            for d_ in shape:
                n *= d_
            nb = n * (4 if dt == F32 else 2)
            nb = (nb + 31) // 32 * 32
            v = self.scr(o[0], shape, dt)
            o[0] += nb
            return v, self.pres(name)
        KS, rKS = al((2, 2048), BF16, "KS")
        KW, rKW = al((2, 2048), BF16, "KW")
        KC, rKC = al((2, 2048), BF16, "KC")
        VC, rVC = al((2, 2048), BF16, "VC")
        rVC = rKC
        VS1, rVS = al((16, 4, 65), BF16, "VS1")
        VW1, rVW = al((16, 4, 65), BF16, "VW1")
        KCT, rKCT = al((2, 128), BF16, "KCT")
        VC1, rVC1 = al((4, 65), BF16, "VC1")
        HID, rHID = al((2, 128), BF16, "HID")
        POS, rPOS = al((2, 32), BF16, "POS")
        POSR, rPOSR = al((128,), F32, "POSR")
        HB, rHB = al((4,), F32, "HB")
        W2S, rW2S = al((2, 2, 128), BF16, "W2S")
        OV1, rOV = al((33,), BF16, "OV1")
        EXb, rEX = al((2048,), BF16, "EX")
        QT, rQT = self.scr(16384 + 1056, (4, 8, 128), BF16), self.pres("QT")
        OTR4 = self.scr(16384 + 1056 + 8192, (2, 512), F32)
        rOTRs = [self.pres("OTR0"), self.pres("OTR1")]
        GT, rGT = al((48,), F32, "GT")
        MC, rMC = al((128,), BF16, "MC")
        E, rE0 = al((4, 512), BF16, "E")
        rE = [rE0, self.pres("E1"), self.pres("E2"), self.pres("E3")]
        IMP, rIMP = al((4, 32), F32, "IMP")
        SA, rSA = al((2, 32), F32, "SA")
        M8, rM8 = al((72,), F32, "M8")
        SELT, rSELT = al((4, 128), BF16, "SELT")
        NUM, rNUM = self.scr(16384, (4, 65), F32), self.pres("NUM")
        OT, rOT = al((8, 512), BF16, "OT")
        SA2, rSA2 = al((4, 32), F32, "SA2")
        TMPI, rTMPI = al((512,), F32, "TMPI")
        F_, rF = al((3, 16), F32, "F")
        OTM, rOTM = al((1024,), BF16, "OTM")
        assert o[0] <= self.SCR_BYTES, o[0]
        w_in = w["w_in"].rearrange("(k p) n -> p k n", p=128)
        w_out = w["w_out"].rearrange("(c p) n -> p c n", p=128)

        def wload2(base, nper, a0=0, na=None):
            na = nper if na is None else na
            i, r = self.wring.next()
            dst = self.WR[:, i, 0:8 * na * 128].rearrange("p (k a u d) -> p k a u d", k=8, a=na, u=2)
            for u in range(2):
                for a in range(na):
                    c0 = base + u * nper * 64 + (a0 + a) * 64
                    Sc.add("pool", lambda h, u=u, a=a, c0=c0: h.dma_start(
                        out=dst[:, :, a, u, :], in_=w_in[:, :, c0:c0 + 64]), writes=[r], dma=r)
            return self.WR[:, i, 0:8 * na * 128].rearrange("p (k n) -> p k n", k=8), r
        Sc.add("pool", lambda h: h.dma_start(out=OV1[0:127, :], in_=w["ov1"]), writes=[rOV], dma=rOV)
        Sc.add("pool", lambda h: h.dma_start(out=EXb[0:32, :], in_=w["ex"]), writes=[rEX], dma=rEX)
        Sc.add("dve", lambda h: h.memset(VS1, 1.0), writes=[rVS])
        Sc.add("dve", lambda h: h.memset(VW1, 1.0), writes=[rVW])
        Sc.add("dve", lambda h: h.memset(VC1, 1.0), writes=[rVC1])
        for kv, nm in enumerate(("pos_k", "pos_v")):
            for u in range(2):
                Sc.add("sp", lambda h, kv=kv, nm=nm, u=u: h.dma_start(
                    out=POSR[kv * 32:(kv + 1) * 32, u * 64:(u + 1) * 64], in_=w[nm]), writes=[rPOSR], dma=rPOSR)
        b, rb = self.bank()
        Sc.add("pe", lambda h, b=b: h.transpose(self.PS[:, b, 0:64], POSR[0:64, :], self.ident[0:64, 0:64]),
               reads=[rPOSR, self.rC], writes=[rb])
        Sc.add("dve", lambda h, b=b: h.tensor_copy(POS.rearrange("p a b -> p (a b)"), self.PS[:, b, 0:64]),
               reads=[rb], writes=[rPOS])
        for kv, nm in enumerate(("k_w2", "v_w2")):
            for u in range(2):
                Sc.add("pool", lambda h, kv=kv, nm=nm, u=u: h.dma_start(
                    out=W2S[:, kv, :, u * 64:(u + 1) * 64], in_=w[nm].rearrange("(c p) d -> p c d", p=128)),
                    writes=[rW2S], dma=rW2S)
        for (base, dstT, rT) in ((1024, KC, rKC), (1280, VC, rVC), (1536, KS, rKS), (2048, KW, rKW)):
            Wg, rw = wload2(base, 2)
            for a in range(2):
                for tb in range(4):
                    b, rb = self.bank()
                    for k in range(8):
                        Sc.add("pe", lambda h, b=b, k=k, a=a, Wg=Wg, tb=tb: h.matmul(
                            self.PS[:, b, :], Wg[:, k, a * 128:(a + 1) * 128], self.HT[:, k, tb * 512:(tb + 1) * 512],
                            start=(k == 0), stop=(k == 7)), reads=[rw, self.rHT[tb]], writes=[rb])
                    Sc.add("act", lambda h, b=b, a=a, tb=tb, dstT=dstT: h.copy(
                        dstT[:, a, tb * 512:(tb + 1) * 512], self.PS[:, b, :]), reads=[rb], writes=[rT])
        for (base, dstV, rV) in ((1792, VS1, rVS), (2304, VW1, rVW)):
            Wg, rw = self.wload(w_in[:, :, base:base + 256], 2048)
            for kt in range(16):
                b, rb = self.bank()
                for k in range(8):
                    Sc.add("pe", lambda h, b=b, k=k, Wg=Wg, kt=kt: h.matmul(
                        self.PS[:, b, 0:256], self.HT[:, k, kt * 128:(kt + 1) * 128], Wg[:, k, :],
                        start=(k == 0), stop=(k == 7)), reads=[rw, self.rHT[kt // 4]], writes=[rb])
                Sc.add("act", lambda h, b=b, kt=kt, dstV=dstV: h.copy(
                    dstV[:, kt, :, 0:64], self.PS[:, b, 0:256].rearrange("p (g d) -> p g d", d=64)),
                    reads=[rb], writes=[rV])
        for kv, (srcT, rS, w1n) in enumerate(((KC, rKC, "k_w1"), (VC, rVC, "v_w1"))):
            w1v = w[w1n].rearrange("(l d) n -> d l n", d=64)
            W1h = []
            for lh in range(2):
                i, r = self.wring.next()
                dst = self.WR[:, i, :].rearrange("p (l n) -> p l n", n=256)
                for u in range(2):
                    Sc.add("pool", lambda h, u=u, lh=lh, dst=dst, w1v=w1v: h.dma_start(
                        out=dst[u * 64:(u + 1) * 64, :, :], in_=w1v[:, lh * 16:(lh + 1) * 16, :]),
                        writes=[r], dma=r)
                W1h.append((dst, r))
            bh, rbh = self.bank()
            for hc in range(2):
                for l in range(32):
                    W1, rw = W1h[l // 16]
                    Sc.add("pe", lambda h, bh=bh, hc=hc, l=l, W1=W1, kv=kv: h.matmul(
                        self.PS[:, bh, hc:hc + 1], W1[0:64, l % 16, hc * 128:(hc + 1) * 128],
                        POS[0:64, kv, l:l + 1], start=(l == 0), stop=(l == 31)),
                        reads=[rw, rPOS], writes=[rbh])
            Sc.add("dve", lambda h, bh=bh, kv=kv: h.tensor_copy(HB[:, kv * 2:kv * 2 + 2], self.PS[:, bh, 0:2]),
                   reads=[rbh], writes=[rHB])
            for g in range(4):
                u, a = g // 2, g % 2
                for hc in range(2):
                    b, rb = self.bank()
                    for l in range(32):
                        W1, rw = W1h[l // 16]
                        Sc.add("pe", lambda h, b=b, hc=hc, l=l, W1=W1, u=u, a=a, srcT=srcT: h.matmul(
                            self.PS[:, b, 0:127], W1[u * 64:(u + 1) * 64, l % 16, hc * 128:(hc + 1) * 128],
                            srcT[u * 64:(u + 1) * 64, a, l:l + 16 * 126 + 1:16],
                            start=(l == 0), stop=(l == 31)), reads=[rw, rS], writes=[rb])
                    Sc.add("act", lambda h, b=b, hc=hc, kv=kv: h.activation(
                        HID[:, hc, 0:127], self.PS[:, b, 0:127], AF.Gelu, bias=HB[:, kv * 2 + hc:kv * 2 + hc + 1]),
                        reads=[rb, rHB], writes=[rHID])
                b, rb = self.bank()
                if kv == 0:
                    for hc in range(2):
                        Sc.add("pe", lambda h, b=b, hc=hc: h.matmul(
                            self.PS[:, b, 0:127], W2S[:, 0, hc, :], HID[:, hc, 0:127],
                            start=(hc == 0), stop=(hc == 1)), reads=[rW2S, rHID], writes=[rb])
                    Sc.add("act", lambda h, b=b, u=u, a=a: h.copy(
                        KCT[u * 64:(u + 1) * 64, a, 0:127], self.PS[u * 64:(u + 1) * 64, b, 0:127]),
                        reads=[rb], writes=[rKCT])
                else:
                    for hc in range(2):
                        Sc.add("pe", lambda h, b=b, hc=hc: h.matmul(
                            self.PS[0:127, b, 0:64], HID[:, hc, 0:127], W2S[:, 1, hc, 0:64],
                            start=(hc == 0), stop=(hc == 1)), reads=[rW2S, rHID], writes=[rb])
                    Sc.add("act", lambda h, b=b, g=g: h.copy(VC1[0:127, g, 0:64], self.PS[0:127, b, 0:64]),
                           reads=[rb], writes=[rVC1])
        for r_al in (rQT, rNUM, rOTRs[0], rOTRs[1]):
            for dct in (rKC.w, rKC.r):
                for ch, sq in dct.items():
                    if r_al.w.get(ch, 0) < sq:
                        r_al.w[ch] = sq
        self.bank_lo = 5
        self.psi = 0
        accb = [(g, self.rPS[g]) for g in range(4)]
        ecnt = [0]

        def scores(KT, rK, g, ksl, nk, qi, masks):
            u, a = g // 2, g % 2
            b, rb = self.bank()
            Sc.add("pe", lambda h, b=b: h.matmul(
                self.PS[0:nk, b, :], KT[u * 64:(u + 1) * 64, a, ksl],
                QT[u * 64:(u + 1) * 64, qi, a * 4:(a + 1) * 4, :].rearrange("p a q -> p (a q)"),
                start=True, stop=True), reads=[rK, rQT], writes=[rb])
            e = ecnt[0] % 4
            ecnt[0] += 1
            Sc.add("act", lambda h, b=b, e=e: h.activation(E[0:nk, e, :], self.PS[0:nk, b, :], AF.Exp),
                   reads=[rb], writes=[rE[e]])
            for (mk, rm) in masks:
                Sc.add("dve", lambda h, e=e, mk=mk: h.tensor_tensor(
                    E[0:nk, e, :].rearrange("p (a q) -> p a q", a=4),
                    E[0:nk, e, :].rearrange("p (a q) -> p a q", a=4),
                    mk.unsqueeze(1).to_broadcast([nk, 4, 128]), ALU.mult),
                    reads=[rE[e]] + rm, writes=[rE[e]])
            return e

        def pv(e, nk, V1, rV, g, first, last):
            bb, rbb = accb[g]
            Sc.add("pe", lambda h, e=e, bb=bb: h.matmul(
                self.PS[0:65, bb, :], V1, E[0:nk, e, :], start=first, stop=last),
                reads=[rE[e], rV], writes=[rbb])

        def finish_a(g):
            bb, rbb = accb[g]
            Sc.add("act", lambda h, bb=bb, g=g: h.copy(OTR4[0:65, g % 2, :], self.PS[0:65, bb, :]),
                   reads=[rbb], writes=[rOTRs[g % 2]])

        def finish(x_, g, first_branch):
            OTR, rOTR = OTR4[:, g % 2, :], rOTRs[g % 2]
            bt, rbt = self.bank()
            for r_ in range(4):
                Sc.add("pe", lambda h, bt=bt, r_=r_: h.transpose(
                    self.PS[:, bt, r_ * 65:(r_ + 1) * 65], OTR[0:65, r_ * 128:(r_ + 1) * 128],
                    self.ident[0:65, 0:65]), reads=[rOTR, self.rC], writes=[rbt])
            Sc.add("dve", lambda h, bt=bt: h.tensor_copy(
                NUM.rearrange("p a b -> p (a b)"), self.PS[:, bt, 0:260]), reads=[rbt], writes=[rNUM])
            self.dbg("OTR", OTR[0:65, :], [rOTR])
            self.dbg("NUM", NUM, [rNUM])
            g4 = slice(g * 4, g * 4 + 4)
            Sc.add("dve", lambda h: h.tensor_scalar(F_[:, 0, g4], NUM[:, :, 64], 1e-30, None, ALU.max),
                   reads=[rNUM], writes=[rF])
            Sc.add("dve", lambda h: h.reciprocal(F_[:, 1, g4], F_[:, 0, g4]), reads=[rF], writes=[rF])
            Sc.add("dve", lambda h, g=g, x_=x_: h.tensor_tensor(
                F_[:, 2, g4], F_[:, 1, g4],
                GT.rearrange("p (h x) -> p x h", x=3)[:, x_, g * 4:(g + 1) * 4], ALU.mult),
                reads=[rF, rGT], writes=[rF])
            self.dbg("F", F_, [rF])
            self.dbg("GT", GT, [rGT])
            ov = OTM[:, g * 256:(g + 1) * 256].rearrange("p (h d) -> p h d", d=64)
            fb = F_[:, 2, g4].unsqueeze(2).to_broadcast([128, 4, 64])
            if first_branch:
                Sc.add("dve", lambda h: h.tensor_tensor(ov, NUM[:, :, 0:64], fb, ALU.mult),
                       reads=[rNUM, rF], writes=[rOTM])
            else:
                Sc.add("dve", lambda h: h.tensor_tensor(NUM[:, :, 0:64], NUM[:, :, 0:64], fb, ALU.mult),
                       reads=[rNUM, rF], writes=[rNUM])
                Sc.add("dve", lambda h: h.tensor_tensor(ov, ov, NUM[:, :, 0:64], ALU.add),
                       reads=[rNUM, rOTM], writes=[rOTM])

        for tb in range(4):
            rH = self.rHT[tb]
            for jh in range(2):
                Wq, rw = wload2(0, 8, jh * 4, 4)
                for jj in range(4):
                    b, rb = self.bank()
                    for k in range(8):
                        Sc.add("pe", lambda h, b=b, k=k, jj=jj, Wq=Wq, tb=tb: h.matmul(
                            self.PS[:, b, :], Wq[:, k, jj * 128:(jj + 1) * 128],
                            self.HT[:, k, tb * 512:(tb + 1) * 512], start=(k == 0), stop=(k == 7)),
                            reads=[rw, rH], writes=[rb])
                    Sc.add("act", lambda h, b=b, j=jh * 4 + jj: h.activation(
                        QT[:, :, j, :], self.PS[:, b, :].rearrange("p (a q) -> p a q", a=4),
                        AF.Copy, scale=0.125), reads=[rb], writes=[rQT])
            Wg_, rwg = self.wload(w_in[:, :, 2560:2608], 384)
            for i in range(tb * 4, tb * 4 + 4):
                t0 = i * 128
                qsl = slice((i % 4) * 128, (i % 4 + 1) * 128)
                b, rb = self.bank()
                for k in range(8):
                    Sc.add("pe", lambda h, b=b, k=k, t0=t0, Wg_=Wg_: h.matmul(
                        self.PS[:, b, 0:48], self.HT[:, k, t0:t0 + 128], Wg_[:, k, :],
                        start=(k == 0), stop=(k == 7)), reads=[rwg, rH], writes=[rb])
                Sc.add("act", lambda h, b=b: h.activation(GT, self.PS[:, b, 0:48], AF.Sigmoid),
                       reads=[rb], writes=[rGT])
                Sc.add("pool", lambda h, t0=t0: h.dma_start(out=MC[0:127, :], in_=w["maskc"][:, t0:t0 + 128]),
                       writes=[rMC], dma=rMC)
                Sc.add("sp", lambda h, t0=t0: h.dma_start(out=SA[:, 0, :], in_=w["selA"][t0:t0 + 128, :]),
                       writes=[rSA], dma=rSA)
                Sc.add("sp", lambda h, t0=t0: h.dma_start(out=SA[:, 1, :], in_=w["selB"][t0:t0 + 128, :]),
                       writes=[rSA], dma=rSA)
                def run_items(items, extra2=None, pre1=None):
                    LOOK = 2
                    slots = []
                    for n in range(len(items) + LOOK):
                        if n < len(items):
                            if pre1 is not None:
                                pre1(n)
                            KT_, rK_, g_, ksl_, nk_, masks_, V1_, rV_, fi_, la_ = items[n]
                            slots.append(scores(KT_, rK_, g_, ksl_, nk_, i % 4, masks_))
                        if n >= LOOK:
                            KT_, rK_, g_, ksl_, nk_, masks_, V1_, rV_, fi_, la_ = items[n - LOOK]
                            e_ = slots[n - LOOK]
                            pv(e_, nk_, V1_, rV_, g_, fi_, la_)
                            if extra2 is not None:
                                extra2(g_, e_)

                bis = {}

                def imp_mm(g, e):
                    bi, rbi = 4, self.rPS[4]
                    bis[g] = (bi, rbi)
                    for r_ in range(4):
                        Sc.add("pe", lambda h, e=e, r_=r_, bi=bi, g=g: h.matmul(
                            self.PS[:, bi, g * 128 + r_ * 32:g * 128 + (r_ + 1) * 32],
                            E[0:127, e, r_ * 128:(r_ + 1) * 128],
                            OV1[0:127, 0:32], start=True, stop=True), reads=[rE[e], rOV], writes=[rbi])
                run_items([(KCT, rKCT, g, slice(0, 127), 127, [(MC[0:127, :], [rMC])],
                            VC1[0:127, g, :], rVC1, True, True) for g in range(4)], imp_mm)
                finish_a(0)
                for g in range(4):
                    if g < 3:
                        finish_a(g + 1)
                    finish(0, g, True)
                bi, rbi = bis[0]
                Sc.add("dve", lambda h, bi=bi: h.tensor_tensor(
                    TMPI.rearrange("p (a j) -> p a j", j=32),
                    self.PS[:, bi, :].rearrange("p (a j) -> p a j", j=32),
                    F_[:, 1, 0:16].unsqueeze(2).to_broadcast([128, 16, 32]), ALU.mult),
                    reads=[rbi, rF], writes=[rTMPI])
                Tv = TMPI.rearrange("p (g r j) -> p g r j", g=4, r=4)
                Sc.add("dve", lambda h: h.tensor_tensor(IMP, Tv[:, :, 0, :], Tv[:, :, 1, :], ALU.add),
                       reads=[rTMPI], writes=[rIMP])
                for r_ in (2, 3):
                    Sc.add("dve", lambda h, r_=r_: h.tensor_tensor(IMP, IMP, Tv[:, :, r_, :], ALU.add),
                           reads=[rTMPI, rIMP], writes=[rIMP])
                Sc.add("dve", lambda h: h.tensor_tensor(
                    IMP, IMP, SA[:, 0, :].unsqueeze(1).to_broadcast([128, 4, 32]), ALU.mult),
                    reads=[rIMP, rSA], writes=[rIMP])
                Sc.add("dve", lambda h: h.tensor_tensor(
                    IMP, IMP, SA[:, 1, :].unsqueeze(1).to_broadcast([128, 4, 32]), ALU.add),
                    reads=[rIMP, rSA], writes=[rIMP])
                for g in range(4):
                    Sc.add("dve", lambda h, g=g: h.max(M8[:, g * 8:(g + 1) * 8], IMP[:, g, :]),
                           reads=[rIMP], writes=[rM8])
                for g in range(4):
                    Sc.add("dve", lambda h, g=g: h.match_replace(
                        SA2[:, g, :], M8[:, g * 8:(g + 1) * 8], IMP[:, g, :], -3e38),
                        reads=[rIMP, rM8], writes=[rSA2])
                for g in range(4):
                    Sc.add("dve", lambda h, g=g: h.max(M8[:, 32 + g * 8:32 + (g + 1) * 8], SA2[:, g, :]),
                           reads=[rSA2], writes=[rM8])
                Sc.add("dve", lambda h: h.tensor_reduce(
                    M8[:, 64:68], M8[:, 32:64].rearrange("p (g k) -> p g k", k=8), AX.X, ALU.min),
                    reads=[rM8], writes=[rM8])
                Sc.add("dve", lambda h: h.tensor_tensor(
                    IMP, IMP, M8[:, 64:68].unsqueeze(2).to_broadcast([128, 4, 32]), ALU.is_ge),
                    reads=[rIMP, rM8], writes=[rIMP])
                kts = [kt for kt in range(i - 4, i + 1) if kt >= 0]
                items = []
                for kt in kts:
                    for g in range(4):
                        masks = []
                        if kt == i:
                            masks.append((self.triub[:], [self.rC3]))
                        if kt == i - 4:
                            masks.append((self.slb[:], [self.rC3]))
                        items.append((KW, rKW, g, slice(kt * 128, (kt + 1) * 128), 128, masks,
                                      VW1[:, kt, g, :], rVW, kt == kts[0], kt == kts[-1]))
                run_items(items)
                bt, rbt = self.bank()
                for g in range(4):
                    Sc.add("pe", lambda h, bt=bt, g=g: h.transpose(
                        self.PS[0:32, bt, g * 128:(g + 1) * 128], IMP[:, g, :], self.ident[:]),
                        reads=[rIMP, self.rC], writes=[rbt])
                Sc.add("act", lambda h, bt=bt: h.copy(
                    SELT[0:32, :, :].rearrange("p g q -> p (g q)"), self.PS[0:32, bt, :]),
                    reads=[rbt], writes=[rSELT])
                finish_a(0)
                for g in range(4):
                    if g < 3:
                        finish_a(g + 1)
                    finish(2, g, False)
                items = []
                for kt in range(i + 1):
                    bm, rbm = 4, self.rPS[4]
                    for g in range(4):
                        masks = [(self.PS[:, bm, g * 128:(g + 1) * 128], [rbm])]
                        if kt == i:
                            masks.append((self.triub[:], [self.rC3]))
                        items.append((KS, rKS, g, slice(kt * 128, (kt + 1) * 128), 128, masks,
                                      VS1[:, kt, g, :], rVS, kt == 0, kt == i))

                def mask_mm(n):
                    if n % 4 == 0:
                        kt = n // 4
                        Sc.add("pe", lambda h, kt=kt: h.matmul(
                            self.PS[:, 4, :], EXb[0:32, kt * 128:(kt + 1) * 128],
                            SELT[0:32, :, :].rearrange("p g q -> p (g q)"),
                            start=True, stop=True), reads=[rEX, rSELT], writes=[self.rPS[4]])
                run_items(items, pre1=mask_mm)
                finish_a(0)
                for g in range(4):
                    if g < 3:
                        finish_a(g + 1)
                    finish(1, g, False)
                for q4 in range(2):
                    b, rb = self.bank()
                    psb = self.PS[:, b, :].bitcast(BF16)
                    for ii in range(4):
                        fch = q4 * 4 + ii
                        Sc.add("pe", lambda h, psb=psb, ii=ii, fch=fch: h.transpose(
                            psb[:, ii * 128:(ii + 1) * 128], OTM[:, fch * 128:(fch + 1) * 128], self.identb[:]),
                            reads=[rOTM, self.rC], writes=[rb])
                    Sc.add("act", lambda h, psb=psb, q4=q4, qsl=qsl: h.copy(
                        OT[:, q4 * 4:(q4 + 1) * 4, qsl], psb[:, 0:512].rearrange("p (a t) -> p a t", a=4)),
                        reads=[rb], writes=[rOT])
            for mg in range(2):
                Wo, rw = self.wload(w_out[:, :, mg * 512:(mg + 1) * 512], 4096)
                for mm in range(4):
                    b, rb = self.bank()
                    for fc in range(8):
                        Sc.add("pe", lambda h, b=b, fc=fc, mm=mm, Wo=Wo: h.matmul(
                            self.PS[:, b, :], Wo[:, fc, mm * 128:(mm + 1) * 128], OT[:, fc, :],
                            start=(fc == 0), stop=(fc == 7)), reads=[rw, rOT], writes=[rb])
                    self.xt_add(mg * 4 + mm, tb, b, rb)
        self.bank_lo = 0
        self.psi = 0


def consts():
    return {"c_ident": np.eye(128, dtype=np.float32),
            "c_tril": np.tril(np.ones((128, 128), dtype=np.float32)),
            "c_triu": np.triu(np.ones((128, 128), dtype=np.float32)),
            "c_sl": np.tril(np.ones((128, 128), dtype=np.float32), -1), **_nsa_consts()}


def _nsa_consts():
    t = np.arange(2048)
    n = np.arange(127)
    cmp_start = n * 16
    cmp_end = cmp_start + 31
    maskc = (cmp_end[:, None] <= t[None, :]).astype(np.float32)
    sel_start = np.arange(32) * 64
    ov = ((cmp_start[:, None] <= sel_start[None, :] + 63) & (cmp_end[:, None] >= sel_start[None, :]))
    ov1 = np.concatenate([ov.astype(np.float32), np.ones((127, 1), np.float32)], axis=1)
    j = np.arange(32)[None, :]
    cur = (t // 64)[:, None]
    forced = (j == 0) | (j == cur) | (j == cur - 1)
    future = sel_start[None, :] > t[:, None]
    selA = (~forced & ~future).astype(np.float32)
    selB = np.where(forced, 1e6, np.where(future, -1e6, 0.0)).astype(np.float32)
    ex = (np.arange(2048)[None, :] // 64 == np.arange(32)[:, None]).astype(np.float32)
    return {"c_maskc": maskc, "c_ov1": ov1, "c_selA": selA, "c_selB": selB, "c_ex": ex}


_INPUT_NAMES = (
    "x", "norm_gains", "final_norm", "ff_w1", "ff_w2",
    "l0_sg_w_in", "l0_sg_vnorm_g", "l0_sg_vnorm_b", "l0_sg_w_s", "l0_sg_b_s", "l0_sg_w_out",
    "l1_ssm_w_in", "l1_ssm_conv_w", "l1_ssm_conv_b", "l1_ssm_dt_bias", "l1_ssm_a_log", "l1_ssm_d_skip",
    "l1_ssm_norm_g", "l1_ssm_w_out",
    "l2_nsa_w_in", "l2_nsa_cmp_pos_k", "l2_nsa_cmp_pos_v", "l2_nsa_cmp_k_w1", "l2_nsa_cmp_k_w2",
    "l2_nsa_cmp_v_w1", "l2_nsa_cmp_v_w2", "l2_nsa_w_out",
    "l3_sg_w_in", "l3_sg_vnorm_g", "l3_sg_vnorm_b", "l3_sg_w_s", "l3_sg_b_s", "l3_sg_w_out",
)
ALL_STAGES = [("sg", 0), ("ffn", 0), ("ssm", 1), ("ffn", 1), ("nsa", 2), ("ffn", 2), ("sg", 3), ("ffn", 3)]


def run(inputs, n_cores, nseq, stages, final=True, trace=False, debug=False):
    p = Prog(nseq, stages, final)
    p.debug_on = debug
    nc = p.build()
    names = list(p.dram.keys())
    cs = consts()
    in_maps = []
    for c in range(n_cores):
        m = {}
        for n in names:
            if n == "x":
                m[n] = np.ascontiguousarray(inputs["x"][c * nseq:(c + 1) * nseq])
            elif n in cs:
                m[n] = cs[n]
            else:
                m[n] = np.ascontiguousarray(inputs[n])
        in_maps.append(m)
    res = run_bass_kernel_spmd(nc, in_maps, core_ids=list(range(n_cores)), trace=trace)
    outs = np.concatenate([r["out"] for r in res.results], axis=0)
    res.dbg = {k: res.results[0]["dbg_" + k] for k in p.dbgs}
    return outs, res


def kernel(**inputs):
    outs, _ = run(inputs, 8, 4, ALL_STAGES, True)
    return outs.astype(np.float32, copy=False)
```

```python
import contextlib
import numpy as np
import concourse.bass as bass
import concourse.mybir as mybir
from concourse.bass_utils import run_bass_kernel_spmd

F32 = mybir.dt.float32
BF16 = mybir.dt.bfloat16
AF = mybir.ActivationFunctionType
ALU = mybir.AluOpType
AX = mybir.AxisListType

D = 1024
S = 2048
DFF = 4096
EPS = 1e-5
COMPUTE = ("pe", "act", "dve", "pool")


class Res:
    __slots__ = ("name", "w", "r", "dma_n")

    def __init__(self, name):
        self.name = name
        self.w = {}
        self.r = {}
        self.dma_n = 0


class Op:
    __slots__ = ("eng", "fn", "deps", "chan", "seq", "signal", "sigval", "isdma")


class Sched:
    def __init__(self, nc, es):
        self.nc = nc
        self.es = es
        self.ops = {e: [] for e in ("pe", "act", "dve", "pool", "sp")}
        self.cops = {e: [] for e in COMPUTE}
        self.sems = {e: es.enter_context(nc.semaphore("s_" + e)) for e in COMPUTE}
        self.dma_sems = {}
        self.dma_cnt = {}
        self.nres = 0

    def res(self, name=""):
        self.nres += 1
        return Res(name or ("r%d" % self.nres))

    def add(self, eng, fn, reads=(), writes=(), dma=None):
        deps = {}
        raw_same = 0
        for r in reads:
            for ch, s in r.w.items():
                if deps.get(ch, 0) < s:
                    deps[ch] = s
                if ch == eng and s > raw_same:
                    raw_same = s
        for r in writes:
            for ch, s in r.w.items():
                if deps.get(ch, 0) < s:
                    deps[ch] = s
            for ch, s in r.r.items():
                if deps.get(ch, 0) < s:
                    deps[ch] = s
        op = Op()
        op.eng = eng
        op.fn = fn
        op.signal = False
        op.sigval = 0
        op.isdma = dma is not None
        if fn is None:
            chan, seq = None, 0
        elif dma is not None:
            chan = ("dma", dma.name)
            if chan not in self.dma_sems:
                self.dma_sems[chan] = self.es.enter_context(
                    self.nc.semaphore("d%d" % len(self.dma_sems)))
                self.dma_cnt[chan] = 0
            self.dma_cnt[chan] += 1
            seq = self.dma_cnt[chan]
        else:
            chan = eng
            self.cops[eng].append(op)
            seq = len(self.cops[eng])
        if eng in COMPUTE and eng in deps and not op.isdma:
            if eng == "pe" or raw_same == 0:
                del deps[eng]
            else:
                deps[eng] = raw_same
        op.deps = deps
        op.chan = chan
        op.seq = seq
        if fn is not None:
            for r in reads:
                if r.r.get(chan, 0) < seq:
                    r.r[chan] = seq
            for r in writes:
                r.w = {chan: seq}
                r.r = {}
        self.ops[eng].append(op)
        return op

    def emit(self):
        for lst in self.ops.values():
            for op in lst:
                for ch, s in op.deps.items():
                    if ch in self.cops:
                        self.cops[ch][s - 1].signal = True
        for ch in COMPUTE:
            n = 0
            for op in self.cops[ch]:
                if op.signal:
                    n += 1
                    op.sigval = n

        def run(eng, h):
            seen = {}
            for op in self.ops[eng]:
                for ch, s in op.deps.items():
                    if ch in self.cops:
                        sem = self.sems[ch]
                        val = self.cops[ch][s - 1].sigval
                    else:
                        sem = self.dma_sems[ch]
                        val = 16 * s
                    if seen.get(ch, 0) < val:
                        h.wait_ge(sem, val)
                        seen[ch] = val
                if op.fn is None:
                    continue
                inst = op.fn(h)
                if op.isdma:
                    inst.then_inc(self.dma_sems[op.chan], 16)
                elif op.signal:
                    inst.then_inc(self.sems[op.chan], 1)

        with self.nc.Block() as block:
            @block.sync
            def _(h):
                run("sp", h)

            @block.gpsimd
            def _(h):
                run("pool", h)

            @block.scalar
            def _(h):
                run("act", h)

            @block.vector
            def _(h):
                run("dve", h)

            @block.tensor
            def _(h):
                run("pe", h)


class Ring:
    def __init__(self, S_, tensor, n):
        self.t = tensor
        self.n = n
        self.res = [S_.res("ring%d" % i) for i in range(n)]
        self.i = 0

    def next(self):
        i = self.i % self.n
        self.i += 1
        return i, self.res[i]


class Prog:
    def __init__(self, nseq, stages, final=True):
        self.nseq = nseq
        self.stages = stages
        self.final = final
        self.nc = bass.Bass("TRN2", target_bir_lowering=False)
        self.es = contextlib.ExitStack()
        self.dram = {}
        self.dbgs = {}
        self.dbg_res = []

    def din(self, name, shape):
        t = self.nc.dram_tensor(name, list(shape), F32, kind="ExternalInput").ap()
        self.dram[name] = t
        return t

    def sb(self, name, shape, dt=F32):
        return self.es.enter_context(self.nc.sbuf_tensor(name, list(shape), dt))

    def scr(self, off, shape, dt):
        n = 1
        for d in shape:
            n *= d
        nb = n * (4 if dt == F32 else 2)
        assert off % 4 == 0 and off + nb <= self.SCR_BYTES, (off, nb)
        v = self.SCR[:, off // 2:(off + nb) // 2]
        if dt == F32:
            v = v.bitcast(F32)
        if len(shape) == 2:
            v = v.rearrange("p (a b) -> p a b", a=shape[0])
        elif len(shape) == 3:
            v = v.rearrange("p (a b c) -> p a b c", a=shape[0], b=shape[1])
        return v

    def dbg(self, name, ap, reads, dt=F32):
        if not getattr(self, "debug_on", False) or name in self.dbgs:
            return
        t = self.nc.dram_tensor("dbg_" + name, list(ap.shape), dt, kind="ExternalOutput").ap()
        self.dbgs[name] = t
        r = self.Sc.res("dbg_" + name)
        self.dbg_res.append(r)
        self.Sc.add("sp", lambda h: h.dma_start(out=t, in_=ap), reads=list(reads), writes=[r], dma=r)

    def begin_phase(self):
        merged = {}
        for o in self.phase_res:
            for dct in (o.w, o.r):
                for ch, sq in dct.items():
                    if merged.get(ch, 0) < sq:
                        merged[ch] = sq
        self.phase_merged = merged
        self.phase_res = []

    def pres(self, name=""):
        r = self.Sc.res(name)
        r.w = dict(self.phase_merged)
        self.phase_res.append(r)
        return r

    def build(self):
        nc, es = self.nc, self.es
        nseq = self.nseq
        x = self.din("x", (nseq, S, D))
        out = nc.dram_tensor("out", [nseq, S, D], F32, kind="ExternalOutput").ap()
        norm_gains = self.din("norm_gains", (4, 2, D))
        final_norm = self.din("final_norm", (D,))
        ff_w1 = self.din("ff_w1", (4, D, DFF))
        ff_w2 = self.din("ff_w2", (4, DFF, D))
        ident_d = self.din("c_ident", (128, 128))
        tril_d = self.din("c_tril", (128, 128))
        kinds = set(k for k, _ in self.stages)
        lw = {}
        for kind, li in self.stages:
            if kind == "sg":
                p = "l%d_sg_" % li
                lw[li] = dict(
                    w_in=self.din(p + "w_in", (D, 4096)), g=self.din(p + "vnorm_g", (2048,)),
                    b=self.din(p + "vnorm_b", (2048,)), w_s=self.din(p + "w_s", (8, 128, 128)),
                    b_s=self.din(p + "b_s", (8, 128)), w_out=self.din(p + "w_out", (2048, D)))

            elif kind == "ssm":
                p = "l%d_ssm_" % li
                lw[li] = dict(
                    w_in=self.din(p + "w_in", (D, 6176)), conv_w=self.din(p + "conv_w", (4, 4096)),
                    conv_b=self.din(p + "conv_b", (4096,)), dt_bias=self.din(p + "dt_bias", (32,)),
                    a_log=self.din(p + "a_log", (32,)), d_skip=self.din(p + "d_skip", (32,)),
                    norm_g=self.din(p + "norm_g", (2048,)), w_out=self.din(p + "w_out", (2048, D)))
            elif kind == "nsa":
                p = "l%d_nsa_" % li
                lw[li] = dict(
                    w_in=self.din(p + "w_in", (D, 2608)), pos_k=self.din(p + "cmp_pos_k", (32, 64)),
                    pos_v=self.din(p + "cmp_pos_v", (32, 64)), k_w1=self.din(p + "cmp_k_w1", (2048, 256)),
                    k_w2=self.din(p + "cmp_k_w2", (256, 64)), v_w1=self.din(p + "cmp_v_w1", (2048, 256)),
                    v_w2=self.din(p + "cmp_v_w2", (256, 64)), w_out=self.din(p + "w_out", (D, D)),
                    maskc=self.din("c_maskc", (127, 2048)), ov1=self.din("c_ov1", (127, 33)),
                    selA=self.din("c_selA", (2048, 32)), selB=self.din("c_selB", (2048, 32)),
                    ex=self.din("c_ex", (32, 2048)))
        triu_d = self.din("c_triu", (128, 128))
        sl_d = self.din("c_sl", (128, 128))

        self.Sc = Sc = Sched(nc, es)
        self.phase_res = []
        self.phase_merged = {}
        self.XT = self.sb("XT", (128, 8, S), F32)
        self.HT = self.sb("HT", (128, 8, S), BF16)
        self.rXT = [Sc.res("XT%d" % i) for i in range(4)]
        self.rHT = [Sc.res("HT%d" % i) for i in range(4)]
        self.PS = es.enter_context(nc.psum_tensor("PS", [128, 8, 512], F32))
        self.rPS = [Sc.res("ps%d" % i) for i in range(8)]
        self.psi = 0
        self.SCR_BYTES = 78848
        self.SCR = self.sb("SCR", (128, self.SCR_BYTES // 2), BF16)
        self.ident = self.sb("ident", (128, 128), F32)
        self.tril = self.sb("tril", (128, 128), F32)
        self.identb = self.sb("identb", (128, 128), BF16)
        self.onesb = self.sb("onesb", (128, 128), BF16)
        self.gains = self.sb("gains", (128, 8, 8), F32)
        rC = self.rC = Sc.res("consts")
        Sc.add("sp", lambda h: h.dma_start(out=self.ident[:], in_=ident_d[:, :]), writes=[rC], dma=rC)
        rC2 = self.rC2 = Sc.res("consts2")
        Sc.add("sp", lambda h: h.dma_start(out=self.tril[:], in_=tril_d[:, :]), writes=[rC2], dma=rC2)
        self.triu = self.sb("triu", (128, 128), F32)
        self.triub = self.sb("triub", (128, 128), BF16)
        self.slb = self.sb("slb", (128, 128), BF16)
        self.onesf = self.sb("onesf", (128, 128), F32)
        rC3 = self.rC3 = Sc.res("consts3")
        Sc.add("sp", lambda h: h.dma_start(out=self.triu[:], in_=triu_d[:, :]), writes=[rC3], dma=rC3)
        rC4 = Sc.res("consts4")
        Sc.add("sp", lambda h: h.dma_start(out=self.onesf[:], in_=sl_d[:, :]), writes=[rC4], dma=rC4)
        Sc.add("dve", lambda h: h.tensor_copy(self.slb[:], self.onesf[:]), reads=[rC4], writes=[rC3])
        Sc.add("dve", lambda h: h.tensor_copy(self.triub[:], self.triu[:]), reads=[rC3], writes=[rC3])
        Sc.add("dve", lambda h: h.memset(self.onesf[:], 1.0), reads=[rC3], writes=[rC4])
        self.rC4 = rC4
        rG = self.rG = Sc.res("gains")
        self.graw = self.sb("graw", (64, 128), F32)
        Sc.add("sp", lambda h: h.dma_start(
            out=self.graw[:], in_=norm_gains.rearrange("l j (c p) -> (l j c) p", p=128)),
            writes=[rG], dma=rG)
        bg, rbg = self.bank()
        Sc.add("pe", lambda h: h.transpose(self.PS[:, bg, 0:64], self.graw[:], self.ident[0:64, 0:64]),
               reads=[rG, rC], writes=[rbg])
        Sc.add("dve", lambda h: h.tensor_copy(
            self.gains[:, :, :].rearrange("p a b -> p (a b)"), self.PS[:, bg, 0:64]),
            reads=[rbg], writes=[rG])
        self.final_norm_d = final_norm
        Sc.add("dve", lambda h: h.tensor_copy(self.identb[:], self.ident[:]), reads=[rC], writes=[rC])
        Sc.add("dve", lambda h: h.memset(self.onesb[:], 1.0 / 1024.0), writes=[rC])

        self.WR = self.sb("WR", (128, 3, 4096), BF16)
        self.wring = Ring(Sc, self.WR, 3)
        self.SQ = self.sb("SQ", (128, 2, 512), BF16)
        self.rSQ = [Sc.res("SQ0"), Sc.res("SQ1")]
        self.RS = self.sb("RS", (128, 2, 512), F32)
        self.rRS = [Sc.res("RS0"), Sc.res("RS1")]
        self.ST = self.sb("ST", (128, 64), F32)
        self.cnt = 0

        for sq in range(nseq):
            self.load_x(x, sq)
            for st in self.stages:
                kind, li = st
                if kind == "ffn":
                    self.rmsnorm(li * 2 + 1)
                    self.ffn(ff_w1[li], ff_w2[li])
                elif kind == "sg":
                    self.rmsnorm(li * 2)
                    self.sgmlp(lw[li])
                elif kind == "ssm":
                    self.rmsnorm(li * 2)
                    self.ssm(lw[li])
                elif kind == "nsa":
                    self.rmsnorm(li * 2)
                    self.nsa(lw[li])
                else:
                    raise ValueError(kind)
            self.store_x(out, sq)
        Sc.add("sp", None, writes=self.rXS + self.dbg_res)
        Sc.emit()
        return nc

    def bank(self):
        lo = getattr(self, "bank_lo", 0)
        b = lo + self.psi % (8 - lo)
        self.psi += 1
        return b, self.rPS[b]

    def bank2(self):
        if self.psi % 2:
            self.psi += 1
        b = self.psi % 8
        self.psi += 2
        return b, self.rPS[b], self.rPS[b + 1]

    def load_x(self, x, sq):
        Sc = self.Sc
        self.begin_phase()
        self.XS = self.scr(0, (2, D), F32)
        self.rXS = [self.pres("XS0"), self.pres("XS1")]
        for t in range(16):
            sl = t % 2
            rs = self.rXS[sl]
            Sc.add("sp", lambda h, sl=sl, t=t: h.dma_start(
                out=self.XS[:, sl, :], in_=x[sq, t * 128:(t + 1) * 128, :]), writes=[rs], dma=rs)
            b0, r0, r1 = self.bank2()
            b1 = b0 + 1
            for k in range(8):
                bb, rb = (b0, r0) if k < 4 else (b1, r1)
                Sc.add("pe", lambda h, sl=sl, k=k, bb=bb: h.transpose(
                    self.PS[:, bb, (k % 4) * 128:(k % 4 + 1) * 128],
                    self.XS[:, sl, k * 128:(k + 1) * 128], self.ident[:]),
                    reads=[rs, self.rC], writes=[rb])
            tb = t // 4
            Sc.add("act", lambda h, b0=b0, t=t: h.copy(
                self.XT[:, :, t * 128:(t + 1) * 128],
                self.PS[:, b0:b0 + 2, :].rearrange("p b (k t) -> p (b k) t", t=128)),
                reads=[r0, r1], writes=[self.rXT[tb]])

    def store_x(self, out, sq):
        Sc = self.Sc
        self.begin_phase()
        self.XS = self.scr(0, (2, D), F32)
        self.rXS = [self.pres("XS0"), self.pres("XS1")]
        JK = self.scr(8192, (D,), F32)
        rJK = self.pres("JK")
        rST = self.pres("ST")
        GF = self.scr(12288, (D,), F32)
        rGF = self.pres("gfin")
        Sc.add("sp", lambda h: h.dma_start(out=GF, in_=self.final_norm_d.partition_broadcast(128)),
               writes=[rGF], dma=rGF)
        for t in range(16):
            tb = t // 4
            b0, r0, r1 = self.bank2()
            b1 = b0 + 1
            for k in range(8):
                bb, rb = (b0, r0) if k < 4 else (b1, r1)
                Sc.add("pe", lambda h, k=k, bb=bb, t=t: h.transpose(
                    self.PS[:, bb, (k % 4) * 128:(k % 4 + 1) * 128],
                    self.XT[:, k, t * 128:(t + 1) * 128], self.ident[:]),
                    reads=[self.rXT[tb], self.rC], writes=[rb])
            sl = t % 2
            rs = self.rXS[sl]
            psv = self.PS[:, b0:b0 + 2, :].rearrange("p b f -> p (b f)")
            if self.final:
                Sc.add("act", lambda h, psv=psv: h.activation(
                    JK, psv, AF.Square, accum_out=self.ST[:, 0:1]),
                    reads=[r0, r1], writes=[rJK, rST])
                Sc.add("act", lambda h: h.activation(
                    self.ST[:, 1:2], self.ST[:, 0:1], AF.Sqrt, bias=EPS, scale=1.0 / D),
                    reads=[rST], writes=[rST])
                Sc.add("dve", lambda h: h.reciprocal(self.ST[:, 2:3], self.ST[:, 1:2]),
                       reads=[rST], writes=[rST])
                Sc.add("dve", lambda h, psv=psv, sl=sl: h.scalar_tensor_tensor(
                    self.XS[:, sl, :], psv, self.ST[:, 2:3], GF, ALU.mult, ALU.mult),
                    reads=[r0, r1, rST, rGF], writes=[rs])
            else:
                Sc.add("dve", lambda h, psv=psv, sl=sl: h.tensor_copy(self.XS[:, sl, :], psv),
                       reads=[r0, r1], writes=[rs])
            Sc.add("sp", lambda h, sl=sl, t=t: h.dma_start(
                out=out[sq, t * 128:(t + 1) * 128, :], in_=self.XS[:, sl, :]), reads=[rs], dma=rs)

    def rmsnorm(self, gi):
        Sc = self.Sc
        for tb in range(4):
            tsl = slice(tb * 512, (tb + 1) * 512)
            b, rb = self.bank()
            for k in range(8):
                sl = self.cnt % 2
                self.cnt += 1
                Sc.add("act", lambda h, sl=sl, k=k, tsl=tsl: h.activation(
                    self.SQ[:, sl, :], self.XT[:, k, tsl], AF.Square),
                    reads=[self.rXT[tb]], writes=[self.rSQ[sl]])
                Sc.add("pe", lambda h, sl=sl, k=k, b=b: h.matmul(
                    self.PS[:, b, :], self.onesb[:], self.SQ[:, sl, :], start=(k == 0), stop=(k == 7)),
                    reads=[self.rSQ[sl], self.rC], writes=[rb])
            rsl = tb % 2
            Sc.add("act", lambda h, rsl=rsl, b=b: h.activation(
                self.RS[:, rsl, :], self.PS[:, b, :], AF.Sqrt, bias=EPS),
                reads=[rb], writes=[self.rRS[rsl]])
            Sc.add("dve", lambda h, rsl=rsl: h.reciprocal(self.RS[:, rsl, :], self.RS[:, rsl, :]),
                   reads=[self.rRS[rsl]], writes=[self.rRS[rsl]])
            for k in range(8):
                Sc.add("dve", lambda h, rsl=rsl, k=k, tsl=tsl: h.scalar_tensor_tensor(
                    self.HT[:, k, tsl], self.XT[:, k, tsl], self.gains[:, gi, k:k + 1], self.RS[:, rsl, :],
                    ALU.mult, ALU.mult),
                    reads=[self.rXT[tb], self.rG, self.rRS[rsl]], writes=[self.rHT[tb]])

    def wload(self, src_ap, nelem):
        Sc = self.Sc
        i, r = self.wring.next()
        shp = src_ap.shape
        dst = self.WR[:, i, 0:nelem]
        if len(shp) == 3:
            dst = dst.rearrange("p (a b) -> p a b", a=shp[1])
        Sc.add("pool", lambda h: h.dma_start(out=dst, in_=src_ap), writes=[r], dma=r)
        return dst, r

    def xt_add(self, m, tb, b, rb):
        tsl = slice(tb * 512, (tb + 1) * 512)
        self.Sc.add("dve", lambda h: h.tensor_tensor(
            self.XT[:, m, tsl], self.XT[:, m, tsl], self.PS[:, b, :], ALU.add),
            reads=[rb, self.rXT[tb]], writes=[self.rXT[tb]])

    def ffn(self, w1, w2):
        Sc = self.Sc
        self.begin_phase()
        AT = self.scr(0, (2, 4, 512), BF16)
        rAT = [self.pres("AT0"), self.pres("AT1")]
        RL = self.scr(8192, (2, 512), F32)
        rRL = [self.pres("RL0"), self.pres("RL1")]
        w1v = w1.rearrange("(k p) n -> p k n", p=128)
        w2v = w2.rearrange("(c p) n -> p c n", p=128)
        pend = None
        it = 0
        for q in range(8):
            W1, r1 = self.wload(w1v[:, :, q * 512:(q + 1) * 512], 4096)
            W2, r2 = self.wload(w2v[:, q * 4:(q + 1) * 4, :], 4096)
            for tb in range(4):
                tsl = slice(tb * 512, (tb + 1) * 512)
                asl = it % 2
                it += 1
                for c in range(4):
                    b, rb = self.bank()
                    for k in range(8):
                        Sc.add("pe", lambda h, b=b, k=k, c=c, W1=W1, tsl=tsl: h.matmul(
                            self.PS[:, b, :], W1[:, k, c * 128:(c + 1) * 128], self.HT[:, k, tsl],
                            start=(k == 0), stop=(k == 7)),
                            reads=[r1, self.rHT[tb]], writes=[rb])
                    rl = self.cnt % 2
                    self.cnt += 1
                    Sc.add("act", lambda h, b=b, rl=rl: h.activation(
                        RL[:, rl, :], self.PS[:, b, :], AF.Relu),
                        reads=[rb], writes=[rRL[rl]])
                    Sc.add("dve", lambda h, c=c, asl=asl, rl=rl: h.tensor_tensor(
                        AT[:, asl, c, :], RL[:, rl, :], RL[:, rl, :], ALU.mult),
                        reads=[rRL[rl]], writes=[rAT[asl]])
                if pend is not None:
                    self.ffn2(AT, rAT, *pend)
                pend = (W2, r2, asl, tb)
        self.ffn2(AT, rAT, *pend)

    def ffn2(self, AT, rAT, W2, r2, asl, tb):
        Sc = self.Sc
        for m in range(8):
            b, rb = self.bank()
            for c in range(4):
                Sc.add("pe", lambda h, b=b, c=c, m=m: h.matmul(
                    self.PS[:, b, :], W2[:, c, m * 128:(m + 1) * 128], AT[:, asl, c, :],
                    start=(c == 0), stop=(c == 3)),
                    reads=[r2, rAT[asl]], writes=[rb])
            self.xt_add(m, tb, b, rb)

    def sgmlp(self, w):
        Sc = self.Sc
        self.begin_phase()
        UT2 = self.scr(0, (2, 16, 512), BF16)
        VB = self.scr(32768, (4, 2048), BF16)
        GB = self.scr(49152, (2, 2048), BF16)
        BS = self.scr(57344, (8, 128), F32)
        WST = self.scr(61440, (8, 128), BF16)
        T1 = self.scr(63488, (2, 512), F32)
        JK = self.scr(67584, (512,), BF16)
        WSR = self.scr(68608, (8, 128), F32)
        rUT2 = [[self.pres("UT%d_%d" % (j, i)) for i in range(16)] for j in range(2)]
        rVB = [self.pres("VB%d" % i) for i in range(4)]
        rGB, rBS, rWST, rWSR = self.pres("GB"), self.pres("BS"), self.pres("WST"), self.pres("WSR")
        rT1 = [self.pres("T1a"), self.pres("T1b")]
        rST = self.pres("STsg")
        rJK = self.pres("JKsg")
        Sc.add("pool", lambda h: h.dma_start(out=GB[:, 0, :], in_=w["g"].partition_broadcast(128)),
               writes=[rGB], dma=rGB)
        rGB2 = self.pres("GB2")
        Sc.add("pool", lambda h: h.dma_start(out=GB[:, 1, :], in_=w["b"].partition_broadcast(128)),
               writes=[rGB2], dma=rGB2)
        Sc.add("sp", lambda h: h.dma_start(
            out=BS.rearrange("p g t -> p (g t)"),
            in_=w["b_s"].rearrange("g t -> (g t)").partition_broadcast(128)), writes=[rBS], dma=rBS)
        Sc.add("sp", lambda h: h.dma_start(out=WSR, in_=w["w_s"].rearrange("g t s -> t g s")),
               writes=[rWSR], dma=rWSR)
        Sc.add("dve", lambda h: h.tensor_tensor(
            WSR, WSR, self.tril[:].unsqueeze(1).to_broadcast([128, 8, 128]), ALU.mult),
            reads=[rWSR, self.rC2], writes=[rWSR])
        for half in range(2):
            b0, rb = self.bank()
            for gg in range(4):
                g = half * 4 + gg
                Sc.add("pe", lambda h, g=g, gg=gg, b0=b0: h.transpose(
                    self.PS[:, b0, gg * 128:(gg + 1) * 128], WSR[:, g, :], self.ident[:]),
                    reads=[rWSR, self.rC], writes=[rb])
            Sc.add("act", lambda h, half=half, b0=b0: h.copy(
                WST[:, half * 4:(half + 1) * 4, :].rearrange("p g t -> p (g t)"), self.PS[:, b0, :]),
                reads=[rb], writes=[rWST])
        w_in = w["w_in"].rearrange("(k p) n -> p k n", p=128)
        w_out = w["w_out"].rearrange("(c p) n -> p c n", p=128)
        def u_phase(tb):
            tsl = slice(tb * 512, (tb + 1) * 512)
            UT, rUT = UT2[:, tb % 2], rUT2[tb % 2]
            for gu in range(4):
                Wg, rw = self.wload(w_in[:, :, gu * 512:(gu + 1) * 512], 4096)
                for fc in range(4):
                    b, rb = self.bank()
                    for k in range(8):
                        Sc.add("pe", lambda h, b=b, k=k, fc=fc, Wg=Wg, tsl=tsl: h.matmul(
                            self.PS[:, b, :], Wg[:, k, fc * 128:(fc + 1) * 128], self.HT[:, k, tsl],
                            start=(k == 0), stop=(k == 7)),
                            reads=[rw, self.rHT[tb]], writes=[rb])
                    fi = gu * 4 + fc
                    Sc.add("act", lambda h, b=b, fi=fi, UT=UT: h.activation(
                        UT[:, fi, :], self.PS[:, b, :], AF.Gelu),
                        reads=[rb], writes=[rUT[fi]])

        u_phase(0)
        for tb in range(4):
            tsl = slice(tb * 512, (tb + 1) * 512)
            UT, rUT = UT2[:, tb % 2], rUT2[tb % 2]
            for gv in range(4):
                Wg, rw = self.wload(w_in[:, :, 2048 + gv * 512:2048 + (gv + 1) * 512], 4096)
                for tc in range(4):
                    b, rb = self.bank()
                    t0 = tb * 512 + tc * 128
                    for k in range(8):
                        Sc.add("pe", lambda h, b=b, k=k, Wg=Wg, t0=t0: h.matmul(
                            self.PS[:, b, :], self.HT[:, k, t0:t0 + 128], Wg[:, k, :],
                            start=(k == 0), stop=(k == 7)),
                            reads=[rw, self.rHT[tb]], writes=[rb])
                    Sc.add("act", lambda h, b=b, tc=tc, gv=gv: h.activation(
                        VB[:, tc, gv * 512:(gv + 1) * 512], self.PS[:, b, :], AF.Gelu,
                        accum_out=self.ST[:, tc * 4 + gv:tc * 4 + gv + 1]),
                        reads=[rb], writes=[rVB[tc], rST])
                    Sc.add("act", lambda h, tc=tc, gv=gv: h.activation(
                        JK, VB[:, tc, gv * 512:(gv + 1) * 512], AF.Square,
                        accum_out=self.ST[:, 16 + tc * 4 + gv:16 + tc * 4 + gv + 1]),
                        reads=[rVB[tc]], writes=[rJK, rST])
            self.dbg("VBraw", VB, rVB, BF16)
            self.dbg("STraw", self.ST[:, :], [rST])
            Sc.add("dve", lambda h: h.reduce_sum(
                self.ST[:, 32:40], self.ST[:, 0:32].rearrange("p (a b) -> p a b", b=4), AX.X),
                reads=[rST], writes=[rST])
            Sc.add("dve", lambda h: h.tensor_scalar(
                self.ST[:, 32:40], self.ST[:, 32:40], 1.0 / 2048.0, None, ALU.mult),
                reads=[rST], writes=[rST])
            Sc.add("dve", lambda h: h.tensor_tensor(
                self.ST[:, 44:48], self.ST[:, 32:36], self.ST[:, 32:36], ALU.mult),
                reads=[rST], writes=[rST])
            Sc.add("dve", lambda h: h.tensor_tensor(
                self.ST[:, 48:52], self.ST[:, 36:40], self.ST[:, 44:48], ALU.subtract),
                reads=[rST], writes=[rST])
            Sc.add("act", lambda h: h.activation(
                self.ST[:, 52:56], self.ST[:, 48:52], AF.Sqrt, bias=EPS),
                reads=[rST], writes=[rST])
            Sc.add("dve", lambda h: h.reciprocal(self.ST[:, 40:44], self.ST[:, 52:56]),
                   reads=[rST], writes=[rST])
            if tb < 3:
                u_phase(tb + 1)
            for tc in range(4):
                Sc.add("dve", lambda h, tc=tc: h.tensor_scalar(
                    VB[:, tc, :], VB[:, tc, :], self.ST[:, 32 + tc:33 + tc], self.ST[:, 40 + tc:41 + tc],
                    ALU.subtract, ALU.mult), reads=[rVB[tc], rST], writes=[rVB[tc]])
                Sc.add("dve", lambda h, tc=tc: h.tensor_tensor(
                    VB[:, tc, :], VB[:, tc, :], GB[:, 0, :], ALU.mult),
                    reads=[rVB[tc], rGB], writes=[rVB[tc]])
                Sc.add("dve", lambda h, tc=tc: h.tensor_tensor(
                    VB[:, tc, :], VB[:, tc, :], GB[:, 1, :], ALU.add),
                    reads=[rVB[tc], rGB2], writes=[rVB[tc]])
            self.dbg("VBn", VB, rVB, BF16)
            self.dbg("STn", self.ST[:, :], [rST])
            self.dbg("WST", WST, [rWST], BF16)
            for fi in range(16):
                g = fi // 2
                b, rb = self.bank()
                for tc in range(4):
                    Sc.add("pe", lambda h, b=b, tc=tc, fi=fi, g=g: h.matmul(
                        self.PS[:, b, tc * 128:(tc + 1) * 128], VB[:, tc, fi * 128:(fi + 1) * 128],
                        WST[:, g, :], start=True, stop=True),
                        reads=[rVB[tc], rWST], writes=[rb])
                tl = fi % 2
                Sc.add("dve", lambda h, b=b, g=g, tl=tl: h.tensor_tensor(
                    T1[:, tl, :].rearrange("p (a t) -> p a t", a=4),
                    self.PS[:, b, :].rearrange("p (a t) -> p a t", a=4),
                    BS[:, g, :].unsqueeze(1).to_broadcast([128, 4, 128]), ALU.add),
                    reads=[rb, rBS], writes=[rT1[tl]])
                Sc.add("dve", lambda h, fi=fi, tl=tl, UT=UT: h.tensor_tensor(
                    UT[:, fi, :], UT[:, fi, :], T1[:, tl, :], ALU.mult),
                    reads=[rT1[tl], rUT[fi]], writes=[rUT[fi]])
            self.dbg("G", UT, rUT, BF16)
            for mg in range(4):
                Wo, rw = self.wload(w_out[:, :, mg * 256:(mg + 1) * 256], 4096)
                for mm in range(2):
                    b, rb = self.bank()
                    for fc in range(16):
                        Sc.add("pe", lambda h, b=b, fc=fc, mm=mm, Wo=Wo, UT=UT: h.matmul(
                            self.PS[:, b, :], Wo[:, fc, mm * 128:(mm + 1) * 128], UT[:, fc, :],
                            start=(fc == 0), stop=(fc == 15)),
                            reads=[rw, rUT[fc]], writes=[rb])
                    self.xt_add(mg * 2 + mm, tb, b, rb)


    def ssm(self, w):
        Sc = self.Sc
        self.begin_phase()
        o = [0]

        def al(shape, dt, name):
            n = 1
            for d_ in shape:
                n *= d_
            nb = n * (4 if dt == F32 else 2)
            nb = (nb + 31) // 32 * 32
            v = self.scr(o[0], shape, dt)
            o[0] += nb
            return v, self.pres(name)
        XC2, rXC = al((32, 256), BF16, "XC")
        Z, rZ = al((2048,), BF16, "Z")
        XTM, rXTM = al((2048,), BF16, "XTM")
        BTM, rBTM = al((1024,), BF16, "BTM")
        XDT, rXDT = al((32, 64), BF16, "XDT")
        XDD, rXDD = al((32, 64), BF16, "XDD")
        SR, rSR = al((32, 128), BF16, "SR")
        DEC, rDEC = al((32, 128), BF16, "DEC")
        CBm, rCBm = al((8, 128), BF16, "CBm")
        CARRY, rCARRY = al((2048,), F32, "CARRY")
        PREVb, rPREVb = al((2048,), BF16, "PREVb")
        NGc, rNG = al((16,), F32, "NGc")
        YT2, rYT2 = al((16, 256), BF16, "YT2")
        CW, rCW = al((5, 32), F32, "CW")
        RB, rRB = al((3, 32), F32, "RB")
        SM, rSM = al((8, 32), F32, "SM")
        ACC, rACC = al((2, 256), F32, "ACC")
        yt_off = o[0] - 8192
        CWR, rCWR = self.scr(yt_off, (128,), F32), rYT2
        CBR, rCBR = self.scr(yt_off + 512, (128,), F32), rYT2
        NGR = self.scr(yt_off + 1024, (128,), F32)
        sr_off = 16384 + 4096 + 4096 + 2048 + 4096 + 4096
        T = self.scr(sr_off, (2048,), F32)
        rSRp = [self.pres("SRp%d" % i) for i in range(8)]
        rDECp = [self.pres("DECp%d" % i) for i in range(8)]
        rTh = [self.pres("Th0"), self.pres("Th1")]
        rXDDh = [self.pres("XDDh0"), self.pres("XDDh1")]
        rCBh = [self.pres("CBh0"), self.pres("CBh1")]
        rSMn = [self.pres("SMn0"), self.pres("SMn1")]
        assert o[0] <= self.SCR_BYTES, o[0]
        w_in = w["w_in"].rearrange("(k p) n -> p k n", p=128)
        w_out = w["w_out"].rearrange("(c p) n -> p c n", p=128)
        Sc.add("sp", lambda h: h.dma_start(out=CWR, in_=w["conv_w"].rearrange("j (f p) -> (j f) p", p=128)),
               writes=[rCWR], dma=rCWR)
        Sc.add("sp", lambda h: h.dma_start(out=CBR[0:32, :], in_=w["conv_b"].rearrange("(f p) -> f p", p=128)),
               writes=[rCBR], dma=rCBR)
        b, rb = self.bank()
        Sc.add("pe", lambda h, b=b: h.transpose(self.PS[:, b, 0:128], CWR, self.ident[:]),
               reads=[rCWR, self.rC], writes=[rb])
        Sc.add("pe", lambda h, b=b: h.transpose(self.PS[:, b, 128:160], CBR[0:32, :], self.ident[0:32, 0:32]),
               reads=[rCBR, self.rC], writes=[rb])
        Sc.add("dve", lambda h, b=b: h.tensor_copy(CW.rearrange("p j f -> p (j f)"), self.PS[:, b, 0:160]),
               reads=[rb], writes=[rCW])
        for i, nm in enumerate(("dt_bias", "a_log", "d_skip")):
            Sc.add("sp", lambda h, i=i, nm=nm: h.dma_start(out=RB[:, i, :], in_=w[nm].partition_broadcast(128)),
                   writes=[rRB], dma=rRB)
        Sc.add("act", lambda h: h.activation(RB[:, 1, :], RB[:, 1, :], AF.Exp), reads=[rRB], writes=[rRB])
        Sc.add("dve", lambda h: h.tensor_scalar(RB[:, 1, :], RB[:, 1, :], -1.0, None, ALU.mult),
               reads=[rRB], writes=[rRB])
        Sc.add("sp", lambda h: h.dma_start(out=NGR[0:16, :], in_=w["norm_g"].rearrange("(f p) -> f p", p=128)),
               writes=[rYT2], dma=rYT2)
        b, rb = self.bank()
        Sc.add("pe", lambda h, b=b: h.transpose(self.PS[:, b, 0:16], NGR[0:16, :], self.ident[0:16, 0:16]),
               reads=[rYT2, self.rC], writes=[rb])
        Sc.add("dve", lambda h, b=b: h.tensor_copy(NGc, self.PS[:, b, 0:16]), reads=[rb], writes=[rNG])
        Sc.add("dve", lambda h: h.memset(CARRY, 0.0), writes=[rCARRY])
        Sc.add("dve", lambda h: h.memset(PREVb, 0.0), writes=[rPREVb])

        for cp in range(8):
            tp0 = cp * 256
            tb = tp0 // 512
            rH = self.rHT[tb]
            rHp = self.rHT[(tp0 - 3) // 512] if cp > 0 else rH
            for fg in range(8):
                Wg, rw = self.wload(w_in[:, :, 2048 + fg * 512:2048 + (fg + 1) * 512], 4096)
                for fc in range(4):
                    fch = fg * 4 + fc
                    b, rb = self.bank()
                    if cp == 0:
                        Sc.add("dve", lambda h, b=b: h.memset(self.PS[:, b, 0:3], 0.0), writes=[rb])
                        for k in range(8):
                            Sc.add("pe", lambda h, b=b, k=k, fc=fc, Wg=Wg: h.matmul(
                                self.PS[:, b, 3:259], Wg[:, k, fc * 128:(fc + 1) * 128], self.HT[:, k, 0:256],
                                start=(k == 0), stop=(k == 7)), reads=[rw, rH], writes=[rb])
                    else:
                        for k in range(8):
                            Sc.add("pe", lambda h, b=b, k=k, fc=fc, Wg=Wg, tp0=tp0: h.matmul(
                                self.PS[:, b, 0:259], Wg[:, k, fc * 128:(fc + 1) * 128],
                                self.HT[:, k, tp0 - 3:tp0 + 256],
                                start=(k == 0), stop=(k == 7)), reads=[rw, rH, rHp], writes=[rb])
                    a = fch % 2
                    Sc.add("dve", lambda h, b=b, fch=fch, a=a: h.tensor_scalar(
                        ACC[:, a, :], self.PS[:, b, 0:256], CW[:, 0, fch:fch + 1], CW[:, 4, fch:fch + 1],
                        ALU.mult, ALU.add), reads=[rb, rCW], writes=[rACC])
                    for j in range(1, 4):
                        Sc.add("dve", lambda h, b=b, fch=fch, a=a, j=j: h.scalar_tensor_tensor(
                            ACC[:, a, :], self.PS[:, b, j:j + 256], CW[:, j, fch:fch + 1], ACC[:, a, :],
                            ALU.mult, ALU.add), reads=[rb, rCW, rACC], writes=[rACC])
                    Sc.add("act", lambda h, fch=fch, a=a: h.activation(XC2[:, fch, :], ACC[:, a, :], AF.Silu),
                           reads=[rACC], writes=[rXC])
            for ci in range(2):
                c = cp * 2 + ci
                t0 = c * 128
                XC = XC2[:, :, ci * 128:(ci + 1) * 128]
                for gz in range(4):
                    Wg, rw = self.wload(w_in[:, :, gz * 512:(gz + 1) * 512], 4096)
                    b, rb = self.bank()
                    for k in range(8):
                        Sc.add("pe", lambda h, XC=XC, b=b, k=k, Wg=Wg, t0=t0: h.matmul(
                            self.PS[:, b, :], self.HT[:, k, t0:t0 + 128], Wg[:, k, :],
                            start=(k == 0), stop=(k == 7)), reads=[rw, rH], writes=[rb])
                    Sc.add("act", lambda h, XC=XC, b=b, gz=gz: h.activation(
                        Z[:, gz * 512:(gz + 1) * 512], self.PS[:, b, :], AF.Silu), reads=[rb], writes=[rZ])
                Wd, rw = self.wload(w_in[:, :, 6144:6176], 256)
                b, rb = self.bank()
                for k in range(8):
                    Sc.add("pe", lambda h, XC=XC, b=b, k=k, Wd=Wd, t0=t0: h.matmul(
                        self.PS[:, b, 0:32], self.HT[:, k, t0:t0 + 128], Wd[:, k, :],
                        start=(k == 0), stop=(k == 7)), reads=[rw, rH], writes=[rb])
                Sc.add("dve", lambda h, XC=XC, b=b: h.tensor_tensor(SM[:, 0, :], self.PS[:, b, 0:32], RB[:, 0, :], ALU.add),
                       reads=[rb, rRB], writes=[rSM])
                Sc.add("act", lambda h, XC=XC: h.activation(SM[:, 0, :], SM[:, 0, :], AF.Exp), reads=[rSM], writes=[rSM])
                Sc.add("act", lambda h, XC=XC: h.activation(SM[:, 0, :], SM[:, 0, :], AF.Ln, bias=1.0),
                       reads=[rSM], writes=[rSM])
                Sc.add("dve", lambda h, XC=XC: h.tensor_tensor(SM[:, 1, :], SM[:, 0, :], RB[:, 1, :], ALU.mult),
                       reads=[rSM, rRB], writes=[rSM])
                b, rb = self.bank()
                Sc.add("pe", lambda h, XC=XC, b=b: h.matmul(self.PS[:, b, 0:32], self.triu[:], SM[:, 1, :],
                                                     start=True, stop=True), reads=[rSM, self.rC3], writes=[rb])
                Sc.add("pe", lambda h, XC=XC, b=b: h.matmul(self.PS[:, b, 32:64], self.onesf[:], SM[:, 1, :],
                                                     start=True, stop=True), reads=[rSM, self.rC4], writes=[rb])
                Sc.add("act", lambda h, XC=XC, b=b: h.activation(
                    SM[:, 2:4, :].rearrange("p a b -> p (a b)"), self.PS[:, b, 0:64], AF.Exp),
                    reads=[rb], writes=[rSM])
                for q4 in range(6):
                    b, rb = self.bank()
                    psb = self.PS[:, b, :].bitcast(BF16)
                    for i in range(4):
                        fch = q4 * 4 + i
                        Sc.add("pe", lambda h, XC=XC, psb=psb, i=i, fch=fch: h.transpose(
                            psb[:, i * 128:(i + 1) * 128], XC[:, fch, :], self.identb[:]),
                            reads=[rXC, self.rC], writes=[rb])
                    if q4 < 4:
                        Sc.add("act", lambda h, XC=XC, psb=psb, q4=q4: h.copy(XTM[:, q4 * 512:(q4 + 1) * 512], psb[:, 0:512]),
                               reads=[rb], writes=[rXTM])
                    else:
                        Sc.add("act", lambda h, XC=XC, psb=psb, q4=q4: h.copy(
                            BTM[:, (q4 - 4) * 512:(q4 - 3) * 512], psb[:, 0:512]), reads=[rb], writes=[rBTM])
                Sc.add("dve", lambda h, XC=XC: h.tensor_tensor(
                    XDT, XTM.rearrange("p (a b) -> p a b", b=64),
                    SM[:, 0, :].unsqueeze(2).to_broadcast([128, 32, 64]), ALU.mult),
                    reads=[rXTM, rSM], writes=[rXDT])
                for hq in range(8):
                    Sc.add("dve", lambda h, hq=hq: h.tensor_tensor(
                        SR[:, hq * 4:(hq + 1) * 4, :], self.triu[:].unsqueeze(1).to_broadcast([128, 4, 128]),
                        SM[:, 1, hq * 4:(hq + 1) * 4].unsqueeze(2).to_broadcast([128, 4, 128]), ALU.mult),
                        reads=[rSM, self.rC3], writes=[rSRp[hq], rTh[hq // 4]])
                    b, rb = self.bank()
                    Sc.add("pe", lambda h, b=b, hq=hq: h.matmul(
                        self.PS[:, b, :], self.slb[:], SR[:, hq * 4:(hq + 1) * 4, :].rearrange("p a b -> p (a b)"),
                        start=True, stop=True), reads=[rSRp[hq], self.rC3], writes=[rb])
                    Sc.add("act", lambda h, b=b, hq=hq: h.activation(
                        DEC[:, hq * 4:(hq + 1) * 4, :].rearrange("p a b -> p (a b)"), self.PS[:, b, :], AF.Exp),
                        reads=[rb], writes=[rDECp[hq]])
                Sc.add("dve", lambda h: h.tensor_tensor(
                    XDD, XDT, DEC[:, :, 127:128].to_broadcast([128, 32, 64]), ALU.mult),
                    reads=[rXDT] + rDECp, writes=rXDDh)
                for half in range(2):
                    b, rb = self.bank()
                    for gg in range(4):
                        g = half * 4 + gg
                        Sc.add("pe", lambda h, XC=XC, b=b, gg=gg, g=g: h.matmul(
                            self.PS[:, b, gg * 128:(gg + 1) * 128], XC[:, 16 + g, :], XC[:, 24 + g, :],
                            start=True, stop=True), reads=[rXC], writes=[rb])
                    Sc.add("dve", lambda h, b=b, half=half: h.tensor_tensor(
                        CBm[:, half * 4:(half + 1) * 4, :], self.PS[:, b, :].rearrange("p (a b) -> p a b", a=4),
                        self.triu[:].unsqueeze(1).to_broadcast([128, 4, 128]), ALU.mult),
                        reads=[rb, self.rC3], writes=[rCBh[half]])
                for hq in range(8):
                    Sc.add("dve", lambda h, hq=hq: h.tensor_tensor(
                        DEC[:, hq * 4:(hq + 1) * 4, :], DEC[:, hq * 4:(hq + 1) * 4, :],
                        CBm[:, hq, :].unsqueeze(1).to_broadcast([128, 4, 128]), ALU.mult),
                        reads=[rDECp[hq], rCBh[hq // 4]], writes=[rDECp[hq]])
                bo, ro0, ro1 = self.bank2()
                bo2, ro2, ro3 = self.bank2()
                rO = [ro0, ro1, ro2, ro3]
                bos = [bo, bo + 1, bo2, bo2 + 1]
                for g in range(8):
                    Sc.add("pe", lambda h, XC=XC, g=g, bb=bos[g // 2]: h.matmul(
                        self.PS[:, bb, (g % 2) * 256:(g % 2 + 1) * 256], XC[:, 24 + g, :],
                        PREVb[:, g * 256:(g + 1) * 256], start=True, stop=True),
                        reads=[rXC, rPREVb], writes=[rO[g // 2]])
                for q in range(4):
                    Sc.add("dve", lambda h, q=q, bb=bos[q]: h.tensor_tensor(
                        T[:, q * 512:(q + 1) * 512].rearrange("p (a b) -> p a b", b=64),
                        self.PS[:, bb, :].rearrange("p (a b) -> p a b", b=64),
                        SM[:, 2, q * 8:(q + 1) * 8].unsqueeze(2).to_broadcast([128, 8, 64]), ALU.mult),
                        reads=[rO[q], rSM], writes=[rTh[q // 2]] + rSRp[4 * (q // 2):4 * (q // 2) + 4])
                by, ry0, ry1 = self.bank2()
                by2, ry2, ry3 = self.bank2()
                rY = [ry0, ry1, ry2, ry3]
                bys = [by, by + 1, by2, by2 + 1]
                for hh in range(32):
                    Sc.add("pe", lambda h, hh=hh, bb=bys[hh // 8]: h.matmul(
                        self.PS[:, bb, (hh % 8) * 64:(hh % 8 + 1) * 64], DEC[:, hh, :], XDT[:, hh, :],
                        start=True, stop=True), reads=[rDECp[hh // 4], rXDT], writes=[rY[hh // 8]])
                bs_, rs0, rs1 = self.bank2()
                bs2, rs2, rs3 = self.bank2()
                rS = [rs0, rs1, rs2, rs3]
                bss = [bs_, bs_ + 1, bs2, bs2 + 1]
                for g in range(8):
                    Sc.add("pe", lambda h, g=g, bb=bss[g // 2]: h.matmul(
                        self.PS[:, bb, (g % 2) * 256:(g % 2 + 1) * 256], BTM[:, g * 128:(g + 1) * 128],
                        XDD[:, g * 4:(g + 1) * 4, :].rearrange("p a b -> p (a b)"), start=True, stop=True),
                        reads=[rBTM] + rXDDh, writes=[rS[g // 2]])
                Sc.add("dve", lambda h: h.tensor_tensor(
                    XDT, XTM.rearrange("p (a b) -> p a b", b=64),
                    RB[:, 2, :].unsqueeze(2).to_broadcast([128, 32, 64]), ALU.mult),
                    reads=[rXTM, rRB, rXDT], writes=[rXDT])
                YN = XDD.rearrange("p a b -> p (a b)")
                XDS = XDT.rearrange("p a b -> p (a b)")
                for hf in range(2):
                    cs = slice(hf * 1024, (hf + 1) * 1024)
                    rT = rTh[hf]
                    for q in (2 * hf, 2 * hf + 1):
                        Sc.add("dve", lambda h, q=q, bb=bys[q]: h.tensor_tensor(
                            T[:, q * 512:(q + 1) * 512], T[:, q * 512:(q + 1) * 512], self.PS[:, bb, :], ALU.add),
                            reads=[rY[q], rT], writes=[rT])
                    Sc.add("dve", lambda h, cs=cs: h.tensor_tensor(T[:, cs], T[:, cs], XDS[:, cs], ALU.add),
                           reads=[rT, rXDT], writes=[rT])
                    Sc.add("dve", lambda h, cs=cs: h.tensor_tensor(T[:, cs], T[:, cs], Z[:, cs], ALU.mult),
                           reads=[rT, rZ], writes=[rT])
                    for g in range(4 * hf, 4 * hf + 4):
                        Sc.add("act", lambda h, g=g, hf=hf: h.activation(
                            YN[:, hf * 1024:hf * 1024 + 256], T[:, g * 256:(g + 1) * 256], AF.Square,
                            accum_out=SM[:, 4, g:g + 1]), reads=[rT], writes=[rXDDh[hf], rSMn[hf]])
                    g4 = slice(4 * hf, 4 * hf + 4)
                    Sc.add("act", lambda h, g4=g4: h.activation(
                        SM[:, 5, g4], SM[:, 4, g4], AF.Sqrt, bias=EPS, scale=1.0 / 256),
                        reads=[rSMn[hf]], writes=[rSMn[hf]])
                    Sc.add("dve", lambda h, g4=g4: h.reciprocal(SM[:, 6, g4], SM[:, 5, g4]),
                           reads=[rSMn[hf]], writes=[rSMn[hf]])
                    Sc.add("dve", lambda h, cs=cs, g4=g4: h.tensor_tensor(
                        YN[:, cs].rearrange("p (a b) -> p a b", b=256), T[:, cs].rearrange("p (a b) -> p a b", b=256),
                        SM[:, 6, g4].unsqueeze(2).to_broadcast([128, 4, 256]), ALU.mult),
                        reads=[rT, rSMn[hf]], writes=[rXDDh[hf]])
                    for q4 in (2 * hf, 2 * hf + 1):
                        b, rb = self.bank()
                        psb = self.PS[:, b, :].bitcast(BF16)
                        for i in range(4):
                            fch = q4 * 4 + i
                            Sc.add("pe", lambda h, psb=psb, i=i, fch=fch: h.transpose(
                                psb[:, i * 128:(i + 1) * 128], YN[:, fch * 128:(fch + 1) * 128], self.identb[:]),
                                reads=[rXDDh[hf], self.rC], writes=[rb])
                        for i in range(4):
                            fch = q4 * 4 + i
                            Sc.add("act", lambda h, psb=psb, i=i, fch=fch, ci=ci: h.activation(
                                YT2[:, fch, ci * 128:(ci + 1) * 128], psb[:, i * 128:(i + 1) * 128], AF.Copy,
                                scale=NGc[:, fch:fch + 1]), reads=[rb, rNG], writes=[rYT2])
                Sc.add("dve", lambda h: h.tensor_tensor(
                    CARRY.rearrange("p (a b) -> p a b", b=64), CARRY.rearrange("p (a b) -> p a b", b=64),
                    SM[:, 3, :].unsqueeze(2).to_broadcast([128, 32, 64]), ALU.mult),
                    reads=[rCARRY, rSM], writes=[rCARRY])
                for q in range(4):
                    Sc.add("dve", lambda h, q=q, bb=bss[q]: h.tensor_tensor(
                        CARRY[:, q * 512:(q + 1) * 512], CARRY[:, q * 512:(q + 1) * 512], self.PS[:, bb, :], ALU.add),
                        reads=[rS[q], rCARRY], writes=[rCARRY])
                Sc.add("act", lambda h: h.copy(PREVb, CARRY), reads=[rCARRY], writes=[rPREVb])
            for mg in range(4):
                Wo, rw = self.wload(w_out[:, :, mg * 256:(mg + 1) * 256], 4096)
                for mm in range(2):
                    b, rb = self.bank()
                    for fc in range(16):
                        Sc.add("pe", lambda h, b=b, fc=fc, mm=mm, Wo=Wo: h.matmul(
                            self.PS[:, b, 0:256], Wo[:, fc, mm * 128:(mm + 1) * 128], YT2[:, fc, :],
                            start=(fc == 0), stop=(fc == 15)), reads=[rw, rYT2], writes=[rb])
                    m = mg * 2 + mm
                    Sc.add("dve", lambda h, b=b, m=m, tp0=tp0: h.tensor_tensor(
                        self.XT[:, m, tp0:tp0 + 256], self.XT[:, m, tp0:tp0 + 256], self.PS[:, b, 0:256], ALU.add),
                        reads=[rb, self.rXT[tb]], writes=[self.rXT[tb]])

    def nsa(self, w):
        Sc = self.Sc
        self.begin_phase()
        o = [0]

        def al(shape, dt, name):
            n = 1
            for d_ in shape:
                n *= d_
            nb = n * (4 if dt == F32 else 2)
            nb = (nb + 31) // 32 * 32
            v = self.scr(o[0], shape, dt)
            o[0] += nb
            return v, self.pres(name)
        KS, rKS = al((2, 2048), BF16, "KS")
        KW, rKW = al((2, 2048), BF16, "KW")
        KC, rKC = al((2, 2048), BF16, "KC")
        VC, rVC = al((2, 2048), BF16, "VC")
        rVC = rKC
        VS1, rVS = al((16, 4, 65), BF16, "VS1")
        VW1, rVW = al((16, 4, 65), BF16, "VW1")
        KCT, rKCT = al((2, 128), BF16, "KCT")
        VC1, rVC1 = al((4, 65), BF16, "VC1")
        HID, rHID = al((2, 128), BF16, "HID")
        POS, rPOS = al((2, 32), BF16, "POS")
        POSR, rPOSR = al((128,), F32, "POSR")
        HB, rHB = al((4,), F32, "HB")
        W2S, rW2S = al((2, 2, 128), BF16, "W2S")
        OV1, rOV = al((33,), BF16, "OV1")
        EXb, rEX = al((2048,), BF16, "EX")
        QT, rQT = self.scr(16384 + 1056, (4, 8, 128), BF16), self.pres("QT")
        OTR4 = self.scr(16384 + 1056 + 8192, (2, 512), F32)
        rOTRs = [self.pres("OTR0"), self.pres("OTR1")]
        GT, rGT = al((48,), F32, "GT")
        MC, rMC = al((128,), BF16, "MC")
        E, rE0 = al((4, 512), BF16, "E")
        rE = [rE0, self.pres("E1"), self.pres("E2"), self.pres("E3")]
        IMP, rIMP = al((4, 32), F32, "IMP")
        SA, rSA = al((2, 32), F32, "SA")
        M8, rM8 = al((72,), F32, "M8")
        SELT, rSELT = al((4, 128), BF16, "SELT")
        NUM, rNUM = self.scr(16384, (4, 65), F32), self.pres("NUM")
        OT, rOT = al((8, 512), BF16, "OT")
        SA2, rSA2 = al((4, 32), F32, "SA2")
        TMPI, rTMPI = al((512,), F32, "TMPI")
        F_, rF = al((3, 16), F32, "F")
        OTM, rOTM = al((1024,), BF16, "OTM")
        assert o[0] <= self.SCR_BYTES, o[0]
        w_in = w["w_in"].rearrange("(k p) n -> p k n", p=128)
        w_out = w["w_out"].rearrange("(c p) n -> p c n", p=128)

        def wload2(base, nper, a0=0, na=None):
            na = nper if na is None else na
            i, r = self.wring.next()
            dst = self.WR[:, i, 0:8 * na * 128].rearrange("p (k a u d) -> p k a u d", k=8, a=na, u=2)
            for u in range(2):
                for a in range(na):
                    c0 = base + u * nper * 64 + (a0 + a) * 64
                    Sc.add("pool", lambda h, u=u, a=a, c0=c0: h.dma_start(
                        out=dst[:, :, a, u, :], in_=w_in[:, :, c0:c0 + 64]), writes=[r], dma=r)
            return self.WR[:, i, 0:8 * na * 128].rearrange("p (k n) -> p k n", k=8), r
        Sc.add("pool", lambda h: h.dma_start(out=OV1[0:127, :], in_=w["ov1"]), writes=[rOV], dma=rOV)
        Sc.add("pool", lambda h: h.dma_start(out=EXb[0:32, :], in_=w["ex"]), writes=[rEX], dma=rEX)
        Sc.add("dve", lambda h: h.memset(VS1, 1.0), writes=[rVS])
        Sc.add("dve", lambda h: h.memset(VW1, 1.0), writes=[rVW])
        Sc.add("dve", lambda h: h.memset(VC1, 1.0), writes=[rVC1])
        for kv, nm in enumerate(("pos_k", "pos_v")):
            for u in range(2):
                Sc.add("sp", lambda h, kv=kv, nm=nm, u=u: h.dma_start(
                    out=POSR[kv * 32:(kv + 1) * 32, u * 64:(u + 1) * 64], in_=w[nm]), writes=[rPOSR], dma=rPOSR)
        b, rb = self.bank()
        Sc.add("pe", lambda h, b=b: h.transpose(self.PS[:, b, 0:64], POSR[0:64, :], self.ident[0:64, 0:64]),
               reads=[rPOSR, self.rC], writes=[rb])
        Sc.add("dve", lambda h, b=b: h.tensor_copy(POS.rearrange("p a b -> p (a b)"), self.PS[:, b, 0:64]),
               reads=[rb], writes=[rPOS])
        for kv, nm in enumerate(("k_w2", "v_w2")):
            for u in range(2):
                Sc.add("pool", lambda h, kv=kv, nm=nm, u=u: h.dma_start(
                    out=W2S[:, kv, :, u * 64:(u + 1) * 64], in_=w[nm].rearrange("(c p) d -> p c d", p=128)),
                    writes=[rW2S], dma=rW2S)
        for (base, dstT, rT) in ((1024, KC, rKC), (1280, VC, rVC), (1536, KS, rKS), (2048, KW, rKW)):
            Wg, rw = wload2(base, 2)
            for a in range(2):
                for tb in range(4):
                    b, rb = self.bank()
                    for k in range(8):
                        Sc.add("pe", lambda h, b=b, k=k, a=a, Wg=Wg, tb=tb: h.matmul(
                            self.PS[:, b, :], Wg[:, k, a * 128:(a + 1) * 128], self.HT[:, k, tb * 512:(tb + 1) * 512],
                            start=(k == 0), stop=(k == 7)), reads=[rw, self.rHT[tb]], writes=[rb])
                    Sc.add("act", lambda h, b=b, a=a, tb=tb, dstT=dstT: h.copy(
                        dstT[:, a, tb * 512:(tb + 1) * 512], self.PS[:, b, :]), reads=[rb], writes=[rT])
        for (base, dstV, rV) in ((1792, VS1, rVS), (2304, VW1, rVW)):
            Wg, rw = self.wload(w_in[:, :, base:base + 256], 2048)
            for kt in range(16):
                b, rb = self.bank()
                for k in range(8):
                    Sc.add("pe", lambda h, b=b, k=k, Wg=Wg, kt=kt: h.matmul(
                        self.PS[:, b, 0:256], self.HT[:, k, kt * 128:(kt + 1) * 128], Wg[:, k, :],
                        start=(k == 0), stop=(k == 7)), reads=[rw, self.rHT[kt // 4]], writes=[rb])
                Sc.add("act", lambda h, b=b, kt=kt, dstV=dstV: h.copy(
                    dstV[:, kt, :, 0:64], self.PS[:, b, 0:256].rearrange("p (g d) -> p g d", d=64)),
                    reads=[rb], writes=[rV])
        for kv, (srcT, rS, w1n) in enumerate(((KC, rKC, "k_w1"), (VC, rVC, "v_w1"))):
            w1v = w[w1n].rearrange("(l d) n -> d l n", d=64)
            W1h = []
            for lh in range(2):
                i, r = self.wring.next()
                dst = self.WR[:, i, :].rearrange("p (l n) -> p l n", n=256)
                for u in range(2):
                    Sc.add("pool", lambda h, u=u, lh=lh, dst=dst, w1v=w1v: h.dma_start(
                        out=dst[u * 64:(u + 1) * 64, :, :], in_=w1v[:, lh * 16:(lh + 1) * 16, :]),
                        writes=[r], dma=r)
                W1h.append((dst, r))
            bh, rbh = self.bank()
            for hc in range(2):
                for l in range(32):
                    W1, rw = W1h[l // 16]
                    Sc.add("pe", lambda h, bh=bh, hc=hc, l=l, W1=W1, kv=kv: h.matmul(
                        self.PS[:, bh, hc:hc + 1], W1[0:64, l % 16, hc * 128:(hc + 1) * 128],
                        POS[0:64, kv, l:l + 1], start=(l == 0), stop=(l == 31)),
                        reads=[rw, rPOS], writes=[rbh])
            Sc.add("dve", lambda h, bh=bh, kv=kv: h.tensor_copy(HB[:, kv * 2:kv * 2 + 2], self.PS[:, bh, 0:2]),
                   reads=[rbh], writes=[rHB])
            for g in range(4):
                u, a = g // 2, g % 2
                for hc in range(2):
                    b, rb = self.bank()
                    for l in range(32):
                        W1, rw = W1h[l // 16]
                        Sc.add("pe", lambda h, b=b, hc=hc, l=l, W1=W1, u=u, a=a, srcT=srcT: h.matmul(
                            self.PS[:, b, 0:127], W1[u * 64:(u + 1) * 64, l % 16, hc * 128:(hc + 1) * 128],
                            srcT[u * 64:(u + 1) * 64, a, l:l + 16 * 126 + 1:16],
                            start=(l == 0), stop=(l == 31)), reads=[rw, rS], writes=[rb])
                    Sc.add("act", lambda h, b=b, hc=hc, kv=kv: h.activation(
                        HID[:, hc, 0:127], self.PS[:, b, 0:127], AF.Gelu, bias=HB[:, kv * 2 + hc:kv * 2 + hc + 1]),
                        reads=[rb, rHB], writes=[rHID])
                b, rb = self.bank()
                if kv == 0:
                    for hc in range(2):
                        Sc.add("pe", lambda h, b=b, hc=hc: h.matmul(
                            self.PS[:, b, 0:127], W2S[:, 0, hc, :], HID[:, hc, 0:127],
                            start=(hc == 0), stop=(hc == 1)), reads=[rW2S, rHID], writes=[rb])
                    Sc.add("act", lambda h, b=b, u=u, a=a: h.copy(
                        KCT[u * 64:(u + 1) * 64, a, 0:127], self.PS[u * 64:(u + 1) * 64, b, 0:127]),
                        reads=[rb], writes=[rKCT])
                else:
                    for hc in range(2):
                        Sc.add("pe", lambda h, b=b, hc=hc: h.matmul(
                            self.PS[0:127, b, 0:64], HID[:, hc, 0:127], W2S[:, 1, hc, 0:64],
                            start=(hc == 0), stop=(hc == 1)), reads=[rW2S, rHID], writes=[rb])
                    Sc.add("act", lambda h, b=b, g=g: h.copy(VC1[0:127, g, 0:64], self.PS[0:127, b, 0:64]),
                           reads=[rb], writes=[rVC1])
        for r_al in (rQT, rNUM, rOTRs[0], rOTRs[1]):
            for dct in (rKC.w, rKC.r):
                for ch, sq in dct.items():
                    if r_al.w.get(ch, 0) < sq:
                        r_al.w[ch] = sq
        self.bank_lo = 5
        self.psi = 0
        accb = [(g, self.rPS[g]) for g in range(4)]
        ecnt = [0]

        def scores(KT, rK, g, ksl, nk, qi, masks):
            u, a = g // 2, g % 2
            b, rb = self.bank()
            Sc.add("pe", lambda h, b=b: h.matmul(
                self.PS[0:nk, b, :], KT[u * 64:(u + 1) * 64, a, ksl],
                QT[u * 64:(u + 1) * 64, qi, a * 4:(a + 1) * 4, :].rearrange("p a q -> p (a q)"),
                start=True, stop=True), reads=[rK, rQT], writes=[rb])
            e = ecnt[0] % 4
            ecnt[0] += 1
            Sc.add("act", lambda h, b=b, e=e: h.activation(E[0:nk, e, :], self.PS[0:nk, b, :], AF.Exp),
                   reads=[rb], writes=[rE[e]])
            for (mk, rm) in masks:
                Sc.add("dve", lambda h, e=e, mk=mk: h.tensor_tensor(
                    E[0:nk, e, :].rearrange("p (a q) -> p a q", a=4),
                    E[0:nk, e, :].rearrange("p (a q) -> p a q", a=4),
                    mk.unsqueeze(1).to_broadcast([nk, 4, 128]), ALU.mult),
                    reads=[rE[e]] + rm, writes=[rE[e]])
            return e

        def pv(e, nk, V1, rV, g, first, last):
            bb, rbb = accb[g]
            Sc.add("pe", lambda h, e=e, bb=bb: h.matmul(
                self.PS[0:65, bb, :], V1, E[0:nk, e, :], start=first, stop=last),
                reads=[rE[e], rV], writes=[rbb])

        def finish_a(g):
            bb, rbb = accb[g]
            Sc.add("act", lambda h, bb=bb, g=g: h.copy(OTR4[0:65, g % 2, :], self.PS[0:65, bb, :]),
                   reads=[rbb], writes=[rOTRs[g % 2]])

        def finish(x_, g, first_branch):
            OTR, rOTR = OTR4[:, g % 2, :], rOTRs[g % 2]
            bt, rbt = self.bank()
            for r_ in range(4):
                Sc.add("pe", lambda h, bt=bt, r_=r_: h.transpose(
                    self.PS[:, bt, r_ * 65:(r_ + 1) * 65], OTR[0:65, r_ * 128:(r_ + 1) * 128],
                    self.ident[0:65, 0:65]), reads=[rOTR, self.rC], writes=[rbt])
            Sc.add("dve", lambda h, bt=bt: h.tensor_copy(
                NUM.rearrange("p a b -> p (a b)"), self.PS[:, bt, 0:260]), reads=[rbt], writes=[rNUM])
            self.dbg("OTR", OTR[0:65, :], [rOTR])
            self.dbg("NUM", NUM, [rNUM])
            g4 = slice(g * 4, g * 4 + 4)
            Sc.add("dve", lambda h: h.tensor_scalar(F_[:, 0, g4], NUM[:, :, 64], 1e-30, None, ALU.max),
                   reads=[rNUM], writes=[rF])
            Sc.add("dve", lambda h: h.reciprocal(F_[:, 1, g4], F_[:, 0, g4]), reads=[rF], writes=[rF])
            Sc.add("dve", lambda h, g=g, x_=x_: h.tensor_tensor(
                F_[:, 2, g4], F_[:, 1, g4],
                GT.rearrange("p (h x) -> p x h", x=3)[:, x_, g * 4:(g + 1) * 4], ALU.mult),
                reads=[rF, rGT], writes=[rF])
            self.dbg("F", F_, [rF])
            self.dbg("GT", GT, [rGT])
            ov = OTM[:, g * 256:(g + 1) * 256].rearrange("p (h d) -> p h d", d=64)
            fb = F_[:, 2, g4].unsqueeze(2).to_broadcast([128, 4, 64])
            if first_branch:
                Sc.add("dve", lambda h: h.tensor_tensor(ov, NUM[:, :, 0:64], fb, ALU.mult),
                       reads=[rNUM, rF], writes=[rOTM])
            else:
                Sc.add("dve", lambda h: h.tensor_tensor(NUM[:, :, 0:64], NUM[:, :, 0:64], fb, ALU.mult),
                       reads=[rNUM, rF], writes=[rNUM])
                Sc.add("dve", lambda h: h.tensor_tensor(ov, ov, NUM[:, :, 0:64], ALU.add),
                       reads=[rNUM, rOTM], writes=[rOTM])

        for tb in range(4):
            rH = self.rHT[tb]
            for jh in range(2):
                Wq, rw = wload2(0, 8, jh * 4, 4)
                for jj in range(4):
                    b, rb = self.bank()
                    for k in range(8):
                        Sc.add("pe", lambda h, b=b, k=k, jj=jj, Wq=Wq, tb=tb: h.matmul(
                            self.PS[:, b, :], Wq[:, k, jj * 128:(jj + 1) * 128],
                            self.HT[:, k, tb * 512:(tb + 1) * 512], start=(k == 0), stop=(k == 7)),
                            reads=[rw, rH], writes=[rb])
                    Sc.add("act", lambda h, b=b, j=jh * 4 + jj: h.activation(
                        QT[:, :, j, :], self.PS[:, b, :].rearrange("p (a q) -> p a q", a=4),
                        AF.Copy, scale=0.125), reads=[rb], writes=[rQT])
            Wg_, rwg = self.wload(w_in[:, :, 2560:2608], 384)
            for i in range(tb * 4, tb * 4 + 4):
                t0 = i * 128
                qsl = slice((i % 4) * 128, (i % 4 + 1) * 128)
                b, rb = self.bank()
                for k in range(8):
                    Sc.add("pe", lambda h, b=b, k=k, t0=t0, Wg_=Wg_: h.matmul(
                        self.PS[:, b, 0:48], self.HT[:, k, t0:t0 + 128], Wg_[:, k, :],
                        start=(k == 0), stop=(k == 7)), reads=[rwg, rH], writes=[rb])
                Sc.add("act", lambda h, b=b: h.activation(GT, self.PS[:, b, 0:48], AF.Sigmoid),
                       reads=[rb], writes=[rGT])
                Sc.add("pool", lambda h, t0=t0: h.dma_start(out=MC[0:127, :], in_=w["maskc"][:, t0:t0 + 128]),
                       writes=[rMC], dma=rMC)
                Sc.add("sp", lambda h, t0=t0: h.dma_start(out=SA[:, 0, :], in_=w["selA"][t0:t0 + 128, :]),
                       writes=[rSA], dma=rSA)
                Sc.add("sp", lambda h, t0=t0: h.dma_start(out=SA[:, 1, :], in_=w["selB"][t0:t0 + 128, :]),
                       writes=[rSA], dma=rSA)
                def run_items(items, extra2=None, pre1=None):
                    LOOK = 2
                    slots = []
                    for n in range(len(items) + LOOK):
                        if n < len(items):
                            if pre1 is not None:
                                pre1(n)
                            KT_, rK_, g_, ksl_, nk_, masks_, V1_, rV_, fi_, la_ = items[n]
                            slots.append(scores(KT_, rK_, g_, ksl_, nk_, i % 4, masks_))
                        if n >= LOOK:
                            KT_, rK_, g_, ksl_, nk_, masks_, V1_, rV_, fi_, la_ = items[n - LOOK]
                            e_ = slots[n - LOOK]
                            pv(e_, nk_, V1_, rV_, g_, fi_, la_)
                            if extra2 is not None:
                                extra2(g_, e_)

                bis = {}

                def imp_mm(g, e):
                    bi, rbi = 4, self.rPS[4]
                    bis[g] = (bi, rbi)
                    for r_ in range(4):
                        Sc.add("pe", lambda h, e=e, r_=r_, bi=bi, g=g: h.matmul(
                            self.PS[:, bi, g * 128 + r_ * 32:g * 128 + (r_ + 1) * 32],
                            E[0:127, e, r_ * 128:(r_ + 1) * 128],
                            OV1[0:127, 0:32], start=True, stop=True), reads=[rE[e], rOV], writes=[rbi])
                run_items([(KCT, rKCT, g, slice(0, 127), 127, [(MC[0:127, :], [rMC])],
                            VC1[0:127, g, :], rVC1, True, True) for g in range(4)], imp_mm)
                finish_a(0)
                for g in range(4):
                    if g < 3:
                        finish_a(g + 1)
                    finish(0, g, True)
                bi, rbi = bis[0]
                Sc.add("dve", lambda h, bi=bi: h.tensor_tensor(
                    TMPI.rearrange("p (a j) -> p a j", j=32),
                    self.PS[:, bi, :].rearrange("p (a j) -> p a j", j=32),
                    F_[:, 1, 0:16].unsqueeze(2).to_broadcast([128, 16, 32]), ALU.mult),
                    reads=[rbi, rF], writes=[rTMPI])
                Tv = TMPI.rearrange("p (g r j) -> p g r j", g=4, r=4)
                Sc.add("dve", lambda h: h.tensor_tensor(IMP, Tv[:, :, 0, :], Tv[:, :, 1, :], ALU.add),
                       reads=[rTMPI], writes=[rIMP])
                for r_ in (2, 3):
                    Sc.add("dve", lambda h, r_=r_: h.tensor_tensor(IMP, IMP, Tv[:, :, r_, :], ALU.add),
                           reads=[rTMPI, rIMP], writes=[rIMP])
                Sc.add("dve", lambda h: h.tensor_tensor(
                    IMP, IMP, SA[:, 0, :].unsqueeze(1).to_broadcast([128, 4, 32]), ALU.mult),
                    reads=[rIMP, rSA], writes=[rIMP])
                Sc.add("dve", lambda h: h.tensor_tensor(
                    IMP, IMP, SA[:, 1, :].unsqueeze(1).to_broadcast([128, 4, 32]), ALU.add),
                    reads=[rIMP, rSA], writes=[rIMP])
                for g in range(4):
                    Sc.add("dve", lambda h, g=g: h.max(M8[:, g * 8:(g + 1) * 8], IMP[:, g, :]),
                           reads=[rIMP], writes=[rM8])
                for g in range(4):
                    Sc.add("dve", lambda h, g=g: h.match_replace(
                        SA2[:, g, :], M8[:, g * 8:(g + 1) * 8], IMP[:, g, :], -3e38),
                        reads=[rIMP, rM8], writes=[rSA2])
                for g in range(4):
                    Sc.add("dve", lambda h, g=g: h.max(M8[:, 32 + g * 8:32 + (g + 1) * 8], SA2[:, g, :]),
                           reads=[rSA2], writes=[rM8])
                Sc.add("dve", lambda h: h.tensor_reduce(
                    M8[:, 64:68], M8[:, 32:64].rearrange("p (g k) -> p g k", k=8), AX.X, ALU.min),
                    reads=[rM8], writes=[rM8])
                Sc.add("dve", lambda h: h.tensor_tensor(
                    IMP, IMP, M8[:, 64:68].unsqueeze(2).to_broadcast([128, 4, 32]), ALU.is_ge),
                    reads=[rIMP, rM8], writes=[rIMP])
                kts = [kt for kt in range(i - 4, i + 1) if kt >= 0]
                items = []
                for kt in kts:
                    for g in range(4):
                        masks = []
                        if kt == i:
                            masks.append((self.triub[:], [self.rC3]))
                        if kt == i - 4:
                            masks.append((self.slb[:], [self.rC3]))
                        items.append((KW, rKW, g, slice(kt * 128, (kt + 1) * 128), 128, masks,
                                      VW1[:, kt, g, :], rVW, kt == kts[0], kt == kts[-1]))
                run_items(items)
                bt, rbt = self.bank()
                for g in range(4):
                    Sc.add("pe", lambda h, bt=bt, g=g: h.transpose(
                        self.PS[0:32, bt, g * 128:(g + 1) * 128], IMP[:, g, :], self.ident[:]),
                        reads=[rIMP, self.rC], writes=[rbt])
                Sc.add("act", lambda h, bt=bt: h.copy(
                    SELT[0:32, :, :].rearrange("p g q -> p (g q)"), self.PS[0:32, bt, :]),
                    reads=[rbt], writes=[rSELT])
                finish_a(0)
                for g in range(4):
                    if g < 3:
                        finish_a(g + 1)
                    finish(2, g, False)
                items = []
                for kt in range(i + 1):
                    bm, rbm = 4, self.rPS[4]
                    for g in range(4):
                        masks = [(self.PS[:, bm, g * 128:(g + 1) * 128], [rbm])]
                        if kt == i:
                            masks.append((self.triub[:], [self.rC3]))
                        items.append((KS, rKS, g, slice(kt * 128, (kt + 1) * 128), 128, masks,
                                      VS1[:, kt, g, :], rVS, kt == 0, kt == i))

                def mask_mm(n):
                    if n % 4 == 0:
                        kt = n // 4
                        Sc.add("pe", lambda h, kt=kt: h.matmul(
                            self.PS[:, 4, :], EXb[0:32, kt * 128:(kt + 1) * 128],
                            SELT[0:32, :, :].rearrange("p g q -> p (g q)"),
                            start=True, stop=True), reads=[rEX, rSELT], writes=[self.rPS[4]])
                run_items(items, pre1=mask_mm)
                finish_a(0)
                for g in range(4):
                    if g < 3:
                        finish_a(g + 1)
                    finish(1, g, False)
                for q4 in range(2):
                    b, rb = self.bank()
                    psb = self.PS[:, b, :].bitcast(BF16)
                    for ii in range(4):
                        fch = q4 * 4 + ii
                        Sc.add("pe", lambda h, psb=psb, ii=ii, fch=fch: h.transpose(
                            psb[:, ii * 128:(ii + 1) * 128], OTM[:, fch * 128:(fch + 1) * 128], self.identb[:]),
                            reads=[rOTM, self.rC], writes=[rb])
                    Sc.add("act", lambda h, psb=psb, q4=q4, qsl=qsl: h.copy(
                        OT[:, q4 * 4:(q4 + 1) * 4, qsl], psb[:, 0:512].rearrange("p (a t) -> p a t", a=4)),
                        reads=[rb], writes=[rOT])
            for mg in range(2):
                Wo, rw = self.wload(w_out[:, :, mg * 512:(mg + 1) * 512], 4096)
                for mm in range(4):
                    b, rb = self.bank()
                    for fc in range(8):
                        Sc.add("pe", lambda h, b=b, fc=fc, mm=mm, Wo=Wo: h.matmul(
                            self.PS[:, b, :], Wo[:, fc, mm * 128:(mm + 1) * 128], OT[:, fc, :],
                            start=(fc == 0), stop=(fc == 7)), reads=[rw, rOT], writes=[rb])
                    self.xt_add(mg * 4 + mm, tb, b, rb)
        self.bank_lo = 0
        self.psi = 0


def consts():
    return {"c_ident": np.eye(128, dtype=np.float32),
            "c_tril": np.tril(np.ones((128, 128), dtype=np.float32)),
            "c_triu": np.triu(np.ones((128, 128), dtype=np.float32)),
            "c_sl": np.tril(np.ones((128, 128), dtype=np.float32), -1), **_nsa_consts()}


def _nsa_consts():
    t = np.arange(2048)
    n = np.arange(127)
    cmp_start = n * 16
    cmp_end = cmp_start + 31
    maskc = (cmp_end[:, None] <= t[None, :]).astype(np.float32)
    sel_start = np.arange(32) * 64
    ov = ((cmp_start[:, None] <= sel_start[None, :] + 63) & (cmp_end[:, None] >= sel_start[None, :]))
    ov1 = np.concatenate([ov.astype(np.float32), np.ones((127, 1), np.float32)], axis=1)
    j = np.arange(32)[None, :]
    cur = (t // 64)[:, None]
    forced = (j == 0) | (j == cur) | (j == cur - 1)
    future = sel_start[None, :] > t[:, None]
    selA = (~forced & ~future).astype(np.float32)
    selB = np.where(forced, 1e6, np.where(future, -1e6, 0.0)).astype(np.float32)
    ex = (np.arange(2048)[None, :] // 64 == np.arange(32)[:, None]).astype(np.float32)
    return {"c_maskc": maskc, "c_ov1": ov1, "c_selA": selA, "c_selB": selB, "c_ex": ex}


_INPUT_NAMES = (
    "x", "norm_gains", "final_norm", "ff_w1", "ff_w2",
    "l0_sg_w_in", "l0_sg_vnorm_g", "l0_sg_vnorm_b", "l0_sg_w_s", "l0_sg_b_s", "l0_sg_w_out",
    "l1_ssm_w_in", "l1_ssm_conv_w", "l1_ssm_conv_b", "l1_ssm_dt_bias", "l1_ssm_a_log", "l1_ssm_d_skip",
    "l1_ssm_norm_g", "l1_ssm_w_out",
    "l2_nsa_w_in", "l2_nsa_cmp_pos_k", "l2_nsa_cmp_pos_v", "l2_nsa_cmp_k_w1", "l2_nsa_cmp_k_w2",
    "l2_nsa_cmp_v_w1", "l2_nsa_cmp_v_w2", "l2_nsa_w_out",
    "l3_sg_w_in", "l3_sg_vnorm_g", "l3_sg_vnorm_b", "l3_sg_w_s", "l3_sg_b_s", "l3_sg_w_out",
)
ALL_STAGES = [("sg", 0), ("ffn", 0), ("ssm", 1), ("ffn", 1), ("nsa", 2), ("ffn", 2), ("sg", 3), ("ffn", 3)]


def run(inputs, n_cores, nseq, stages, final=True, trace=False, debug=False):
    p = Prog(nseq, stages, final)
    p.debug_on = debug
    nc = p.build()
    names = list(p.dram.keys())
    cs = consts()
    in_maps = []
    for c in range(n_cores):
        m = {}
        for n in names:
            if n == "x":
                m[n] = np.ascontiguousarray(inputs["x"][c * nseq:(c + 1) * nseq])
            elif n in cs:
                m[n] = cs[n]
            else:
                m[n] = np.ascontiguousarray(inputs[n])
        in_maps.append(m)
    res = run_bass_kernel_spmd(nc, in_maps, core_ids=list(range(n_cores)), trace=trace)
    outs = np.concatenate([r["out"] for r in res.results], axis=0)
    res.dbg = {k: res.results[0]["dbg_" + k] for k in p.dbgs}
    return outs, res


def kernel(**inputs):
    outs, _ = run(inputs, 8, 4, ALL_STAGES, True)
    return outs.astype(np.float32, copy=False)
```
